# Optimizing a Trainium2 kernel written in Bass

```python
import math
import jax
import jax.numpy as jnp
from jax import lax
import numpy as np

D_MODEL = 1024
BATCH = 4
SEQ = 4096
DEPTH = 2
DEC_BATCH = 32
DEC_SEQ = 1
PAST_LEN = 16384
PAGE_SIZE = 128

N_PAGES = PAST_LEN // PAGE_SIZE
N_POOL = DEC_BATCH * N_PAGES + max(1, (DEC_BATCH * N_PAGES) // 4)

N_EVEN = (DEPTH + 1) // 2
N_ODD = DEPTH // 2

H_A = 4
DK_A = D_MODEL // 16
DV_A = 2 * DK_A
Q_BLOCK = 128
H_B = 4
DK_B = D_MODEL // 16
DV_B = D_MODEL // 8
GLA_RANK = 16
GLA_NORMALIZER = 16.0
GLA_CHUNK = 16
H_C = 4
DH_C = D_MODEL // H_C
MLSTM_CHUNK = 64
D_FF = 256 * int(round(8 * D_MODEL / 3 / 256))
EPS = 1e-6

EVEN_SIZES = (H_A * 2 * DK_A, H_A * 2 * DK_A, H_A * DV_A,
              H_B * DK_B, H_B * DK_B, H_B * DV_B, GLA_RANK, H_B * DV_B)
EVEN_SPLITS = tuple(int(s) for s in np.cumsum(EVEN_SIZES)[:-1])
P_EVEN = sum(EVEN_SIZES)
M_EVEN = H_A * DV_A + H_B * DV_B
ODD_SIZES = (H_C * DH_C, H_C * DH_C, H_C * DH_C, H_C * DH_C, H_C, H_C)
ODD_SPLITS = tuple(int(s) for s in np.cumsum(ODD_SIZES)[:-1])
P_ODD = sum(ODD_SIZES)
M_ODD = H_C * DH_C

kernel_name = 'hybrid_diffattn_gla_mlstm_macaron_step'

F32 = jnp.float32


def rmsnorm(x, g):
    x32 = x.astype(F32)
    y = x32 * lax.rsqrt(jnp.mean(x32 * x32, axis=-1, keepdims=True) + EPS)
    return (y * g.astype(F32)).astype(x.dtype)


def swiglu(x, w_gate, w_up, w_down):
    return (jax.nn.silu(x @ w_gate) * (x @ w_up)) @ w_down


def macaron_half(x, g, w_gate, w_up, w_down):
    return x + (0.5 * swiglu(rmsnorm(x, g), w_gate, w_up, w_down)).astype(x.dtype)


def alibi_slopes(n):
    return jnp.asarray([2.0 ** (-8.0 * (h + 1) / n) for h in range(n)], dtype=F32)


def diff_lambda(lam_vecs, lam_init):
    lv = lam_vecs.astype(F32)
    return jnp.exp(jnp.sum(lv[0] * lv[1])) - jnp.exp(jnp.sum(lv[2] * lv[3])) + lam_init


def alibi_logits(q, k, q_pos, k_pos, slopes):
    s = jnp.einsum('bqhmd,bkhmd->bhmqk', q, k).astype(F32) * (DK_A ** -0.5)
    dist = (q_pos[:, None] - k_pos[None, :]).astype(F32)
    s = s - slopes[None, :, None, None, None] * dist
    return jnp.where(dist >= 0, s, -jnp.inf)


def diff_weights(logits, lam):
    p = jax.nn.softmax(logits, axis=-1)
    return p[:, :, 0] - lam * p[:, :, 1]


def diff_attn_prompt(q, k, v, lam, slopes):
    bsz, seq = q.shape[:2]
    n_blk = seq // Q_BLOCK
    q_blocks = jnp.swapaxes(q.reshape(bsz, n_blk, Q_BLOCK, H_A, 2, DK_A), 0, 1)
    k_pos = jnp.arange(seq)
    v32 = v.astype(F32)

    def block(args):
        q_blk, i = args
        q_pos = i * Q_BLOCK + jnp.arange(Q_BLOCK)
        w = diff_weights(alibi_logits(q_blk, k, q_pos, k_pos, slopes), lam)
        return jnp.einsum('bhqk,bkhv->bqhv', w, v32)

    out = lax.map(block, (q_blocks, jnp.arange(n_blk)))
    return jnp.swapaxes(out, 0, 1).reshape(bsz, seq, H_A, DV_A)


def diff_attn_sample(q, k_new, v_new, k_past, v_past, lam, slopes):
    past = k_past.shape[1]
    n_new = q.shape[1]
    q_pos = past + jnp.arange(n_new)
    logits = jnp.concatenate([
        alibi_logits(q, k_past, q_pos, jnp.arange(past), slopes),
        alibi_logits(q, k_new, q_pos, q_pos, slopes)], axis=-1)
    w = diff_weights(logits, lam)
    return (jnp.einsum('bhqk,bkhv->bqhv', w[..., :past], v_past.astype(F32))
            + jnp.einsum('bhqk,bkhv->bqhv', w[..., past:], v_new.astype(F32)))


def gla_chunked(q, k, v, g, s0):
    bsz, seq = q.shape[:2]
    c = GLA_CHUNK if seq % GLA_CHUNK == 0 else seq
    n = seq // c
    r = lambda t: t.reshape(bsz, n, c, *t.shape[2:]).astype(F32)
    q, k, v, g = r(q), r(k), r(v), r(g)
    b = jnp.cumsum(g, axis=2)
    b_last = b[:, :, -1]
    causal = jnp.tril(jnp.ones((c, c), dtype=bool))
    diff = b[:, :, :, None] - b[:, :, None, :]
    decay = jnp.exp(jnp.where(causal[None, None, :, :, None, None], diff, -jnp.inf))
    a_intra = jnp.einsum('bnthd,bntshd,bnshd->bnhts', q, decay, k)
    o_intra = jnp.einsum('bnhts,bnshv->bnthv', a_intra, v)
    u = jnp.einsum('bnshd,bnshv->bnhdv', k * jnp.exp(b_last[:, :, None] - b), v)

    def step(s, xs):
        u_n, a_n = xs
        return a_n[..., None] * s + u_n, s

    s_fin, s_prev = lax.scan(step, s0.astype(F32),
                             (jnp.swapaxes(u, 0, 1), jnp.swapaxes(jnp.exp(b_last), 0, 1)))
    s_prev = jnp.swapaxes(s_prev, 0, 1)
    o_inter = jnp.einsum('bnthd,bnhdv->bnthv', q * jnp.exp(b), s_prev)
    return (o_intra + o_inter).reshape(bsz, seq, H_B, DV_B), s_fin


def mlstm_chunked(q, k, v, i_pre, logf, c0, n0, m0):
    bsz, seq = q.shape[:2]
    c = MLSTM_CHUNK if seq % MLSTM_CHUNK == 0 else seq
    n = seq // c
    r = lambda t: jnp.moveaxis(t.reshape(bsz, n, c, *t.shape[2:]).astype(F32), 1, 0)
    causal = jnp.tril(jnp.ones((c, c), dtype=bool))

    def step(carry, xs):
        cs, ns, ms = carry
        qc, kc, vc, ic, fc = xs
        b = jnp.cumsum(fc, axis=1)
        dlog = b[:, :, None, :] - b[:, None, :, :] + ic[:, None, :, :]
        dlog = jnp.where(causal[None, :, :, None], dlog, -jnp.inf)
        prev_log = b + ms[:, None, :]
        m_t = jnp.maximum(prev_log, jnp.max(dlog, axis=2))
        dw = jnp.exp(dlog - m_t[:, :, None, :])
        pw = jnp.exp(prev_log - m_t)
        sw = jnp.einsum('bthd,bshd->btsh', qc, kc) * dw
        num = (jnp.einsum('btsh,bshv->bthv', sw, vc)
               + pw[..., None] * jnp.einsum('bhvd,bthd->bthv', cs, qc))
        den = jnp.sum(sw, axis=2) + pw * jnp.einsum('bhd,bthd->bth', ns, qc)
        h = num / jnp.maximum(jnp.abs(den), jnp.exp(-m_t))[..., None]
        m_end = m_t[:, -1]
        wk = jnp.exp(b[:, -1:, :] - b + ic - m_end[:, None, :])
        a = jnp.exp(b[:, -1] + ms - m_end)
        c_new = a[..., None, None] * cs + jnp.einsum('bsh,bshv,bshd->bhvd', wk, vc, kc)
        n_new = a[..., None] * ns + jnp.einsum('bsh,bshd->bhd', wk, kc)
        return (c_new, n_new, m_end), h

    (c_fin, n_fin, m_fin), h = lax.scan(
        step, (c0.astype(F32), n0.astype(F32), m0.astype(F32)),
        (r(q), r(k), r(v), r(i_pre), r(logf)))
    h = jnp.moveaxis(h, 0, 1).reshape(bsz, seq, H_C, DH_C)
    return h, c_fin, n_fin, m_fin


def even_inputs(hn, w_in, qk_norm_g, gate_w2, gate_b):
    bsz, seq = hn.shape[:2]
    aq, ak, av, bq, bk, bv, bg, br = jnp.split(hn @ w_in, EVEN_SPLITS, axis=-1)
    aq = rmsnorm(aq.reshape(bsz, seq, H_A, 2, DK_A), qk_norm_g[0])
    ak = rmsnorm(ak.reshape(bsz, seq, H_A, 2, DK_A), qk_norm_g[1])
    av = av.reshape(bsz, seq, H_A, DV_A)
    bq = bq.reshape(bsz, seq, H_B, DK_B) * (DK_B ** -0.5)
    bk = bk.reshape(bsz, seq, H_B, DK_B)
    bv = bv.reshape(bsz, seq, H_B, DV_B)
    glog = jax.nn.log_sigmoid((bg @ gate_w2 + gate_b).astype(F32)) / GLA_NORMALIZER
    glog = glog.reshape(bsz, seq, H_B, DK_B)
    return aq, ak, av, bq, bk, bv, glog, br


def even_output(a_out, b_out, br, lam_init, a_norm_g, b_norm_g, w_out):
    bsz, seq = a_out.shape[:2]
    a = rmsnorm(a_out, a_norm_g) * (1.0 - lam_init)
    b = rmsnorm(b_out, b_norm_g) * jax.nn.silu(br.astype(F32)).reshape(bsz, seq, H_B, DV_B)
    merged = jnp.concatenate([a.reshape(bsz, seq, H_A * DV_A), b.reshape(bsz, seq, H_B * DV_B)], axis=-1)
    return merged @ w_out


def odd_mixer(hn, w_in, gate_b, norm_g, w_out, c0, n0, m0):
    bsz, seq = hn.shape[:2]
    q, k, v, o, ig, fg = jnp.split(hn @ w_in, ODD_SPLITS, axis=-1)
    heads = lambda t: t.reshape(bsz, seq, H_C, DH_C)
    i_pre = ig.astype(F32) + gate_b[0].astype(F32)
    logf = jax.nn.log_sigmoid(fg.astype(F32) + gate_b[1].astype(F32))
    h, c_fin, n_fin, m_fin = mlstm_chunked(heads(q), heads(k) * (DH_C ** -0.5), heads(v),
                                           i_pre, logf, c0, n0, m0)
    y = jax.nn.sigmoid(heads(o).astype(F32)) * rmsnorm(h, norm_g)
    return y.reshape(bsz, seq, M_ODD) @ w_out, c_fin, n_fin, m_fin


def setup_inputs(seed: int = 0) -> dict:
    key = jax.random.key(seed)
    ks = iter(jax.random.split(key, 40))
    nrm = lambda shape, scale=1.0: scale * jax.random.normal(next(ks), shape, F32)
    page_table = jax.random.permutation(next(ks), N_POOL)[:DEC_BATCH * N_PAGES]
    page_table = page_table.reshape(DEC_BATCH, N_PAGES).astype(jnp.int32)
    return {
        'x_prompt': nrm((BATCH, SEQ, D_MODEL)),
        'x_sample': nrm((DEC_BATCH, DEC_SEQ, D_MODEL)),
        'cache_k': nrm((N_EVEN, N_POOL, PAGE_SIZE, H_A, 2, DK_A)),
        'cache_v': nrm((N_EVEN, N_POOL, PAGE_SIZE, H_A, DV_A)),
        'state_gla': nrm((N_EVEN, DEC_BATCH, H_B, DK_B, DV_B), 0.5),
        'state_mlstm_C': nrm((N_ODD, DEC_BATCH, H_C, DH_C, DH_C), 0.1),
        'state_mlstm_n': nrm((N_ODD, DEC_BATCH, H_C, DH_C), 0.1),
        'state_mlstm_m': nrm((N_ODD, DEC_BATCH, H_C), 1.0),
        'page_table': page_table,
        'norm_g': 1.0 + nrm((DEPTH, 3, D_MODEL), 0.02),
        'ffn_w_gate': nrm((DEPTH, 2, D_MODEL, D_FF), D_MODEL ** -0.5),
        'ffn_w_up': nrm((DEPTH, 2, D_MODEL, D_FF), D_MODEL ** -0.5),
        'ffn_w_down': nrm((DEPTH, 2, D_FF, D_MODEL), D_FF ** -0.5),
        'even_w_in': nrm((N_EVEN, D_MODEL, P_EVEN), D_MODEL ** -0.5),
        'even_w_out': nrm((N_EVEN, M_EVEN, D_MODEL), M_EVEN ** -0.5),
        'a_qk_norm': 1.0 + nrm((N_EVEN, 2, DK_A), 0.02),
        'a_lambda': nrm((N_EVEN, 4, DK_A), 0.1),
        'a_head_norm': 1.0 + nrm((N_EVEN, H_A, DV_A), 0.02),
        'b_gate_w2': nrm((N_EVEN, GLA_RANK, H_B * DK_B), GLA_RANK ** -0.5),
        'b_gate_bias': nrm((N_EVEN, H_B * DK_B), 0.1),
        'b_head_norm': 1.0 + nrm((N_EVEN, H_B, DV_B), 0.02),
        'odd_w_in': nrm((N_ODD, D_MODEL, P_ODD), D_MODEL ** -0.5),
        'odd_w_out': nrm((N_ODD, M_ODD, D_MODEL), M_ODD ** -0.5),
        'c_gate_bias': jnp.stack([nrm((N_ODD, H_C), 0.1),
                                  jnp.linspace(3.0, 6.0, H_C, dtype=F32)[None, :] + nrm((N_ODD, H_C), 0.1)], axis=1),
        'c_head_norm': 1.0 + nrm((N_ODD, H_C, DH_C), 0.02),
    }


def reference(x_prompt, x_sample, cache_k, cache_v, state_gla, state_mlstm_C, state_mlstm_n,
              state_mlstm_m, page_table, norm_g, ffn_w_gate, ffn_w_up, ffn_w_down,
              even_w_in, even_w_out, a_qk_norm, a_lambda, a_head_norm, b_gate_w2, b_gate_bias,
              b_head_norm, odd_w_in, odd_w_out, c_gate_bias, c_head_norm):
    yp, ys = x_prompt, x_sample
    slopes = alibi_slopes(H_A)
    n_pages = page_table.shape[1]
    past = n_pages * cache_k.shape[2]
    dec_b = x_sample.shape[0]
    k_p, v_p, k_s, v_s, gla_p, gla_s = [], [], [], [], [], []
    cm_p, nm_p, mm_p, cm_s, nm_s, mm_s = [], [], [], [], [], []
    for li in range(DEPTH):
        g = norm_g[li]
        yp = macaron_half(yp, g[0], ffn_w_gate[li, 0], ffn_w_up[li, 0], ffn_w_down[li, 0])
        ys = macaron_half(ys, g[0], ffn_w_gate[li, 0], ffn_w_up[li, 0], ffn_w_down[li, 0])
        hp = rmsnorm(yp, g[1])
        hs = rmsnorm(ys, g[1])
        if li % 2 == 0:
            e = li // 2
            lam_init = 0.8 - 0.6 * math.exp(-0.3 * li)
            lam = diff_lambda(a_lambda[e], lam_init)
            aq, ak, av, bq, bk, bv, bg, br = even_inputs(hp, even_w_in[e], a_qk_norm[e], b_gate_w2[e], b_gate_bias[e])
            a_out = diff_attn_prompt(aq, ak, av, lam, slopes)
            b_out, s_fin = gla_chunked(bq, bk, bv, bg, jnp.zeros((yp.shape[0], H_B, DK_B, DV_B), F32))
            yp = yp + even_output(a_out, b_out, br, lam_init, a_head_norm[e], b_head_norm[e], even_w_out[e]).astype(yp.dtype)
            k_p.append(ak)
            v_p.append(av)
            gla_p.append(s_fin)
            aq, ak, av, bq, bk, bv, bg, br = even_inputs(hs, even_w_in[e], a_qk_norm[e], b_gate_w2[e], b_gate_bias[e])
            k_past = cache_k[e, page_table].reshape(dec_b, past, H_A, 2, DK_A)
            v_past = cache_v[e, page_table].reshape(dec_b, past, H_A, DV_A)
            a_out = diff_attn_sample(aq, ak, av, k_past, v_past, lam, slopes)
            b_out, s_fin = gla_chunked(bq, bk, bv, bg, state_gla[e])
            ys = ys + even_output(a_out, b_out, br, lam_init, a_head_norm[e], b_head_norm[e], even_w_out[e]).astype(ys.dtype)
            k_s.append(ak)
            v_s.append(av)
            gla_s.append(s_fin)
        else:
            o = li // 2
            bp = yp.shape[0]
            out, c_fin, n_fin, m_fin = odd_mixer(
                hp, odd_w_in[o], c_gate_bias[o], c_head_norm[o], odd_w_out[o],
                jnp.zeros((bp, H_C, DH_C, DH_C), F32), jnp.zeros((bp, H_C, DH_C), F32), jnp.zeros((bp, H_C), F32))
            yp = yp + out.astype(yp.dtype)
            cm_p.append(c_fin)
            nm_p.append(n_fin)
            mm_p.append(m_fin)
            out, c_fin, n_fin, m_fin = odd_mixer(
                hs, odd_w_in[o], c_gate_bias[o], c_head_norm[o], odd_w_out[o],
                state_mlstm_C[o], state_mlstm_n[o], state_mlstm_m[o])
            ys = ys + out.astype(ys.dtype)
            cm_s.append(c_fin)
            nm_s.append(n_fin)
            mm_s.append(m_fin)
        yp = macaron_half(yp, g[2], ffn_w_gate[li, 1], ffn_w_up[li, 1], ffn_w_down[li, 1])
        ys = macaron_half(ys, g[2], ffn_w_gate[li, 1], ffn_w_up[li, 1], ffn_w_down[li, 1])
    return (yp, ys,
            jnp.stack(k_p), jnp.stack(v_p), jnp.stack(k_s), jnp.stack(v_s),
            jnp.stack(gla_p), jnp.stack(gla_s),
            jnp.stack(cm_p), jnp.stack(nm_p), jnp.stack(mm_p),
            jnp.stack(cm_s), jnp.stack(nm_s), jnp.stack(mm_s))
```

```python
import contextlib, math, os
KSTOP = int(os.environ.get('KSTOP', '0'))
KSUB = int(os.environ.get('KSUB', '0'))
import numpy as np
import ml_dtypes
import concourse.bass as bass
import concourse.mybir as mybir
from concourse.bass_utils import run_bass_kernel_spmd

F32 = mybir.dt.float32; BF16 = mybir.dt.bfloat16; I32 = mybir.dt.int32
AF = mybir.ActivationFunctionType; ALU = mybir.AluOpType; AX = mybir.AxisListType
EPS = 1e-6
D = 1024; DFF = 2816; PE_ = 3088; PO_ = 4104


class V:
    def __init__(s, t, ap): s.t = t; s.ap = ap
    def __getitem__(s, k): return V(s.t, s.ap[k])
    def re(s, pat, **kw): return V(s.t, s.ap.rearrange(pat, **kw))
    def bitcast(s, dt): return V(s.t, s.ap.bitcast(dt))
    def bc(s, shape): return V(s.t, s.ap.to_broadcast(list(shape)))
    def un(s, ax): return V(s.t, s.ap.unsqueeze(ax))


class T:
    def __init__(s, h, name): s.h = h; s.name = name; s.w = None; s.r = []; s.psum = False
    def __getitem__(s, k): return V(s, s.h[k])
    def sub(s, k, name):
        n = T(s.h[k], name); n.w = s.w; n.r = list(s.r)
        return n


class P:
    def __init__(self, nc, es, n_dma_sems=32):
        self.nc = nc; self.es = es
        self.eng = {"pe": nc.tensor, "act": nc.scalar, "dve": nc.vector, "pool": nc.gpsimd, "sp": nc.sync}
        self.sem = {e: es.enter_context(nc.semaphore("s_" + e)) for e in self.eng}
        self.cnt = {e: 0 for e in self.eng}
        self.seen = {e: {} for e in self.eng}
        self.dsems = [es.enter_context(nc.semaphore("d%d" % i)) for i in range(n_dma_sems)]
        self.dval = [0] * n_dma_sems
        self.dnext = {"sp": 0, "pool": 0}
        self.drange = {"sp": (0, n_dma_sems // 2), "pool": (n_dma_sems // 2, n_dma_sems)}
    def sb(self, name, shape, dt=F32):
        return T(self.es.enter_context(self.nc.sbuf_tensor(name, list(shape), dt)), name)
    def psum(self, name, shape, dt=F32):
        t = T(self.es.enter_context(self.nc.psum_tensor(name, list(shape), dt)), name)
        t.psum = True
        return t
    def _wait(self, e, d):
        key, sem, val, deng = d
        if deng == e and e == "pe":
            return
        if self.seen[e].get(key, 0) >= val:
            return
        self.eng[e].wait_ge(sem, val)
        self.seen[e][key] = val
    def _deps(self, e, reads, writes):
        deps = []
        for t in reads:
            if t.w is not None: deps.append(t.w)
            if t.psum:
                deps.extend(d for d in t.r if d[3] != e)
        for t in writes:
            if t.w is not None: deps.append(t.w)
            deps.extend(t.r)
        for d in deps:
            self._wait(e, d)
    def _mark(self, me, reads, writes):
        for t in reads:
            t.r.append(me)
            if len(t.r) > 64:
                t.r = self._prune(t.r)
        for t in writes:
            t.w = me; t.r = []
    @staticmethod
    def _prune(r):
        best = {}
        for d in r:
            if d[0] not in best or best[d[0]][2] < d[2]:
                best[d[0]] = d
        return list(best.values())
    def I(self, e, name, outs, ins, **kw):
        args = {}
        reads = []; writes = []
        for k, v in outs.items():
            args[k] = v.ap; writes.append(v.t)
        for k, v in ins.items():
            if isinstance(v, V):
                args[k] = v.ap; reads.append(v.t)
            else:
                args[k] = v
        args.update(kw)
        self._deps(e, reads, writes)
        ins_ = getattr(self.eng[e], name)(**args)
        self.cnt[e] += 1
        ins_.then_inc(self.sem[e], 1)
        self._mark((e, self.sem[e], self.cnt[e], e), reads, writes)
        return ins_
    def dma(self, e, out, in_, indirect=None):
        reads = [in_.t] if isinstance(in_, V) else []
        writes = [out.t] if isinstance(out, V) else []
        if indirect is not None:
            reads.append(indirect.t)
        self._deps(e, reads, writes)
        lo, hi = self.drange[e]
        i = lo + self.dnext[e]; self.dnext[e] = (self.dnext[e] + 1) % (hi - lo)
        sem = self.dsems[i]
        if self.dval[i] > 0:
            self._wait(e, ("d%d" % i, sem, self.dval[i], "dma"))
        self.dval[i] += 16
        o = out.ap if isinstance(out, V) else out
        s = in_.ap if isinstance(in_, V) else in_
        if indirect is not None:
            ins_ = self.eng[e].indirect_dma_start(out=o, out_offset=None, in_=s,
                                                  in_offset=bass.IndirectOffsetOnAxis(ap=indirect.ap, axis=0))
        else:
            ins_ = self.eng[e].dma_start(out=o, in_=s)
        ins_.then_inc(sem, 16)
        self._mark(("d%d" % i, sem, self.dval[i], "dma"), reads, writes)
        return ins_
    def finish(self):
        for i, sem in enumerate(self.dsems):
            if self.dval[i] > 0:
                self._wait("sp", ("d%d" % i, sem, self.dval[i], "dma"))
        for e in self.eng:
            if e != "sp" and self.cnt[e] > 0:
                self._wait("sp", (e, self.sem[e], self.cnt[e], e))


def build(TSEQ, NPG, NPOOL):
    NTT = TSEQ // 128
    STT = min(4, NTT)
    NST = NTT // STT
    NMAX = STT * 128
    PAST = NPG * 128
    nc = bass.Bass("TRN2", target_bir_lowering=False)

    def din(name, shape, dt=F32):
        return nc.dram_tensor(name, list(shape), dt, kind="ExternalInput").ap()
    def dout(name, shape):
        return nc.dram_tensor(name, list(shape), F32, kind="ExternalOutput").ap()

    xp = din("xp", [TSEQ, D]); xs = din("xs", [4, D])
    ck = din("ck", [NPOOL * 128, 512]); cv = din("cv", [NPOOL * 128, 512])
    sgla = din("sgla", [4, 4, 64, 128]); sC = din("sC", [4, 4, 256, 256]); sn = din("sn", [4, 4, 256]); sm = din("sm", [4, 4])
    ptab = din("ptab", [1, 4 * NPG], I32)
    normg = din("normg", [6, D])
    Wg = din("wg", [4, D, DFF]); Wu = din("wu", [4, D, DFF]); Wd = din("wd", [4, DFF, D])
    EWin = din("ewin", [D, PE_]); EWout = din("ewout", [D, D])
    aqk = din("aqk", [1, 128]); alam = din("alam", [1, 256]); ahn = din("ahn", [1, 512])
    bw2 = din("bw2", [16, 256]); bgb = din("bgb", [1, 256]); bhn = din("bhn", [1, 512])
    OWin = din("owin", [D, PO_]); OWout = din("owout", [D, D])
    cgb = din("cgb", [1, 8]); chn = din("chn", [1, 1024])
    c_identf = din("c_identf", [128, 128]); c_tri = din("c_tri", [128, 128]); c_trirev = din("c_trirev", [128, 128])
    c_negts = din("c_negts", [128, 128]); c_pos4 = din("c_pos4", [128, 128]); c_negbf = din("c_negbf", [128, 128])
    c_abp = din("c_abp", [128, 4 * NTT]); c_abs = din("c_abs", [128, NPG * 8]); c_iota = din("c_iota", [128, 1])
    c_bm = din("c_bm", [8, 8])

    yp = dout("yp", [TSEQ, D]); ys = dout("ys", [4, D])
    kp = dout("kp", [TSEQ, 512]); vp = dout("vp", [TSEQ, 512]); ks = dout("ks", [4, 512]); vs = dout("vs", [4, 512])
    glap = dout("glap", [4, 64, 128]); glas = dout("glas", [4, 4, 64, 128])
    Cp = dout("Cp", [4, 256, 256]); np_ = dout("np", [4, 256]); mp = dout("mp", [1, 4])
    Cs = dout("Cs", [4, 4, 256, 256]); ns = dout("ns", [4, 4, 256]); ms = dout("ms", [4, 4])

    es = contextlib.ExitStack()
    with es:
        p = P(nc, es)
        X = p.sb("X", [128, STT, D])
        tokb = p.sb("tokb", [128, D], BF16)
        junk = tokb
        actT = p.sb("actT", [128, 8, NMAX], BF16)
        NW = 3
        Wb = [p.sb("Wb%d" % i, [128, 8, 512], BF16) for i in range(NW)]
        KTbuf = p.sb("KT", [128, max(4 * TSEQ, 7168)], BF16)
        KT = KTbuf[:, 0:4 * TSEQ].re("p (h n) -> p h n", h=4)
        VSZ = max(NTT * 520, 2 * NPG * 8 + 2 * 4 * NPG + 16)
        Vbuf = p.sb("Vst", [128, VSZ], BF16)
        Vst = Vbuf[:, 0:NTT * 520].re("p (a b c) -> p a b c", a=NTT, b=4)
        gbc = p.sb("gbc", [128, D])
        mixA = p.sb("mixA", [128, 8192], BF16)
        mixB = p.sb("mixB", [128, 8256], BF16)
        mixC = p.sb("mixC", [128, 4096], BF16)
        Sg = p.sb("Sg", [128, 2, 128]); Sgb = p.sb("Sgb", [128, 2, 128], BF16)
        CT = p.sb("CT", [128, 4, 2, 258]); CTb = p.sb("CTb", [128, 4, 2, 258], BF16)
        mprev = p.sb("mprev", [128, 4])
        identf = p.sb("identf", [128, 128]); identb = p.sb("identb", [128, 128], BF16)
        tri = p.sb("tri", [128, 128]); trirev = p.sb("trirev", [128, 128]); negts = p.sb("negts", [128, 128])
        pos4 = p.sb("pos4", [128, 128]); negbf = p.sb("negbf", [128, 128], BF16)
        onesf = p.sb("onesf", [128, 128]); onesb = p.sb("onesb", [128, 128], BF16)
        abp = p.sb("abp", [128, 4 * NTT]); iota = p.sb("iota", [128, 1])
        bm = p.sb("bm", [8, 8])
        aqk_bc = p.sb("aqk_bc", [128, 128]); ahn_bc = p.sb("ahn_bc", [128, 512]); bhn_bc = p.sb("bhn_bc", [128, 512])
        chn_bc = p.sb("chn_bc", [128, 1024]); cgb_bc = p.sb("cgb_bc", [128, 8]); alam_bc = p.sb("alam_bc", [128, 256])
        W2aug = p.sb("W2aug", [32, 256])
        neglam = p.sb("neglam", [128, 1])
        sm1 = p.sb("sm1", [128, 64]); sm2 = p.sb("sm2", [128, 64]); sm3 = p.sb("sm3", [128, 64])
        s512a = p.sb("s512a", [128, 512]); s512b = p.sb("s512b", [128, 512]); s512c = p.sb("s512c", [128, 512])
        b512a = p.sb("b512a", [128, 512], BF16); b512b = p.sb("b512b", [128, 512], BF16); b512c = p.sb("b512c", [128, 512], BF16)
        sg = [s512a, s512b]
        bgT = p.sb("bgT", [32, NMAX])
        gates = p.sb("gates", [128, STT, 8])
        aout = p.sb("aout", [128, 1024])
        PS = [p.psum("ps%d" % i, [128, 512]) for i in range(8)]
        st = {"rot": 0, "w": 0, "sg": 0}
        def rot():
            b = PS[st["rot"]]; st["rot"] = (st["rot"] + 1) % 4
            return b
        def nextW():
            w = Wb[st["w"]]; st["w"] = (st["w"] + 1) % NW
            return w

        def mm(out, lhsT, rhs, start, stop):
            p.I("pe", "matmul", {"out": out}, {"lhsT": lhsT, "rhs": rhs}, start=start, stop=stop)
        def tr(out, in_, ident):
            p.I("pe", "transpose", {"out": out}, {"in_": in_, "identity": ident})
        def act(out, in_, func, bias=None, scale=None, accum=None):
            outs = {"out": out}; ins = {"in_": in_}
            kw = {"func": func}
            if accum is not None: outs["accum_out"] = accum
            if bias is not None: ins["bias"] = bias
            if scale is not None: ins["scale"] = scale
            p.I("act", "activation", outs, ins, **kw)
        def tt_(out, in0, in1, op, e="dve"):
            p.I(e, "tensor_tensor", {"out": out}, {"in0": in0, "in1": in1}, op=op)
        def ts(out, in0, s1, op0, s2=None, op1=None, e="dve"):
            kw = {"op0": op0}
            if op1 is not None: kw["op1"] = op1
            p.I(e, "tensor_scalar", {"out": out}, {"in0": in0, "scalar1": s1, "scalar2": s2}, **kw)
        def stt(out, in0, scalar, in1, op0, op1):
            p.I("dve", "scalar_tensor_tensor", {"out": out}, {"in0": in0, "scalar": scalar, "in1": in1}, op0=op0, op1=op1)
        def red(out, in_, op=ALU.add):
            p.I("dve", "tensor_reduce", {"out": out}, {"in_": in_}, axis=AX.X, op=op)
        def recip(out, in_):
            p.I("dve", "reciprocal", {"out": out}, {"in_": in_})
        def cp(out, in_, e="dve"):
            if e == "act":
                act(out, in_, AF.Copy)
            else:
                p.I(e, "tensor_copy", {"out": out}, {"in_": in_})
        def memset(v, val, e="dve"):
            p.I(e, "memset", {"ap": v}, {}, constant=val)
        def loadW(dst, W2d, r0, nk, c0, ncol):
            src = W2d[r0:r0 + nk * 128, c0:c0 + ncol].rearrange("(k p) f -> p k f", p=128)
            p.dma("pool", dst, src)
        def rstd_of(out, ss, n, sl):
            act(out, ss, AF.Sqrt, bias=eps_t[sl, 0:1], scale=1.0 / n)
            recip(out, out)

        eps_t = p.sb("eps_t", [128, 1]); one_t = p.sb("one_t", [128, 1])
        memset(eps_t[:, :], EPS); memset(one_t[:, :], 1.0)

        for t_, d_ in ((identf, c_identf), (tri, c_tri), (trirev, c_trirev), (negts, c_negts), (pos4, c_pos4),
                       (abp, c_abp), (iota, c_iota), (bm, c_bm)):
            p.dma("sp", t_[:], d_[:, :] if len(d_.shape) == 2 else d_)
        p.dma("pool", negbf[:], c_negbf[:, :])
        cp(identb[:, :], identf[:, :])
        memset(onesf[:, :], 1.0); memset(onesb[:, :], 1.0)
        for t_, d_, n_ in ((aqk_bc, aqk, 128), (ahn_bc, ahn, 512), (bhn_bc, bhn, 512), (chn_bc, chn, 1024),
                           (cgb_bc, cgb, 8), (alam_bc, alam, 256)):
            p.dma("sp", t_[:], d_[0:1, :].partition_broadcast(128))
        memset(W2aug[:, :], 0.0)
        p.dma("sp", W2aug[0:16, :], bw2[:, :])
        p.dma("sp", W2aug[16:17, :], bgb[0:1, :])
        memset(bgT[:, :], 1.0)
        memset(Vst.re("p a b c -> p (a b) c")[:, :, 128:130], 1.0)
        tt_(s512a[:, 0:64], alam_bc[:, 0:64], alam_bc[:, 64:128], ALU.mult)
        tt_(s512a[:, 64:128], alam_bc[:, 128:192], alam_bc[:, 192:256], ALU.mult)
        red(sm1[:, 0:2], s512a[:, 0:128].re("p (a b) -> p a b", a=2))
        act(sm1[:, 2:4], sm1[:, 0:2], AF.Exp)
        tt_(sm1[:, 4:5], sm1[:, 3:4], sm1[:, 2:3], ALU.subtract)
        ts(neglam[:, :], sm1[:, 4:5], -0.2, ALU.add)

        hT = mixA
        def hT_view(fb, N):
            per = 8192 // NMAX
            reg = (mixA, mixB, mixC)[fb // per]
            o = (fb % per) * NMAX
            return reg[:, o:o + N]

        def norm_T(nt, gi):
            p.dma("sp", gbc[:], normg[gi:gi + 1, :].partition_broadcast(128))
            for t in range(nt):
                act(junk[:, :], X[:, t, :], AF.Square, accum=sm1[:, 8:9])
                rstd_of(sm1[:, 9:10], sm1[:, 8:9], D, slice(0, 128))
                stt(tokb[:, :], X[:, t, :], sm1[:, 9:10], gbc[:, :], ALU.mult, ALU.mult)
                tok_to_actT(t)
        def tok_to_actT(t):
            ps = rot(); psb = ps[:, :].bitcast(BF16)
            for kc in range(8):
                tr(psb[:, kc * 128:(kc + 1) * 128], tokb[:, kc * 128:(kc + 1) * 128], identb[:, :])
            cp(actT[:, :, t * 128:(t + 1) * 128], psb[:, 0:1024].re("p (k n) -> p k n", k=8), e="act")

        def ffn(f, gi, nt):
            N = nt * 128
            norm_T(nt, gi)
            for sbk in range(6):
                c0 = sbk * 512; ncol = min(512, DFF - c0)
                wg_ = nextW(); loadW(wg_[:, :, 0:ncol], Wg[f], 0, 8, c0, ncol)
                wu_ = nextW(); loadW(wu_[:, :, 0:ncol], Wu[f], 0, 8, c0, ncol)
                for q in range(ncol // 128):
                    fb = sbk * 4 + q
                    pg = rot(); pu = rot()
                    for kc in range(8):
                        mm(pg[:, 0:N], wg_[:, kc, q * 128:(q + 1) * 128], actT[:, kc, 0:N], kc == 0, kc == 7)
                    for kc in range(8):
                        mm(pu[:, 0:N], wu_[:, kc, q * 128:(q + 1) * 128], actT[:, kc, 0:N], kc == 0, kc == 7)
                    s_ = sg[st["sg"]]; st["sg"] ^= 1
                    act(s_[:, 0:N], pg[:, 0:N], AF.Silu)
                    tt_(hT_view(fb, N), s_[:, 0:N], pu[:, 0:N], ALU.mult)
            for db in range(2):
                for g in range(6):
                    nf = min(4, 22 - g * 4)
                    w = nextW()
                    wv = w[:].re("p k f -> p (k f)")[:, 0:nf * 512].re("p (k f) -> p k f", k=nf)
                    loadW(wv, Wd[f], g * 512, nf, db * 512, 512)
                    for t in range(nt):
                        for q in range(nf):
                            fc = g * 4 + q
                            mm(PS[4 + t][:, :], hT_view(fc, NMAX)[:, t * 128:(t + 1) * 128], wv[:, q, :], fc == 0, fc == 21)
                for t in range(nt):
                    stt(X[:, t, db * 512:(db + 1) * 512], PS[4 + t][:, :], 0.5, X[:, t, db * 512:(db + 1) * 512], ALU.mult, ALU.add)

        def out_proj(W2d, nt):
            for db in range(2):
                w = nextW(); loadW(w[:, :, :], W2d, 0, 8, db * 512, 512)
                for t in range(nt):
                    ps = rot()
                    for kc in range(8):
                        mm(ps[:, :], actT[:, kc, t * 128:(t + 1) * 128], w[:, kc, :], kc == 0, kc == 7)
                    tt_(X[:, t, db * 512:(db + 1) * 512], X[:, t, db * 512:(db + 1) * 512], ps[:, :], ALU.add)

        def proj_blocks(W2d, nt, c0, ncol, consumer):
            w = nextW(); loadW(w[:, :, 0:ncol], W2d, 0, 8, c0, ncol)
            for t in range(nt):
                ps = rot()
                for kc in range(8):
                    mm(ps[:, 0:ncol], actT[:, kc, t * 128:(t + 1) * 128], w[:, kc, 0:ncol], kc == 0, kc == 7)
                consumer(t, ps)

        def headnorm(src, nh, dh, gain_bc, sl, extra_scale=None):
            n = nh * dh
            sq = s512a[sl, 0:n] if n <= 512 else gbc[sl, 0:n]
            act(sq, src, AF.Square)
            red(sm2[sl, 0:nh], sq.re("p (a b) -> p a b", a=nh))
            rstd_of(sm2[sl, 8:8 + nh], sm2[sl, 0:nh], dh, sl)
            if extra_scale is not None:
                ts(sm2[sl, 8:8 + nh], sm2[sl, 8:8 + nh], extra_scale, ALU.mult)
            tt_(src.re("p (a b) -> p a b", a=nh), src.re("p (a b) -> p a b", a=nh),
                sm2[sl, 8:8 + nh].un(2).bc([sl.stop - sl.start, nh, dh]), ALU.mult)
            tt_(src, src, gain_bc[sl, 0:n], ALU.mult)

        def QT(h, c0, n): return mixA[:, h * NMAX + c0: h * NMAX + c0 + n]
        def vB(t): return mixA[:, 4096 + t * 512: 4096 + (t + 1) * 512]
        def qkB(t): return mixB[:, t * 1024:(t + 1) * 1024].bitcast(F32)
        def srB(t): return mixB[:, 4096 + t * 1024: 4096 + (t + 1) * 1024].bitcast(F32)
        def qnS(t): return mixC[:, t * 1024:(t + 1) * 1024].bitcast(F32)
        def knS(t): return mixC[:, 2048 + t * 1024: 2048 + (t + 1) * 1024].bitcast(F32)
        def vnS(t): return SB["vn"][:, t * 512:(t + 1) * 512]
        SB = {}

        def qknorm(ps, which, dst):
            act(s512a[:, :], ps[:, :], AF.Square)
            red(sm2[:, 0:8], s512a[:, :].re("p (a b) -> p a b", a=8))
            rstd_of(sm2[:, 8:16], sm2[:, 0:8], 64, slice(0, 128))
            tt_(dst.re("p (a b) -> p a b", a=8), ps[:, :].re("p (a b) -> p a b", a=8), sm2[:, 8:16].un(2).bc([128, 8, 64]), ALU.mult)
            tt_(dst.re("p (a b) -> p a b", a=8), dst.re("p (a b) -> p a b", a=8),
                aqk_bc[:, which * 64:(which + 1) * 64].un(1).bc([128, 8, 64]), ALU.mult)

        def even_mixer(nt, g0, sample):
            N = nt * 128
            norm_T(nt, 1)
            def c_q(t, ps):
                dst = qnS(t) if sample else s512b[:, :]
                qknorm(ps, 0, dst)
                if not sample:
                    cp(b512a[:, :], dst, e="act")
                    pt_ = rot(); ptb_ = pt_[:, :].bitcast(BF16)
                    for h in range(4):
                        tr(ptb_[:, h * 128:(h + 1) * 128], b512a[:, h * 128:(h + 1) * 128], identb[:, :])
                    for h in range(4):
                        cp(QT(h, t * 128, 128), ptb_[:, h * 128:(h + 1) * 128])
            def c_k(t, ps):
                dst = knS(t) if sample else s512c[:, :]
                qknorm(ps, 1, dst)
                if sample:
                    for r2 in range(2):
                        p.dma("sp", ks[2 * t + r2:2 * t + r2 + 1, :], dst[64 * r2:64 * r2 + 1, :])
                else:
                    g = g0 + t
                    p.dma("sp", kp[g * 128:(g + 1) * 128, :], dst)
                    cp(b512b[:, :], dst, e="act")
                    pt_ = rot(); ptb_ = pt_[:, :].bitcast(BF16)
                    for h in range(4):
                        tr(ptb_[:, h * 128:(h + 1) * 128], b512b[:, h * 128:(h + 1) * 128], identb[:, :])
                    cp(KT[:, :, g * 128:(g + 1) * 128], ptb_[:, 0:512].re("p (h n) -> p h n", h=4))
            def c_v(t, ps):
                cp(s512b[:, :], ps[:, :], e="act")
                if sample:
                    for r2 in range(2):
                        p.dma("sp", vs[2 * t + r2:2 * t + r2 + 1, :], s512b[64 * r2:64 * r2 + 1, :])
                    cp(vnS(t), ps[:, :])
                else:
                    g = g0 + t
                    p.dma("sp", vp[g * 128:(g + 1) * 128, :], s512b[:, :])
                    cp(Vst[:, g, :, 0:128], ps[:, :].re("p (h v) -> p h v", h=4))
            def c_qkB(t, ps): cp(qkB(t), ps[:, :], e="act")
            def c_vB(t, ps): cp(vB(t), ps[:, :])
            def c_r(t, ps): act(srB(t), ps[:, :], AF.Silu)
            proj_blocks(EWin, nt, 0, 512, c_q)
            proj_blocks(EWin, nt, 512, 512, c_k)
            proj_blocks(EWin, nt, 1024, 512, c_v)
            proj_blocks(EWin, nt, 1536, 512, c_qkB)
            proj_blocks(EWin, nt, 2048, 512, c_vB)
            proj_blocks(EWin, nt, 2576, 512, c_r)
            if KSUB == 1: return
            w = nextW(); loadW(w[:, :, 0:16], EWin, 0, 8, 2560, 16)
            ps = rot()
            for kc in range(8):
                mm(ps[0:16, 0:N], w[:, kc, 0:16], actT[:, kc, 0:N], kc == 0, kc == 7)
            cp(bgT[0:16, 0:N], ps[0:16, 0:N])
            if KSUB == 2: return

            for t in range(nt):
                if sample:
                    for r2 in range(2):
                        r = 2 * t + r2; sl = slice(64 * r2, 64 * r2 + 1)
                        sample_attn(r, t, sl)
                        p.dma("sp", Sg[:], sgla[r].rearrange("(j hh) d v -> (hh d) j v", hh=2))
                        cp(Sgb[:, :, :], Sg[:, :, :], e="act")
                        gla_chunk(t, sl)
                        p.dma("sp", glas[r].rearrange("(j hh) d v -> (hh d) j v", hh=2), Sg[:])
                    sl_all = slice(0, 128)
                else:
                    attn_tile(t, g0 + t)
                    if KSUB == 3: return
                    if KSUB == 8 and t == 1: return
                    if not os.environ.get("KSKIPGLA"): gla_chunk(t, slice(0, 128))
                    if KSUB == 4: return
                headnorm(aout[:, 0:512], 4, 128, ahn_bc, slice(0, 128), extra_scale=0.8)
                headnorm(aout[:, 512:1024], 4, 128, bhn_bc, slice(0, 128))
                tt_(aout[:, 512:1024], aout[:, 512:1024], srB(t), ALU.mult)
                if KSUB == 5: return
                cp(tokb[:, :], aout[:, :], e="act")
                tok_to_actT(t)
                if KSUB == 6: return
            if KSUB == 7: return
            out_proj(EWout, nt)

        def attn_tile(t, i):
            for h in range(4):
                a0 = PS[4 + 2 * (h % 2)]; a1 = PS[5 + 2 * (h % 2)]
                for kb in range(i + 1):
                    pT = (b512a, b512b, b512c)[kb % 3]
                    for m in range(2):
                        s_ = rot()
                        mm(s_[:, 0:128], KT[64 * m:64 * m + 64, h, kb * 128:(kb + 1) * 128],
                           QT(h, t * 128, 128)[64 * m:64 * m + 64, :], True, kb != i)
                        if kb == i:
                            mm(s_[:, 0:128], identb[:, :], negbf[:, :], False, True)
                        act(pT[:, m * 128:(m + 1) * 128], s_[:, 0:128], AF.Exp,
                            bias=abp[:, h * NTT + (i - kb): h * NTT + (i - kb) + 1], scale=0.125)
                    mm(a0[:, 0:129], pT[:, 0:128], Vst[:, kb, h, 0:129], kb == 0, kb == i)
                    mm(a1[:, 0:129], pT[:, 128:256], Vst[:, kb, h, 0:129], kb == 0, kb == i)
                recip(sm3[:, 0:1], a0[:, 128:129]); recip(sm3[:, 1:2], a1[:, 128:129])
                tt_(sm3[:, 2:3], sm3[:, 1:2], neglam[:, :], ALU.mult)
                ts(s512a[:, 0:128], a0[:, 0:128], sm3[:, 0:1], ALU.mult)
                stt(aout[:, h * 128:(h + 1) * 128], a1[:, 0:128], sm3[:, 2:3], s512a[:, 0:128], ALU.mult, ALU.add)

        def sample_attn(r, t, sl):
            qn = qnS(t); kn = knS(t)
            ps = rot()
            mm(ps[:, :], onesf[sl, 0:128], qn[sl, :], True, True)
            qbc = SB["qbc"][:, :].bitcast(F32); abs_ = SB["abs"][:, :].bitcast(F32); idx = SB["idx"][:, :].bitcast(I32)
            cp(qbc, ps[:, :], e="act")
            tt_(s512a[sl, :], qn[sl, :], kn[sl, :], ALU.mult)
            red(sm3[sl, 8:16], s512a[sl, :].re("p (a b) -> p a b", a=8))
            act(b512c[sl, 0:8], sm3[sl, 8:16], AF.Exp, scale=0.125)
            acc = PS[4]; den = PS[5]
            for j in range(NPG):
                Kb = SB["K%d" % (j % 2)][:, :].bitcast(F32); Vb_ = SB["V%d" % (j % 2)][:, :].bitcast(F32)
                col = r * NPG + j
                p.dma("pool", Kb, ck[:, :], indirect=idx[:, col:col + 1])
                p.dma("pool", Vb_, cv[:, :], indirect=idx[:, col:col + 1])
                tt_(s512b[:, :], Kb, qbc, ALU.mult)
                red(sm3[:, 16:24], s512b[:, :].re("p (a b) -> p a b", a=8))
                stt(sm3[:, 24:32], sm3[:, 16:24], 0.125, abs_[:, j * 8:(j + 1) * 8], ALU.mult, ALU.add)
                pb = b512a if j % 2 == 0 else b512b
                act(pb[:, 0:8], sm3[:, 24:32], AF.Exp)
                vbf = SB["vb%d" % (j % 2)][:, :]
                cp(vbf, Vb_, e="act")
                mm(acc[0:8, :], pb[:, 0:8], vbf, j == 0, False)
                mm(den[0:8, 0:1], pb[:, 0:8], onesb[:, 0:1], j == 0, False)
            mm(acc[0:8, :], b512c[sl, 0:8], vnS(t)[sl, :], NPG == 0, True)
            mm(den[0:8, 0:1], b512c[sl, 0:8], onesb[sl, 0:1], NPG == 0, True)
            recip(sm3[0:8, 32:33], den[0:8, 0:1])
            ts(s512a[0:8, :], acc[0:8, :], sm3[0:8, 32:33], ALU.mult)
            v3 = lambda x: x.re("p (h v) -> p h v", h=4)
            tt_(v3(s512b[0:8, :]), v3(s512a[0:8, :]), bm[0:8, 0:4].un(2).bc([8, 4, 128]), ALU.mult)
            tt_(v3(s512c[0:8, :]), v3(s512a[0:8, :]), bm[0:8, 4:8].un(2).bc([8, 4, 128]), ALU.mult)
            stt(s512b[0:8, :], s512c[0:8, :], neglam[0:8, 0:1], s512b[0:8, :], ALU.mult, ALU.add)
            ps2 = rot()
            mm(ps2[sl, :], onesf[0:8, 0:1], s512b[0:8, :], True, True)
            cp(aout[sl, 0:512], ps2[sl, :])

        def gla_chunk(t, sl):
            n = sl.stop - sl.start; c0 = t * 128 + sl.start
            ps = rot()
            mm(ps[sl, 0:256], bgT[0:17, c0:c0 + n], W2aug[0:17, :], True, True)
            act(s512a[sl, 0:256], ps[sl, 0:256], AF.Exp, scale=-1.0)
            act(s512a[sl, 0:256], s512a[sl, 0:256], AF.Ln, bias=one_t[sl, 0:1])
            ts(s512a[sl, 0:256], s512a[sl, 0:256], -1.0 / 16.0, ALU.mult)
            ps = rot()
            mm(ps[sl, 0:256], tri[sl, sl], s512a[sl, 0:256], True, True)
            mm(ps[sl, 256:512], trirev[sl, sl], s512a[sl, 0:256], True, True)
            psl = rot()
            for j in range(2):
                mm(psl[:, j:j + 1], s512a[sl, j * 128:(j + 1) * 128], onesf[sl, 0:1], True, True)
            act(sm3[:, 40:42], psl[:, 0:2], AF.Exp)
            act(s512b[sl, 0:256], ps[sl, 0:256], AF.Exp)
            act(s512b[sl, 256:512], ps[sl, 0:256], AF.Exp, scale=-1.0)
            act(s512c[sl, 0:256], ps[sl, 256:512], AF.Exp)
            qk = qkB(t)
            stt(b512a[sl, 0:256], qk[sl, 0:256], 0.125, s512b[sl, 0:256], ALU.mult, ALU.mult)
            tt_(b512a[sl, 256:512], qk[sl, 256:512], s512b[sl, 256:512], ALU.mult)
            tt_(b512b[sl, 0:256], qk[sl, 256:512], s512c[sl, 0:256], ALU.mult)
            pt_ = rot(); ptb_ = pt_[:, :].bitcast(BF16)
            for q in range(4):
                tr(ptb_[:, q * 128:q * 128 + n], b512a[sl, q * 128:(q + 1) * 128], identb[sl, sl])
            for q in range(4):
                cp(b512c[:, q * 128:q * 128 + n], ptb_[:, q * 128:q * 128 + n], e="act")
            for h in range(4):
                j = h // 2; hp = slice(64 * (h % 2), 64 * (h % 2) + 64)
                pa = rot()
                mm(pa[sl, 0:n], b512c[hp, (2 + j) * 128:(2 + j) * 128 + n], b512c[hp, j * 128:j * 128 + n], True, True)
                tt_(b512b[sl, 256:256 + n], pa[sl, 0:n], tri[sl, sl], ALU.mult)
                po = rot(); po2 = rot()
                mm(po[sl, 0:128], b512b[sl, 256:256 + n], vB(t)[sl, h * 128:(h + 1) * 128], True, True)
                mm(po2[sl, 0:128], b512c[hp, j * 128:j * 128 + n], Sgb[hp, j, :], True, True)
                cp(aout[sl, 512 + h * 128:512 + (h + 1) * 128], po[sl, 0:128], e="act")
                tt_(aout[sl, 512 + h * 128:512 + (h + 1) * 128], aout[sl, 512 + h * 128:512 + (h + 1) * 128], po2[sl, 0:128], ALU.add)
            for h in range(4):
                j = h // 2; hp = slice(64 * (h % 2), 64 * (h % 2) + 64)
                pS = rot()
                mm(pS[:, 0:128], b512b[sl, j * 128:(j + 1) * 128], vB(t)[sl, h * 128:(h + 1) * 128], True, True)
                stt(Sg[hp, j, :], Sg[hp, j, :], sm3[hp, 40 + j:41 + j], pS[hp, 0:128], ALU.mult, ALU.add)
            cp(Sgb[:, :, :], Sg[:, :, :], e="act")

        def qT_o(i, c0, n): return mixA[:, i * NMAX + c0: i * NMAX + c0 + n]
        def kT_o(i, c0, n): return mixA[:, 4096 + i * NMAX + c0: 4096 + i * NMAX + c0 + n]
        def ktok(t): return mixB[:, t * 1024:(t + 1) * 1024]
        def vaug(t): return mixB[:, 4096 + t * 1032: 4096 + (t + 1) * 1032].re("p (h v) -> p h v", h=4)
        def so(t): return mixC[:, t * 1024:(t + 1) * 1024]

        def odd_mixer(nt, sample):
            N = nt * 128
            norm_T(nt, 4)
            for t in range(nt):
                memset(vaug(t)[:, :, 256:258], 1.0)
            def c_q(blk):
                def c(t, ps):
                    cp(b512a[:, :], ps[:, :], e="act")
                    pt_ = rot(); ptb_ = pt_[:, :].bitcast(BF16)
                    for q in range(4):
                        tr(ptb_[:, q * 128:(q + 1) * 128], b512a[:, q * 128:(q + 1) * 128], identb[:, :])
                    for q in range(4):
                        cp(qT_o(blk * 4 + q, t * 128, 128), ptb_[:, q * 128:(q + 1) * 128])
                return c
            def c_k(blk):
                def c(t, ps):
                    act(ktok(t)[:, blk * 512:(blk + 1) * 512], ps[:, :], AF.Copy, scale=1.0 / 16.0)
                    pt_ = rot(); ptb_ = pt_[:, :].bitcast(BF16)
                    for q in range(4):
                        tr(ptb_[:, q * 128:(q + 1) * 128], ktok(t)[:, blk * 512 + q * 128: blk * 512 + (q + 1) * 128], identb[:, :])
                    for q in range(4):
                        cp(kT_o(blk * 4 + q, t * 128, 128), ptb_[:, q * 128:(q + 1) * 128])
                return c
            def c_v(blk):
                def c(t, ps):
                    cp(vaug(t)[:, 2 * blk:2 * blk + 2, 0:256], ps[:, :].re("p (h v) -> p h v", h=2))
                return c
            def c_o(blk):
                def c(t, ps):
                    act(so(t)[:, blk * 512:(blk + 1) * 512], ps[:, :], AF.Sigmoid)
                return c
            def c_g(t, ps):
                tt_(gates[:, t, :], ps[:, 0:8], cgb_bc[:, :], ALU.add)
            for blk in range(2): proj_blocks(OWin, nt, blk * 512, 512, c_q(blk))
            for blk in range(2): proj_blocks(OWin, nt, 1024 + blk * 512, 512, c_k(blk))
            for blk in range(2): proj_blocks(OWin, nt, 2048 + blk * 512, 512, c_v(blk))
            for blk in range(2): proj_blocks(OWin, nt, 3072 + blk * 512, 512, c_o(blk))
            proj_blocks(OWin, nt, 4096, 8, c_g)
            for t in range(nt):
                if sample:
                    for r2 in range(2):
                        r = 2 * t + r2; sl = slice(64 * r2, 64 * r2 + 1)
                        load_C(r)
                        mlstm_chunk(t, sl)
                        store_C(Cs[r], ns[r], ms[r:r + 1, :])
                else:
                    mlstm_chunk(t, slice(0, 128))
                headnorm(aout[:, 0:1024], 4, 256, chn_bc, slice(0, 128))
                tt_(aout[:, :], aout[:, :], so(t), ALU.mult)
                cp(tokb[:, :], aout[:, :], e="act")
                tok_to_actT(t)
            out_proj(OWout, nt)

        def load_C(r):
            for h in range(4):
                for vc in range(2):
                    p.dma("sp", s512a[:, 0:256], sC[r, h, vc * 128:(vc + 1) * 128, :])
                    ps = rot()
                    for dc in range(2):
                        tr(ps[:, dc * 128:(dc + 1) * 128], s512a[:, dc * 128:(dc + 1) * 128], identf[:, :])
                    for dc in range(2):
                        cp(CT[:, h, dc, vc * 128:(vc + 1) * 128], ps[:, dc * 128:(dc + 1) * 128])
                for dc in range(2):
                    p.dma("sp", CT[:, h, dc, 256:257], sn[r, h:h + 1, dc * 128:(dc + 1) * 128].rearrange("o d -> d o"))
            p.dma("sp", mprev[:], sm[r:r + 1, :].partition_broadcast(128))
            cp(CTb[:].re("p a b c -> p (a b c)"), CT[:].re("p a b c -> p (a b c)"), e="act")

        def store_C(Cd, nd, md):
            for h in range(4):
                for vc in range(2):
                    ps = rot()
                    for dc in range(2):
                        tr(ps[:, dc * 128:(dc + 1) * 128], CT[:, h, dc, vc * 128:(vc + 1) * 128], identf[:, :])
                    cp(s512b[:, 0:256], ps[:, 0:256])
                    p.dma("sp", Cd[h, vc * 128:(vc + 1) * 128, :], s512b[:, 0:256])
                for dc in range(2):
                    p.dma("sp", nd[h:h + 1, dc * 128:(dc + 1) * 128].rearrange("o d -> d o"), CT[:, h, dc, 256:257])
            p.dma("sp", md, mprev[0:1, :])

        def mlstm_chunk(t, sl):
            n = sl.stop - sl.start; c0 = t * 128 + sl.start
            g = gates
            ipre = g[sl, t, 0:4]
            act(sm1[sl, 16:20], g[sl, t, 4:8], AF.Exp, scale=-1.0)
            act(sm1[sl, 16:20], sm1[sl, 16:20], AF.Ln, bias=one_t[sl, 0:1])
            ts(sm1[sl, 16:20], sm1[sl, 16:20], -1.0, ALU.mult)
            ps = rot()
            mm(ps[sl, 0:4], tri[sl, sl], sm1[sl, 16:20], True, True)
            mm(ps[sl, 4:8], onesf[sl, sl], sm1[sl, 16:20], True, True)
            cp(sm1[sl, 20:28], ps[sl, 0:8])
            tt_(sm1[sl, 28:32], ipre, sm1[sl, 20:24], ALU.subtract)
            for h in range(4):
                ts(s512a[sl, h * 128:h * 128 + n], identf[sl, sl], sm1[sl, 28 + h:29 + h], ALU.mult)
            pa = rot()
            mm(pa[sl, :].re("p (h s) -> p h s", h=4)[:, :, 0:n], onesf[sl, sl], s512a[sl, :].re("p (h s) -> p h s", h=4)[:, :, 0:n], True, True)
            red(sm1[sl, 32:36], pa[sl, :].re("p (h s) -> p h s", h=4)[:, :, 0:n], op=ALU.max)
            tt_(sm1[sl, 32:36], sm1[sl, 32:36], mprev[sl, :], ALU.max)
            tt_(s512b[sl, :].re("p (h s) -> p h s", h=4)[:, :, 0:n], pa[sl, :].re("p (h s) -> p h s", h=4)[:, :, 0:n],
                negts[sl, sl].un(1).bc([n, 4, n]), ALU.add)
            red(sm1[sl, 36:40], s512b[sl, :].re("p (h s) -> p h s", h=4)[:, :, 0:n], op=ALU.max)
            tt_(sm1[sl, 36:40], sm1[sl, 36:40], mprev[sl, :], ALU.max)
            tt_(sm1[sl, 40:44], sm1[sl, 20:24], sm1[sl, 36:40], ALU.add)
            act(sm1[sl, 44:48], sm1[sl, 40:44], AF.Exp, scale=-1.0)
            tt_(sm1[sl, 48:52], mprev[sl, :], sm1[sl, 36:40], ALU.subtract)
            act(sm1[sl, 48:52], sm1[sl, 48:52], AF.Exp)
            for h in range(4):
                ts(s512a[sl, h * 128:h * 128 + n], identf[sl, sl], sm1[sl, 36 + h:37 + h], ALU.mult)
            pm = rot()
            mv = pm[sl, :].re("p (h s) -> p h s", h=4)[:, :, 0:n]
            for h in range(4):
                mm(pm[sl, h * 128:h * 128 + n], onesf[sl, sl], s512a[sl, h * 128:h * 128 + n], True, False)
                mm(pm[sl, h * 128:h * 128 + n], identf[sl, sl], pos4[sl, sl.start:sl.start + n], False, True)
            for h in range(4):
                act(s512c[sl, h * 128:h * 128 + n], pm[sl, h * 128:h * 128 + n], AF.Exp, bias=sm1[sl, 28 + h:29 + h], scale=-1.0)
            tt_(sm1[sl, 52:56], sm1[sl, 28:32], sm1[sl, 32:36], ALU.subtract)
            act(sm1[sl, 52:56], sm1[sl, 52:56], AF.Exp)
            tt_(sm1[sl, 56:60], mprev[sl, :], sm1[sl, 32:36], ALU.subtract)
            act(sm1[sl, 56:60], sm1[sl, 56:60], AF.Exp)
            pb_ = rot()
            for h in range(4):
                ts(s512a[sl, h * 128:h * 128 + n], identf[sl, sl], sm1[sl, 56 + h:57 + h], ALU.mult)
            mm(pb_[:, 0:4], onesf[sl, 0:128], s512a[sl, :].re("p (h s) -> p h s", h=4)[:, :, 0:1], True, True)
            cp(sm2[:, 16:20], pb_[:, 0:4])
            tt_(sm1[sl, 60:64], sm1[sl, 24:28], sm1[sl, 32:36], ALU.add)
            for h in range(4):
                ts(s512a[sl, h * 128:h * 128 + n], identf[sl, sl], sm1[sl, 60 + h:61 + h], ALU.mult)
            pb2 = rot()
            mm(pb2[:, 0:4], onesf[sl, 0:128], s512a[sl, :].re("p (h s) -> p h s", h=4)[:, :, 0:1], True, True)
            cp(sm2[:, 20:24], pb2[:, 0:4])
            for h in range(4):
                pk = rot()
                for dc in range(2):
                    mm(pk[sl, 0:n], kT_o(h * 2 + dc, c0, n), qT_o(h * 2 + dc, c0, n), dc == 0, dc == 1)
                tt_(b512b[sl, 0:n], pk[sl, 0:n], s512c[sl, h * 128:h * 128 + n], ALU.mult)
                pn1 = PS[4 + (h % 2) * 2]; pn2 = PS[5 + (h % 2) * 2]
                mm(pn1[sl, 0:257], b512b[sl, 0:n], vaug(t)[sl, h, 0:257], True, True)
                for dc in range(2):
                    mm(pn2[sl, 0:257], qT_o(h * 2 + dc, c0, n), CTb[:, h, dc, 0:257], dc == 0, dc == 1)
                ts(s512b[sl, 0:257], pn2[sl, 0:257], sm1[sl, 48 + h:49 + h], ALU.mult)
                tt_(s512b[sl, 0:257], s512b[sl, 0:257], pn1[sl, 0:257], ALU.add)
                ts(sm2[sl, 26:27], s512b[sl, 256:257], -1.0, ALU.mult)
                tt_(sm2[sl, 24:25], s512b[sl, 256:257], sm2[sl, 26:27], ALU.max)
                tt_(sm2[sl, 24:25], sm2[sl, 24:25], sm1[sl, 44 + h:45 + h], ALU.max)
                recip(sm2[sl, 25:26], sm2[sl, 24:25])
                ts(aout[sl, h * 256:(h + 1) * 256], s512b[sl, 0:256], sm2[sl, 25:26], ALU.mult)
            for h in range(4):
                ts(b512a[sl, 0:256], ktok(t)[sl, h * 256:(h + 1) * 256], sm1[sl, 52 + h:53 + h], ALU.mult)
                for dc in range(2):
                    pc = rot()
                    mm(pc[:, 0:257], b512a[sl, dc * 128:(dc + 1) * 128], vaug(t)[sl, h, 0:257], True, True)
                    stt(CT[:, h, dc, 0:257], CT[:, h, dc, 0:257], sm2[:, 16 + h:17 + h], pc[:, 0:257], ALU.mult, ALU.add)
            cp(CTb[:].re("p a b c -> p (a b c)"), CT[:].re("p a b c -> p (a b c)"), e="act")
            cp(mprev[:, :], sm2[:, 20:24])

        def main_prog():
            memset(Sg[:, :, :], 0.0); memset(Sgb[:, :, :], 0.0)
            memset(CT[:].re("p a b c -> p (a b c)"), 0.0); memset(CTb[:].re("p a b c -> p (a b c)"), 0.0)
            memset(mprev[:, :], 0.0)
            if KSTOP == 1: return
            for s_i in range(NST):
                g0 = s_i * STT
                for t in range(STT):
                    p.dma("sp", X[:, t, :], xp[(g0 + t) * 128:(g0 + t + 1) * 128, :])
                ffn(0, 0, STT)
                if KSTOP == 2: return
                even_mixer(STT, g0, False)
                if KSTOP == 3: return
                ffn(1, 2, STT)
                ffn(2, 3, STT)
                odd_mixer(STT, False)
                if KSTOP == 4: return
                ffn(3, 5, STT)
                for t in range(STT):
                    p.dma("sp", yp[(g0 + t) * 128:(g0 + t + 1) * 128, :], X[:, t, :])
            p.dma("sp", glap.rearrange("(j hh) d v -> (hh d) j v", hh=2), Sg[:])
            store_C(Cp, np_, mp[0:1, :])
            if KSTOP == 5: return
            for i_, nm in enumerate(("K0", "V0", "K1", "V1", "qbc")):
                SB[nm] = KTbuf.sub((slice(None), slice(i_ * 1024, (i_ + 1) * 1024)), nm)
            SB["vb0"] = KTbuf.sub((slice(None), slice(5120, 5632)), "vb0"); SB["vb1"] = KTbuf.sub((slice(None), slice(5632, 6144)), "vb1")
            SB["vn"] = KTbuf.sub((slice(None), slice(6144, 7168)), "vn")
            SB["abs"] = Vbuf.sub((slice(None), slice(0, 2 * NPG * 8)), "abs")
            SB["idx"] = Vbuf.sub((slice(None), slice(2 * NPG * 8, 2 * NPG * 8 + 8 * NPG)), "idx")
            p.dma("sp", SB["abs"][:, :].bitcast(F32), c_abs[:, :])
            ptb = s512b[:, 0:4 * NPG].bitcast(I32)
            p.dma("sp", ptb, ptab[0:1, :].partition_broadcast(128))
            cp(s512c[:, 0:4 * NPG], ptb)
            ts(SB["idx"][:, :].bitcast(I32), s512c[:, 0:4 * NPG], 128.0, ALU.mult, iota[:, 0:1], ALU.add)
            memset(X[:, 0:2, :], 0.0)
            for r in range(4):
                p.dma("sp", X[64 * (r % 2):64 * (r % 2) + 1, r // 2, :], xs[r:r + 1, :])
            ffn(0, 0, 2)
            if KSTOP == 6: return
            even_mixer(2, 0, True)
            if KSTOP == 7: return
            ffn(1, 2, 2)
            ffn(2, 3, 2)
            odd_mixer(2, True)
            ffn(3, 5, 2)
            for r in range(4):
                p.dma("sp", ys[r:r + 1, :], X[64 * (r % 2):64 * (r % 2) + 1, r // 2, :])

        main_prog()
        p.finish()
    return nc


def make_consts(TSEQ, NPG):
    NTT = TSEQ // 128; PAST = NPG * 128
    ar = np.arange(128)
    c = {}
    c["c_identf"] = np.eye(128, dtype=np.float32)
    c["c_tri"] = (ar[:, None] <= ar[None, :]).astype(np.float32)
    c["c_trirev"] = (ar[:, None] > ar[None, :]).astype(np.float32)
    c["c_negts"] = np.where(ar[None, :] > ar[:, None], -1e30, 0.0).astype(np.float32)
    pos = np.where(ar[:, None] > ar[None, :], 1e30, 0.0).astype(np.float32)
    c["c_pos4"] = pos
    c["c_negbf"] = np.where(ar[:, None] > ar[None, :], -30000.0, 0.0).astype(np.float32)
    slopes = np.array([2.0 ** (-8.0 * (h + 1) / 4) for h in range(4)], np.float32)
    abp = np.zeros((128, 4, NTT), np.float32)
    for h in range(4):
        for dd in range(NTT):
            abp[:, h, dd] = slopes[h] * (-128.0 * dd + ar - 127.0)
    c["c_abp"] = abp.reshape(128, 4 * NTT)
    ab = np.zeros((128, NPG, 8), np.float32)
    for j in range(NPG):
        for h in range(4):
            ab[:, j, 2 * h] = ab[:, j, 2 * h + 1] = slopes[h] * (128.0 * j + ar - PAST)
    c["c_abs"] = ab.reshape(128, NPG * 8)
    c["c_iota"] = ar.astype(np.float32).reshape(128, 1)
    bmk = np.zeros((8, 8), np.float32)
    for h in range(4):
        bmk[2 * h, h] = 1.0
        bmk[2 * h + 1, 4 + h] = 1.0
    c["c_bm"] = bmk
    return c


def run(inputs, TSEQ, NPG, NPOOL, n_cores=8):
    f = lambda a: np.ascontiguousarray(np.asarray(a))
    nc = build(TSEQ, NPG, NPOOL)
    consts = make_consts(TSEQ, NPG)
    B = inputs["x_prompt"].shape[0]
    shared = {
        "ck": f(inputs["cache_k"]).reshape(NPOOL * 128, 512), "cv": f(inputs["cache_v"]).reshape(NPOOL * 128, 512),
        "normg": f(inputs["norm_g"]).reshape(6, D),
        "wg": f(inputs["ffn_w_gate"]).reshape(4, D, DFF), "wu": f(inputs["ffn_w_up"]).reshape(4, D, DFF),
        "wd": f(inputs["ffn_w_down"]).reshape(4, DFF, D),
        "ewin": f(inputs["even_w_in"])[0], "ewout": f(inputs["even_w_out"])[0],
        "aqk": f(inputs["a_qk_norm"]).reshape(1, 128), "alam": f(inputs["a_lambda"]).reshape(1, 256),
        "ahn": f(inputs["a_head_norm"]).reshape(1, 512), "bw2": f(inputs["b_gate_w2"])[0],
        "bgb": f(inputs["b_gate_bias"]).reshape(1, 256), "bhn": f(inputs["b_head_norm"]).reshape(1, 512),
        "owin": f(inputs["odd_w_in"])[0], "owout": f(inputs["odd_w_out"])[0],
        "cgb": f(inputs["c_gate_bias"]).reshape(1, 8), "chn": f(inputs["c_head_norm"]).reshape(1, 1024),
    }
    shared.update(consts)
    shared["c_negbf"] = consts["c_negbf"]
    in_maps = []
    for c in range(n_cores):
        b = c % B; s0 = 4 * c
        m = dict(shared)
        m["xp"] = f(inputs["x_prompt"][b]); m["xs"] = f(inputs["x_sample"][s0:s0 + 4, 0])
        m["sgla"] = f(inputs["state_gla"][0, s0:s0 + 4]); m["sC"] = f(inputs["state_mlstm_C"][0, s0:s0 + 4])
        m["sn"] = f(inputs["state_mlstm_n"][0, s0:s0 + 4]); m["sm"] = f(inputs["state_mlstm_m"][0, s0:s0 + 4])
        m["ptab"] = f(inputs["page_table"][s0:s0 + 4]).reshape(1, 4 * NPG).astype(np.int32)
        in_maps.append(m)
    res = run_bass_kernel_spmd(nc, in_maps, core_ids=list(range(n_cores)))
    R = res.results
    cat = lambda k, cores: np.stack([R[c][k] for c in cores])
    pc = list(range(min(B, n_cores))); ac = list(range(n_cores)); B = len(pc)
    y_prompt = cat("yp", pc)
    y_sample = np.concatenate([R[c]["ys"] for c in ac])[:, None, :]
    k_prompt = cat("kp", pc).reshape(1, B, TSEQ, 4, 2, 64)
    v_prompt = cat("vp", pc).reshape(1, B, TSEQ, 4, 128)
    k_sample = np.concatenate([R[c]["ks"] for c in ac]).reshape(1, 4 * n_cores, 1, 4, 2, 64)
    v_sample = np.concatenate([R[c]["vs"] for c in ac]).reshape(1, 4 * n_cores, 1, 4, 128)
    gla_prompt = cat("glap", pc)[None]
    gla_sample = np.concatenate([R[c]["glas"] for c in ac])[None]
    C_prompt = cat("Cp", pc)[None]; n_prompt = cat("np", pc)[None]; m_prompt = cat("mp", pc).reshape(1, B, 4)
    C_sample = np.concatenate([R[c]["Cs"] for c in ac])[None]
    n_sample = np.concatenate([R[c]["ns"] for c in ac])[None]
    m_sample = np.concatenate([R[c]["ms"] for c in ac])[None]
    outs = (y_prompt, y_sample, k_prompt, v_prompt, k_sample, v_sample, gla_prompt, gla_sample,
            C_prompt, n_prompt, m_prompt, C_sample, n_sample, m_sample)
    return tuple(np.ascontiguousarray(o, dtype=np.float32) for o in outs)


def kernel(**inputs):
    TSEQ = inputs["x_prompt"].shape[1]
    NPG = inputs["page_table"].shape[1]
    NPOOL = inputs["cache_k"].shape[1]
    return run(inputs, TSEQ, NPG, NPOOL)
```

```python
import contextlib, math, os
KSTOP = int(os.environ.get('KSTOP', '0'))
KSUB = int(os.environ.get('KSUB', '0'))
import numpy as np
import ml_dtypes
import concourse.bass as bass
import concourse.mybir as mybir
from concourse.bass_utils import run_bass_kernel_spmd

F32 = mybir.dt.float32; BF16 = mybir.dt.bfloat16; I32 = mybir.dt.int32
AF = mybir.ActivationFunctionType; ALU = mybir.AluOpType; AX = mybir.AxisListType
EPS = 1e-6
D = 1024; DFF = 2816; PE_ = 3088; PO_ = 4104


class V:
    def __init__(s, t, ap): s.t = t; s.ap = ap
    def __getitem__(s, k): return V(s.t, s.ap[k])
    def re(s, pat, **kw): return V(s.t, s.ap.rearrange(pat, **kw))
    def bitcast(s, dt): return V(s.t, s.ap.bitcast(dt))
    def bc(s, shape): return V(s.t, s.ap.to_broadcast(list(shape)))
    def un(s, ax): return V(s.t, s.ap.unsqueeze(ax))


class T:
    def __init__(s, h, name): s.h = h; s.name = name; s.w = None; s.r = []; s.psum = False
    def __getitem__(s, k): return V(s, s.h[k])
    def sub(s, k, name):
        n = T(s.h[k], name); n.w = s.w; n.r = list(s.r)
        return n


class P:
    def __init__(self, nc, es, n_dma_sems=32):
        self.nc = nc; self.es = es
        self.eng = {"pe": nc.tensor, "act": nc.scalar, "dve": nc.vector, "pool": nc.gpsimd, "sp": nc.sync}
        self.sem = {e: es.enter_context(nc.semaphore("s_" + e)) for e in self.eng}
        self.cnt = {e: 0 for e in self.eng}
        self.seen = {e: {} for e in self.eng}
        self.dsems = [es.enter_context(nc.semaphore("d%d" % i)) for i in range(n_dma_sems)]
        self.dval = [0] * n_dma_sems
        self.dnext = {"sp": 0, "pool": 0}
        self.drange = {"sp": (0, n_dma_sems // 2), "pool": (n_dma_sems // 2, n_dma_sems)}
    def sb(self, name, shape, dt=F32):
        return T(self.es.enter_context(self.nc.sbuf_tensor(name, list(shape), dt)), name)
    def psum(self, name, shape, dt=F32):
        t = T(self.es.enter_context(self.nc.psum_tensor(name, list(shape), dt)), name)
        t.psum = True
        return t
    def _wait(self, e, d):
        key, sem, val, deng = d
        if deng == e and e == "pe":
            return
        if self.seen[e].get(key, 0) >= val:
            return
        self.eng[e].wait_ge(sem, val)
        self.seen[e][key] = val
    def _deps(self, e, reads, writes):
        deps = []
        for t in reads:
            if t.w is not None: deps.append(t.w)
            if t.psum:
                deps.extend(d for d in t.r if d[3] != e)
        for t in writes:
            if t.w is not None: deps.append(t.w)
            deps.extend(t.r)
        for d in deps:
            self._wait(e, d)
    def _mark(self, me, reads, writes):
        for t in reads:
            t.r.append(me)
            if len(t.r) > 64:
                t.r = self._prune(t.r)
        for t in writes:
            t.w = me; t.r = []
    @staticmethod
    def _prune(r):
        best = {}
        for d in r:
            if d[0] not in best or best[d[0]][2] < d[2]:
                best[d[0]] = d
        return list(best.values())
    def I(self, e, name, outs, ins, **kw):
        args = {}
        reads = []; writes = []
        for k, v in outs.items():
            args[k] = v.ap; writes.append(v.t)
        for k, v in ins.items():
            if isinstance(v, V):
                args[k] = v.ap; reads.append(v.t)
            else:
                args[k] = v
        args.update(kw)
        self._deps(e, reads, writes)
        ins_ = getattr(self.eng[e], name)(**args)
        self.cnt[e] += 1
        ins_.then_inc(self.sem[e], 1)
        self._mark((e, self.sem[e], self.cnt[e], e), reads, writes)
        return ins_
    def dma(self, e, out, in_, indirect=None):
        reads = [in_.t] if isinstance(in_, V) else []
        writes = [out.t] if isinstance(out, V) else []
        if indirect is not None:
            reads.append(indirect.t)
        self._deps(e, reads, writes)
        lo, hi = self.drange[e]
        i = lo + self.dnext[e]; self.dnext[e] = (self.dnext[e] + 1) % (hi - lo)
        sem = self.dsems[i]
        if self.dval[i] > 0:
            self._wait(e, ("d%d" % i, sem, self.dval[i], "dma"))
        self.dval[i] += 16
        o = out.ap if isinstance(out, V) else out
        s = in_.ap if isinstance(in_, V) else in_
        if indirect is not None:
            ins_ = self.eng[e].indirect_dma_start(out=o, out_offset=None, in_=s,
                                                  in_offset=bass.IndirectOffsetOnAxis(ap=indirect.ap, axis=0))
        else:
            ins_ = self.eng[e].dma_start(out=o, in_=s)
        ins_.then_inc(sem, 16)
        self._mark(("d%d" % i, sem, self.dval[i], "dma"), reads, writes)
        return ins_
    def finish(self):
        for i, sem in enumerate(self.dsems):
            if self.dval[i] > 0:
                self._wait("sp", ("d%d" % i, sem, self.dval[i], "dma"))
        for e in self.eng:
            if e != "sp" and self.cnt[e] > 0:
                self._wait("sp", (e, self.sem[e], self.cnt[e], e))


def build(TSEQ, NPG, NPOOL):
    NTT = TSEQ // 128
    STT = min(4, NTT)
    NST = NTT // STT
    NMAX = STT * 128
    PAST = NPG * 128
    nc = bass.Bass("TRN2", target_bir_lowering=False)

    def din(name, shape, dt=F32):
        return nc.dram_tensor(name, list(shape), dt, kind="ExternalInput").ap()
    def dout(name, shape):
        return nc.dram_tensor(name, list(shape), F32, kind="ExternalOutput").ap()

    xp = din("xp", [TSEQ, D]); xs = din("xs", [4, D])
    ck = din("ck", [NPOOL * 128, 512]); cv = din("cv", [NPOOL * 128, 512])
    sgla = din("sgla", [4, 4, 64, 128]); sC = din("sC", [4, 4, 256, 256]); sn = din("sn", [4, 4, 256]); sm = din("sm", [4, 4])
    ptab = din("ptab", [1, 4 * NPG], I32)
    normg = din("normg", [6, D])
    Wg = din("wg", [4, D, DFF]); Wu = din("wu", [4, D, DFF]); Wd = din("wd", [4, DFF, D])
    EWin = din("ewin", [D, PE_]); EWout = din("ewout", [D, D])
    aqk = din("aqk", [1, 128]); alam = din("alam", [1, 256]); ahn = din("ahn", [1, 512])
    bw2 = din("bw2", [16, 256]); bgb = din("bgb", [1, 256]); bhn = din("bhn", [1, 512])
    OWin = din("owin", [D, PO_]); OWout = din("owout", [D, D])
    cgb = din("cgb", [1, 8]); chn = din("chn", [1, 1024])
    c_identf = din("c_identf", [128, 128]); c_tri = din("c_tri", [128, 128]); c_trirev = din("c_trirev", [128, 128])
    c_negts = din("c_negts", [128, 128]); c_pos4 = din("c_pos4", [128, 128]); c_negbf = din("c_negbf", [128, 128])
    c_abp = din("c_abp", [128, 4 * NTT]); c_abs = din("c_abs", [128, NPG * 8]); c_iota = din("c_iota", [128, 1])
    c_bm = din("c_bm", [8, 8])

    yp = dout("yp", [TSEQ, D]); ys = dout("ys", [4, D])
    kp = dout("kp", [TSEQ, 512]); vp = dout("vp", [TSEQ, 512]); ks = dout("ks", [4, 512]); vs = dout("vs", [4, 512])
    glap = dout("glap", [4, 64, 128]); glas = dout("glas", [4, 4, 64, 128])
    Cp = dout("Cp", [4, 256, 256]); np_ = dout("np", [4, 256]); mp = dout("mp", [1, 4])
    Cs = dout("Cs", [4, 4, 256, 256]); ns = dout("ns", [4, 4, 256]); ms = dout("ms", [4, 4])

    es = contextlib.ExitStack()
    with es:
        p = P(nc, es)
        X = p.sb("X", [128, STT, D])
        tokb = p.sb("tokb", [128, D], BF16)
        junk = tokb
        actT = p.sb("actT", [128, 8, NMAX], BF16)
        NW = 6
        Wb = [p.sb("Wb%d" % i, [128, 8, 256], BF16) for i in range(NW)]
        def dscr(name, shape):
            return T(nc.dram_tensor(name, list(shape), BF16, kind="Internal").ap(), name)
        WgB = [dscr("wgb%d" % f, [D, DFF]) for f in range(4)]
        WuB = [dscr("wub%d" % f, [D, DFF]) for f in range(4)]
        WdB = [dscr("wdb%d" % f, [DFF, D]) for f in range(4)]
        EWinB = dscr("ewinb", [D, PE_]); EWoutB = dscr("ewoutb", [D, D])
        OWinB = dscr("owinb", [D, PO_]); OWoutB = dscr("owoutb", [D, D])
        KTbuf = p.sb("KT", [128, max(4 * TSEQ, 7168)], BF16)
        KT = KTbuf[:, 0:4 * TSEQ].re("p (h n) -> p h n", h=4)
        VSZ = max(NTT * 520, 2 * NPG * 8 + 2 * 4 * NPG + 16)
        Vbuf = p.sb("Vst", [128, VSZ], BF16)
        Vst = Vbuf[:, 0:NTT * 520].re("p (a b c) -> p a b c", a=NTT, b=4)
        gbc = p.sb("gbc", [128, D])
        mixA = p.sb("mixA", [128, 8192], BF16)
        mixB = p.sb("mixB", [128, 8256], BF16)
        mixC = p.sb("mixC", [128, 4096], BF16)
        Sg = p.sb("Sg", [128, 2, 128]); Sgb = p.sb("Sgb", [128, 2, 128], BF16)
        CT = p.sb("CT", [128, 4, 2, 258]); CTb = p.sb("CTb", [128, 4, 2, 258], BF16)
        mprev = p.sb("mprev", [128, 4])
        identf = p.sb("identf", [128, 128]); identb = p.sb("identb", [128, 128], BF16)
        tri = p.sb("tri", [128, 128]); trirev = p.sb("trirev", [128, 128]); negts = p.sb("negts", [128, 128])
        pos4 = p.sb("pos4", [128, 128]); negbf = p.sb("negbf", [128, 128], BF16)
        onesf = p.sb("onesf", [128, 128]); onesb = p.sb("onesb", [128, 128], BF16)
        abp = p.sb("abp", [128, 4 * NTT]); iota = p.sb("iota", [128, 1])
        bm = p.sb("bm", [8, 8])
        aqk_bc = p.sb("aqk_bc", [128, 128]); ahn_bc = p.sb("ahn_bc", [128, 512]); bhn_bc = p.sb("bhn_bc", [128, 512])
        chn_bc = p.sb("chn_bc", [128, 1024]); cgb_bc = p.sb("cgb_bc", [128, 8]); alam_bc = p.sb("alam_bc", [128, 256])
        W2aug = p.sb("W2aug", [32, 256])
        neglam = p.sb("neglam", [128, 1])
        sm1 = p.sb("sm1", [128, 64]); sm2 = p.sb("sm2", [128, 64]); sm3 = p.sb("sm3", [128, 64])
        s512a = p.sb("s512a", [128, 512]); s512b = p.sb("s512b", [128, 512]); s512c = p.sb("s512c", [128, 512])
        b512a = p.sb("b512a", [128, 512], BF16); b512b = p.sb("b512b", [128, 512], BF16); b512c = p.sb("b512c", [128, 512], BF16)
        sg = [s512a, s512b]
        bgT = p.sb("bgT", [32, NMAX])
        gates = p.sb("gates", [128, STT, 8])
        aout = p.sb("aout", [128, 1024])
        PS = [p.psum("ps%d" % i, [128, 512]) for i in range(8)]
        st = {"rot": 0, "w": 0, "sg": 0}
        def rot():
            b = PS[st["rot"]]; st["rot"] = (st["rot"] + 1) % 4
            return b
        def nextW():
            w = Wb[st["w"]]; st["w"] = (st["w"] + 1) % NW
            return w

        def mm(out, lhsT, rhs, start, stop):
            p.I("pe", "matmul", {"out": out}, {"lhsT": lhsT, "rhs": rhs}, start=start, stop=stop)
        def tr(out, in_, ident):
            p.I("pe", "transpose", {"out": out}, {"in_": in_, "identity": ident})
        def act(out, in_, func, bias=None, scale=None, accum=None):
            outs = {"out": out}; ins = {"in_": in_}
            kw = {"func": func}
            if accum is not None: outs["accum_out"] = accum
            if bias is not None: ins["bias"] = bias
            if scale is not None: ins["scale"] = scale
            p.I("act", "activation", outs, ins, **kw)
        def tt_(out, in0, in1, op, e="dve"):
            p.I(e, "tensor_tensor", {"out": out}, {"in0": in0, "in1": in1}, op=op)
        def ts(out, in0, s1, op0, s2=None, op1=None, e="dve"):
            kw = {"op0": op0}
            if op1 is not None: kw["op1"] = op1
            p.I(e, "tensor_scalar", {"out": out}, {"in0": in0, "scalar1": s1, "scalar2": s2}, **kw)
        def stt(out, in0, scalar, in1, op0, op1):
            p.I("dve", "scalar_tensor_tensor", {"out": out}, {"in0": in0, "scalar": scalar, "in1": in1}, op0=op0, op1=op1)
        def red(out, in_, op=ALU.add):
            p.I("dve", "tensor_reduce", {"out": out}, {"in_": in_}, axis=AX.X, op=op)
        def recip(out, in_):
            p.I("dve", "reciprocal", {"out": out}, {"in_": in_})
        def cp(out, in_, e="dve"):
            if e == "act":
                act(out, in_, AF.Copy)
            else:
                p.I(e, "tensor_copy", {"out": out}, {"in_": in_})
        def memset(v, val, e="dve"):
            p.I(e, "memset", {"ap": v}, {}, constant=val)
        def loadW(dst, WT, r0, nk, c0, ncol):
            src = WT[r0:r0 + nk * 128, c0:c0 + ncol].re("(k p) f -> p k f", p=128)
            p.dma("sp", dst, src)
        def convW(WT, src2d, nrows):
            for r0 in range(0, nrows, 128):
                p.dma("pool", WT[r0:r0 + 128, :], src2d[r0:r0 + 128, :])
        def rstd_of(out, ss, n, sl):
            act(out, ss, AF.Sqrt, bias=eps_t[sl, 0:1], scale=1.0 / n)
            recip(out, out)

        eps_t = p.sb("eps_t", [128, 1]); one_t = p.sb("one_t", [128, 1])
        memset(eps_t[:, :], EPS); memset(one_t[:, :], 1.0)

        for t_, d_ in ((identf, c_identf), (tri, c_tri), (trirev, c_trirev), (negts, c_negts), (pos4, c_pos4),
                       (abp, c_abp), (iota, c_iota), (bm, c_bm)):
            p.dma("sp", t_[:], d_[:, :] if len(d_.shape) == 2 else d_)
        p.dma("pool", negbf[:], c_negbf[:, :])
        cp(identb[:, :], identf[:, :])
        memset(onesf[:, :], 1.0); memset(onesb[:, :], 1.0)
        for t_, d_, n_ in ((aqk_bc, aqk, 128), (ahn_bc, ahn, 512), (bhn_bc, bhn, 512), (chn_bc, chn, 1024),
                           (cgb_bc, cgb, 8), (alam_bc, alam, 256)):
            p.dma("sp", t_[:], d_[0:1, :].partition_broadcast(128))
        memset(W2aug[:, :], 0.0)
        p.dma("sp", W2aug[0:16, :], bw2[:, :])
        p.dma("sp", W2aug[16:17, :], bgb[0:1, :])
        memset(bgT[:, :], 1.0)
        memset(Vst.re("p a b c -> p (a b) c")[:, :, 128:130], 1.0)
        tt_(s512a[:, 0:64], alam_bc[:, 0:64], alam_bc[:, 64:128], ALU.mult)
        tt_(s512a[:, 64:128], alam_bc[:, 128:192], alam_bc[:, 192:256], ALU.mult)
        red(sm1[:, 0:2], s512a[:, 0:128].re("p (a b) -> p a b", a=2))
        act(sm1[:, 2:4], sm1[:, 0:2], AF.Exp)
        tt_(sm1[:, 4:5], sm1[:, 3:4], sm1[:, 2:3], ALU.subtract)
        ts(neglam[:, :], sm1[:, 4:5], -0.2, ALU.add)

        hT = mixA
        def hT_view(fb, N):
            per = 8192 // NMAX
            reg = (mixA, mixB, mixC)[fb // per]
            o = (fb % per) * NMAX
            return reg[:, o:o + N]

        def norm_T(nt, gi):
            p.dma("sp", gbc[:], normg[gi:gi + 1, :].partition_broadcast(128))
            for t in range(nt):
                act(junk[:, :], X[:, t, :], AF.Square, accum=sm1[:, 8:9])
                rstd_of(sm1[:, 9:10], sm1[:, 8:9], D, slice(0, 128))
                stt(tokb[:, :], X[:, t, :], sm1[:, 9:10], gbc[:, :], ALU.mult, ALU.mult)
                tok_to_actT(t)
        def tok_to_actT(t):
            ps = rot(); psb = ps[:, :].bitcast(BF16)
            for kc in range(8):
                tr(psb[:, kc * 128:(kc + 1) * 128], tokb[:, kc * 128:(kc + 1) * 128], identb[:, :])
            cp(actT[:, :, t * 128:(t + 1) * 128], psb[:, 0:1024].re("p (k n) -> p k n", k=8), e="act")

        def ffn(f, gi, nt):
            N = nt * 128
            norm_T(nt, gi)
            for sbk in range(11):
                c0 = sbk * 256; ncol = 256
                wg_ = nextW(); loadW(wg_[:, :, 0:ncol], WgB[f], 0, 8, c0, ncol)
                wu_ = nextW(); loadW(wu_[:, :, 0:ncol], WuB[f], 0, 8, c0, ncol)
                for q in range(ncol // 128):
                    fb = sbk * 2 + q
                    pg = rot(); pu = rot()
                    for kc in range(8):
                        mm(pg[:, 0:N], wg_[:, kc, q * 128:(q + 1) * 128], actT[:, kc, 0:N], kc == 0, kc == 7)
                    for kc in range(8):
                        mm(pu[:, 0:N], wu_[:, kc, q * 128:(q + 1) * 128], actT[:, kc, 0:N], kc == 0, kc == 7)
                    s_ = sg[st["sg"]]; st["sg"] ^= 1
                    act(s_[:, 0:N], pg[:, 0:N], AF.Silu)
                    tt_(hT_view(fb, N), s_[:, 0:N], pu[:, 0:N], ALU.mult)
            for db in range(2):
                for g in range(6):
                    nf = min(4, 22 - g * 4)
                    w = nextW()
                    wv = w[:].re("p k f -> p (k f)")[:, 0:nf * 512].re("p (k f) -> p k f", k=nf)
                    loadW(wv, WdB[f], g * 512, nf, db * 512, 512)
                    for t in range(nt):
                        for q in range(nf):
                            fc = g * 4 + q
                            mm(PS[4 + t][:, :], hT_view(fc, NMAX)[:, t * 128:(t + 1) * 128], wv[:, q, :], fc == 0, fc == 21)
                for t in range(nt):
                    stt(X[:, t, db * 512:(db + 1) * 512], PS[4 + t][:, :], 0.5, X[:, t, db * 512:(db + 1) * 512], ALU.mult, ALU.add)

        def out_proj(W2d, nt):
            for db in range(2):
                ws = []
                for hf in range(2):
                    w = nextW(); loadW(w[:, :, :], W2d, 0, 8, db * 512 + hf * 256, 256); ws.append(w)
                for t in range(nt):
                    ps = rot()
                    for hf in range(2):
                        for kc in range(8):
                            mm(ps[:, hf * 256:(hf + 1) * 256], actT[:, kc, t * 128:(t + 1) * 128], ws[hf][:, kc, :], kc == 0, kc == 7)
                    tt_(X[:, t, db * 512:(db + 1) * 512], X[:, t, db * 512:(db + 1) * 512], ps[:, :], ALU.add)

        def proj_blocks(W2d, nt, c0, ncol, consumer):
            ws = []
            for h0 in range(0, ncol, 256):
                hw = min(256, ncol - h0)
                w = nextW(); loadW(w[:, :, 0:hw], W2d, 0, 8, c0 + h0, hw); ws.append((w, h0, hw))
            for t in range(nt):
                ps = rot()
                for (w, h0, hw) in ws:
                    for kc in range(8):
                        mm(ps[:, h0:h0 + hw], actT[:, kc, t * 128:(t + 1) * 128], w[:, kc, 0:hw], kc == 0, kc == 7)
                consumer(t, ps)

        def headnorm(src, nh, dh, gain_bc, sl, extra_scale=None):
            n = nh * dh
            sq = s512a[sl, 0:n] if n <= 512 else gbc[sl, 0:n]
            act(sq, src, AF.Square)
            red(sm2[sl, 0:nh], sq.re("p (a b) -> p a b", a=nh))
            rstd_of(sm2[sl, 8:8 + nh], sm2[sl, 0:nh], dh, sl)
            if extra_scale is not None:
                ts(sm2[sl, 8:8 + nh], sm2[sl, 8:8 + nh], extra_scale, ALU.mult)
            tt_(src.re("p (a b) -> p a b", a=nh), src.re("p (a b) -> p a b", a=nh),
                sm2[sl, 8:8 + nh].un(2).bc([sl.stop - sl.start, nh, dh]), ALU.mult)
            tt_(src, src, gain_bc[sl, 0:n], ALU.mult)

        def QT(h, c0, n): return mixA[:, h * NMAX + c0: h * NMAX + c0 + n]
        def vB(t): return mixA[:, 4096 + t * 512: 4096 + (t + 1) * 512]
        def qkB(t): return mixB[:, t * 1024:(t + 1) * 1024].bitcast(F32)
        def srB(t): return mixB[:, 4096 + t * 1024: 4096 + (t + 1) * 1024].bitcast(F32)
        def qnS(t): return mixC[:, t * 1024:(t + 1) * 1024].bitcast(F32)
        def knS(t): return mixC[:, 2048 + t * 1024: 2048 + (t + 1) * 1024].bitcast(F32)
        def vnS(t): return SB["vn"][:, t * 512:(t + 1) * 512]
        SB = {}

        def qknorm(ps, which, dst):
            act(s512a[:, :], ps[:, :], AF.Square)
            red(sm2[:, 0:8], s512a[:, :].re("p (a b) -> p a b", a=8))
            rstd_of(sm2[:, 8:16], sm2[:, 0:8], 64, slice(0, 128))
            tt_(dst.re("p (a b) -> p a b", a=8), ps[:, :].re("p (a b) -> p a b", a=8), sm2[:, 8:16].un(2).bc([128, 8, 64]), ALU.mult)
            tt_(dst.re("p (a b) -> p a b", a=8), dst.re("p (a b) -> p a b", a=8),
                aqk_bc[:, which * 64:(which + 1) * 64].un(1).bc([128, 8, 64]), ALU.mult)

        def even_mixer(nt, g0, sample):
            N = nt * 128
            norm_T(nt, 1)
            def c_q(t, ps):
                dst = qnS(t) if sample else s512b[:, :]
                qknorm(ps, 0, dst)
                if not sample:
                    cp(b512a[:, :], dst, e="act")
                    pt_ = rot(); ptb_ = pt_[:, :].bitcast(BF16)
                    for h in range(4):
                        tr(ptb_[:, h * 128:(h + 1) * 128], b512a[:, h * 128:(h + 1) * 128], identb[:, :])
                    for h in range(4):
                        cp(QT(h, t * 128, 128), ptb_[:, h * 128:(h + 1) * 128])
            def c_k(t, ps):
                dst = knS(t) if sample else s512c[:, :]
                qknorm(ps, 1, dst)
                if sample:
                    for r2 in range(2):
                        p.dma("sp", ks[2 * t + r2:2 * t + r2 + 1, :], dst[64 * r2:64 * r2 + 1, :])
                else:
                    g = g0 + t
                    p.dma("sp", kp[g * 128:(g + 1) * 128, :], dst)
                    cp(b512b[:, :], dst, e="act")
                    pt_ = rot(); ptb_ = pt_[:, :].bitcast(BF16)
                    for h in range(4):
                        tr(ptb_[:, h * 128:(h + 1) * 128], b512b[:, h * 128:(h + 1) * 128], identb[:, :])
                    cp(KT[:, :, g * 128:(g + 1) * 128], ptb_[:, 0:512].re("p (h n) -> p h n", h=4))
            def c_v(t, ps):
                cp(s512b[:, :], ps[:, :], e="act")
                if sample:
                    for r2 in range(2):
                        p.dma("sp", vs[2 * t + r2:2 * t + r2 + 1, :], s512b[64 * r2:64 * r2 + 1, :])
                    cp(vnS(t), ps[:, :])
                else:
                    g = g0 + t
                    p.dma("sp", vp[g * 128:(g + 1) * 128, :], s512b[:, :])
                    cp(Vst[:, g, :, 0:128], ps[:, :].re("p (h v) -> p h v", h=4))
            def c_qkB(t, ps): cp(qkB(t), ps[:, :], e="act")
            def c_vB(t, ps): cp(vB(t), ps[:, :])
            def c_r(t, ps): act(srB(t), ps[:, :], AF.Silu)
            proj_blocks(EWinB, nt, 0, 512, c_q)
            proj_blocks(EWinB, nt, 512, 512, c_k)
            proj_blocks(EWinB, nt, 1024, 512, c_v)
            proj_blocks(EWinB, nt, 1536, 512, c_qkB)
            proj_blocks(EWinB, nt, 2048, 512, c_vB)
            proj_blocks(EWinB, nt, 2576, 512, c_r)
            if KSUB == 1: return
            w = nextW(); loadW(w[:, :, 0:16], EWinB, 0, 8, 2560, 16)
            ps = rot()
            for kc in range(8):
                mm(ps[0:16, 0:N], w[:, kc, 0:16], actT[:, kc, 0:N], kc == 0, kc == 7)
            cp(bgT[0:16, 0:N], ps[0:16, 0:N])
            if KSUB == 2: return

            for t in range(nt):
                if sample:
                    for r2 in range(2):
                        r = 2 * t + r2; sl = slice(64 * r2, 64 * r2 + 1)
                        sample_attn(r, t, sl)
                        p.dma("sp", Sg[:], sgla[r].rearrange("(j hh) d v -> (hh d) j v", hh=2))
                        cp(Sgb[:, :, :], Sg[:, :, :], e="act")
                        gla_chunk(t, sl)
                        p.dma("sp", glas[r].rearrange("(j hh) d v -> (hh d) j v", hh=2), Sg[:])
                    sl_all = slice(0, 128)
                else:
                    attn_tile(t, g0 + t)
                    if KSUB == 3: return
                    if KSUB == 8 and t == 1: return
                    if not os.environ.get("KSKIPGLA"): gla_chunk(t, slice(0, 128))
                    if KSUB == 4: return
                headnorm(aout[:, 0:512], 4, 128, ahn_bc, slice(0, 128), extra_scale=0.8)
                headnorm(aout[:, 512:1024], 4, 128, bhn_bc, slice(0, 128))
                tt_(aout[:, 512:1024], aout[:, 512:1024], srB(t), ALU.mult)
                if KSUB == 5: return
                cp(tokb[:, :], aout[:, :], e="act")
                tok_to_actT(t)
                if KSUB == 6: return
            if KSUB == 7: return
            out_proj(EWoutB, nt)

        def attn_tile(t, i):
            for h in range(4):
                a0 = PS[4 + 2 * (h % 2)]; a1 = PS[5 + 2 * (h % 2)]
                for kb in range(i + 1):
                    pT = (b512a, b512b, b512c)[kb % 3]
                    for m in range(2):
                        s_ = rot()
                        mm(s_[:, 0:128], KT[64 * m:64 * m + 64, h, kb * 128:(kb + 1) * 128],
                           QT(h, t * 128, 128)[64 * m:64 * m + 64, :], True, kb != i)
                        if kb == i:
                            mm(s_[:, 0:128], identb[:, :], negbf[:, :], False, True)
                        act(pT[:, m * 128:(m + 1) * 128], s_[:, 0:128], AF.Exp,
                            bias=abp[:, h * NTT + (i - kb): h * NTT + (i - kb) + 1], scale=0.125)
                    mm(a0[:, 0:129], pT[:, 0:128], Vst[:, kb, h, 0:129], kb == 0, kb == i)
                    mm(a1[:, 0:129], pT[:, 128:256], Vst[:, kb, h, 0:129], kb == 0, kb == i)
                recip(sm3[:, 0:1], a0[:, 128:129]); recip(sm3[:, 1:2], a1[:, 128:129])
                tt_(sm3[:, 2:3], sm3[:, 1:2], neglam[:, :], ALU.mult)
                ts(s512a[:, 0:128], a0[:, 0:128], sm3[:, 0:1], ALU.mult)
                stt(aout[:, h * 128:(h + 1) * 128], a1[:, 0:128], sm3[:, 2:3], s512a[:, 0:128], ALU.mult, ALU.add)

        def sample_attn(r, t, sl):
            qn = qnS(t); kn = knS(t)
            ps = rot()
            mm(ps[:, :], onesf[sl, 0:128], qn[sl, :], True, True)
            qbc = SB["qbc"][:, :].bitcast(F32); abs_ = SB["abs"][:, :].bitcast(F32); idx = SB["idx"][:, :].bitcast(I32)
            cp(qbc, ps[:, :], e="act")
            tt_(s512a[sl, :], qn[sl, :], kn[sl, :], ALU.mult)
            red(sm3[sl, 8:16], s512a[sl, :].re("p (a b) -> p a b", a=8))
            act(b512c[sl, 0:8], sm3[sl, 8:16], AF.Exp, scale=0.125)
            acc = PS[4]; den = PS[5]
            for j in range(NPG):
                Kb = SB["K%d" % (j % 2)][:, :].bitcast(F32); Vb_ = SB["V%d" % (j % 2)][:, :].bitcast(F32)
                col = r * NPG + j
                p.dma("pool", Kb, ck[:, :], indirect=idx[:, col:col + 1])
                p.dma("pool", Vb_, cv[:, :], indirect=idx[:, col:col + 1])
                tt_(s512b[:, :], Kb, qbc, ALU.mult)
                red(sm3[:, 16:24], s512b[:, :].re("p (a b) -> p a b", a=8))
                stt(sm3[:, 24:32], sm3[:, 16:24], 0.125, abs_[:, j * 8:(j + 1) * 8], ALU.mult, ALU.add)
                pb = b512a if j % 2 == 0 else b512b
                act(pb[:, 0:8], sm3[:, 24:32], AF.Exp)
                vbf = SB["vb%d" % (j % 2)][:, :]
                cp(vbf, Vb_, e="act")
                mm(acc[0:8, :], pb[:, 0:8], vbf, j == 0, False)
                mm(den[0:8, 0:1], pb[:, 0:8], onesb[:, 0:1], j == 0, False)
            mm(acc[0:8, :], b512c[sl, 0:8], vnS(t)[sl, :], NPG == 0, True)
            mm(den[0:8, 0:1], b512c[sl, 0:8], onesb[sl, 0:1], NPG == 0, True)
            recip(sm3[0:8, 32:33], den[0:8, 0:1])
            ts(s512a[0:8, :], acc[0:8, :], sm3[0:8, 32:33], ALU.mult)
            v3 = lambda x: x.re("p (h v) -> p h v", h=4)
            tt_(v3(s512b[0:8, :]), v3(s512a[0:8, :]), bm[0:8, 0:4].un(2).bc([8, 4, 128]), ALU.mult)
            tt_(v3(s512c[0:8, :]), v3(s512a[0:8, :]), bm[0:8, 4:8].un(2).bc([8, 4, 128]), ALU.mult)
            stt(s512b[0:8, :], s512c[0:8, :], neglam[0:8, 0:1], s512b[0:8, :], ALU.mult, ALU.add)
            ps2 = rot()
            mm(ps2[sl, :], onesf[0:8, 0:1], s512b[0:8, :], True, True)
            cp(aout[sl, 0:512], ps2[sl, :])

        def gla_chunk(t, sl):
            n = sl.stop - sl.start; c0 = t * 128 + sl.start
            ps = rot()
            mm(ps[sl, 0:256], bgT[0:17, c0:c0 + n], W2aug[0:17, :], True, True)
            act(s512a[sl, 0:256], ps[sl, 0:256], AF.Exp, scale=-1.0)
            act(s512a[sl, 0:256], s512a[sl, 0:256], AF.Ln, bias=one_t[sl, 0:1])
            ts(s512a[sl, 0:256], s512a[sl, 0:256], -1.0 / 16.0, ALU.mult)
            ps = rot()
            mm(ps[sl, 0:256], tri[sl, sl], s512a[sl, 0:256], True, True)
            mm(ps[sl, 256:512], trirev[sl, sl], s512a[sl, 0:256], True, True)
            psl = rot()
            for j in range(2):
                mm(psl[:, j:j + 1], s512a[sl, j * 128:(j + 1) * 128], onesf[sl, 0:1], True, True)
            act(sm3[:, 40:42], psl[:, 0:2], AF.Exp)
            act(s512b[sl, 0:256], ps[sl, 0:256], AF.Exp)
            act(s512b[sl, 256:512], ps[sl, 0:256], AF.Exp, scale=-1.0)
            act(s512c[sl, 0:256], ps[sl, 256:512], AF.Exp)
            qk = qkB(t)
            stt(b512a[sl, 0:256], qk[sl, 0:256], 0.125, s512b[sl, 0:256], ALU.mult, ALU.mult)
            tt_(b512a[sl, 256:512], qk[sl, 256:512], s512b[sl, 256:512], ALU.mult)
            tt_(b512b[sl, 0:256], qk[sl, 256:512], s512c[sl, 0:256], ALU.mult)
            pt_ = rot(); ptb_ = pt_[:, :].bitcast(BF16)
            for q in range(4):
                tr(ptb_[:, q * 128:q * 128 + n], b512a[sl, q * 128:(q + 1) * 128], identb[sl, sl])
            for q in range(4):
                cp(b512c[:, q * 128:q * 128 + n], ptb_[:, q * 128:q * 128 + n], e="act")
            for h in range(4):
                j = h // 2; hp = slice(64 * (h % 2), 64 * (h % 2) + 64)
                pa = rot()
                mm(pa[sl, 0:n], b512c[hp, (2 + j) * 128:(2 + j) * 128 + n], b512c[hp, j * 128:j * 128 + n], True, True)
                tt_(b512b[sl, 256:256 + n], pa[sl, 0:n], tri[sl, sl], ALU.mult)
                po = rot(); po2 = rot()
                mm(po[sl, 0:128], b512b[sl, 256:256 + n], vB(t)[sl, h * 128:(h + 1) * 128], True, True)
                mm(po2[sl, 0:128], b512c[hp, j * 128:j * 128 + n], Sgb[hp, j, :], True, True)
                cp(aout[sl, 512 + h * 128:512 + (h + 1) * 128], po[sl, 0:128], e="act")
                tt_(aout[sl, 512 + h * 128:512 + (h + 1) * 128], aout[sl, 512 + h * 128:512 + (h + 1) * 128], po2[sl, 0:128], ALU.add)
            for h in range(4):
                j = h // 2; hp = slice(64 * (h % 2), 64 * (h % 2) + 64)
                pS = rot()
                mm(pS[:, 0:128], b512b[sl, j * 128:(j + 1) * 128], vB(t)[sl, h * 128:(h + 1) * 128], True, True)
                stt(Sg[hp, j, :], Sg[hp, j, :], sm3[hp, 40 + j:41 + j], pS[hp, 0:128], ALU.mult, ALU.add)
            cp(Sgb[:, :, :], Sg[:, :, :], e="act")

        def qT_o(i, c0, n): return mixA[:, i * NMAX + c0: i * NMAX + c0 + n]
        def kT_o(i, c0, n): return mixA[:, 4096 + i * NMAX + c0: 4096 + i * NMAX + c0 + n]
        def ktok(t): return mixB[:, t * 1024:(t + 1) * 1024]
        def vaug(t): return mixB[:, 4096 + t * 1032: 4096 + (t + 1) * 1032].re("p (h v) -> p h v", h=4)
        def so(t): return mixC[:, t * 1024:(t + 1) * 1024]

        def odd_mixer(nt, sample):
            N = nt * 128
            norm_T(nt, 4)
            for t in range(nt):
                memset(vaug(t)[:, :, 256:258], 1.0)
            def c_q(blk):
                def c(t, ps):
                    cp(b512a[:, :], ps[:, :], e="act")
                    pt_ = rot(); ptb_ = pt_[:, :].bitcast(BF16)
                    for q in range(4):
                        tr(ptb_[:, q * 128:(q + 1) * 128], b512a[:, q * 128:(q + 1) * 128], identb[:, :])
                    for q in range(4):
                        cp(qT_o(blk * 4 + q, t * 128, 128), ptb_[:, q * 128:(q + 1) * 128])
                return c
            def c_k(blk):
                def c(t, ps):
                    act(ktok(t)[:, blk * 512:(blk + 1) * 512], ps[:, :], AF.Copy, scale=1.0 / 16.0)
                    pt_ = rot(); ptb_ = pt_[:, :].bitcast(BF16)
                    for q in range(4):
                        tr(ptb_[:, q * 128:(q + 1) * 128], ktok(t)[:, blk * 512 + q * 128: blk * 512 + (q + 1) * 128], identb[:, :])
                    for q in range(4):
                        cp(kT_o(blk * 4 + q, t * 128, 128), ptb_[:, q * 128:(q + 1) * 128])
                return c
            def c_v(blk):
                def c(t, ps):
                    cp(vaug(t)[:, 2 * blk:2 * blk + 2, 0:256], ps[:, :].re("p (h v) -> p h v", h=2))
                return c
            def c_o(blk):
                def c(t, ps):
                    act(so(t)[:, blk * 512:(blk + 1) * 512], ps[:, :], AF.Sigmoid)
                return c
            def c_g(t, ps):
                tt_(gates[:, t, :], ps[:, 0:8], cgb_bc[:, :], ALU.add)
            for blk in range(2): proj_blocks(OWinB, nt, blk * 512, 512, c_q(blk))
            for blk in range(2): proj_blocks(OWinB, nt, 1024 + blk * 512, 512, c_k(blk))
            for blk in range(2): proj_blocks(OWinB, nt, 2048 + blk * 512, 512, c_v(blk))
            for blk in range(2): proj_blocks(OWinB, nt, 3072 + blk * 512, 512, c_o(blk))
            proj_blocks(OWinB, nt, 4096, 8, c_g)
            for t in range(nt):
                if sample:
                    for r2 in range(2):
                        r = 2 * t + r2; sl = slice(64 * r2, 64 * r2 + 1)
                        load_C(r)
                        mlstm_chunk(t, sl)
                        store_C(Cs[r], ns[r], ms[r:r + 1, :])
                else:
                    mlstm_chunk(t, slice(0, 128))
                headnorm(aout[:, 0:1024], 4, 256, chn_bc, slice(0, 128))
                tt_(aout[:, :], aout[:, :], so(t), ALU.mult)
                cp(tokb[:, :], aout[:, :], e="act")
                tok_to_actT(t)
            out_proj(OWoutB, nt)

        def load_C(r):
            for h in range(4):
                for vc in range(2):
                    p.dma("sp", s512a[:, 0:256], sC[r, h, vc * 128:(vc + 1) * 128, :])
                    ps = rot()
                    for dc in range(2):
                        tr(ps[:, dc * 128:(dc + 1) * 128], s512a[:, dc * 128:(dc + 1) * 128], identf[:, :])
                    for dc in range(2):
                        cp(CT[:, h, dc, vc * 128:(vc + 1) * 128], ps[:, dc * 128:(dc + 1) * 128])
                for dc in range(2):
                    p.dma("sp", CT[:, h, dc, 256:257], sn[r, h:h + 1, dc * 128:(dc + 1) * 128].rearrange("o d -> d o"))
            p.dma("sp", mprev[:], sm[r:r + 1, :].partition_broadcast(128))
            cp(CTb[:].re("p a b c -> p (a b c)"), CT[:].re("p a b c -> p (a b c)"), e="act")

        def store_C(Cd, nd, md):
            for h in range(4):
                for vc in range(2):
                    ps = rot()
                    for dc in range(2):
                        tr(ps[:, dc * 128:(dc + 1) * 128], CT[:, h, dc, vc * 128:(vc + 1) * 128], identf[:, :])
                    cp(s512b[:, 0:256], ps[:, 0:256])
                    p.dma("sp", Cd[h, vc * 128:(vc + 1) * 128, :], s512b[:, 0:256])
                for dc in range(2):
                    p.dma("sp", nd[h:h + 1, dc * 128:(dc + 1) * 128].rearrange("o d -> d o"), CT[:, h, dc, 256:257])
            p.dma("sp", md, mprev[0:1, :])

        def mlstm_chunk(t, sl):
            n = sl.stop - sl.start; c0 = t * 128 + sl.start
            g = gates
            ipre = g[sl, t, 0:4]
            act(sm1[sl, 16:20], g[sl, t, 4:8], AF.Exp, scale=-1.0)
            act(sm1[sl, 16:20], sm1[sl, 16:20], AF.Ln, bias=one_t[sl, 0:1])
            ts(sm1[sl, 16:20], sm1[sl, 16:20], -1.0, ALU.mult)
            ps = rot()
            mm(ps[sl, 0:4], tri[sl, sl], sm1[sl, 16:20], True, True)
            mm(ps[sl, 4:8], onesf[sl, sl], sm1[sl, 16:20], True, True)
            cp(sm1[sl, 20:28], ps[sl, 0:8])
            tt_(sm1[sl, 28:32], ipre, sm1[sl, 20:24], ALU.subtract)
            for h in range(4):
                ts(s512a[sl, h * 128:h * 128 + n], identf[sl, sl], sm1[sl, 28 + h:29 + h], ALU.mult)
            pa = rot()
            mm(pa[sl, :].re("p (h s) -> p h s", h=4)[:, :, 0:n], onesf[sl, sl], s512a[sl, :].re("p (h s) -> p h s", h=4)[:, :, 0:n], True, True)
            red(sm1[sl, 32:36], pa[sl, :].re("p (h s) -> p h s", h=4)[:, :, 0:n], op=ALU.max)
            tt_(sm1[sl, 32:36], sm1[sl, 32:36], mprev[sl, :], ALU.max)
            tt_(s512b[sl, :].re("p (h s) -> p h s", h=4)[:, :, 0:n], pa[sl, :].re("p (h s) -> p h s", h=4)[:, :, 0:n],
                negts[sl, sl].un(1).bc([n, 4, n]), ALU.add)
            red(sm1[sl, 36:40], s512b[sl, :].re("p (h s) -> p h s", h=4)[:, :, 0:n], op=ALU.max)
            tt_(sm1[sl, 36:40], sm1[sl, 36:40], mprev[sl, :], ALU.max)
            tt_(sm1[sl, 40:44], sm1[sl, 20:24], sm1[sl, 36:40], ALU.add)
            act(sm1[sl, 44:48], sm1[sl, 40:44], AF.Exp, scale=-1.0)
            tt_(sm1[sl, 48:52], mprev[sl, :], sm1[sl, 36:40], ALU.subtract)
            act(sm1[sl, 48:52], sm1[sl, 48:52], AF.Exp)
            for h in range(4):
                ts(s512a[sl, h * 128:h * 128 + n], identf[sl, sl], sm1[sl, 36 + h:37 + h], ALU.mult)
            pm = rot()
            mv = pm[sl, :].re("p (h s) -> p h s", h=4)[:, :, 0:n]
            for h in range(4):
                mm(pm[sl, h * 128:h * 128 + n], onesf[sl, sl], s512a[sl, h * 128:h * 128 + n], True, False)
                mm(pm[sl, h * 128:h * 128 + n], identf[sl, sl], pos4[sl, sl.start:sl.start + n], False, True)
            for h in range(4):
                act(s512c[sl, h * 128:h * 128 + n], pm[sl, h * 128:h * 128 + n], AF.Exp, bias=sm1[sl, 28 + h:29 + h], scale=-1.0)
            tt_(sm1[sl, 52:56], sm1[sl, 28:32], sm1[sl, 32:36], ALU.subtract)
            act(sm1[sl, 52:56], sm1[sl, 52:56], AF.Exp)
            tt_(sm1[sl, 56:60], mprev[sl, :], sm1[sl, 32:36], ALU.subtract)
            act(sm1[sl, 56:60], sm1[sl, 56:60], AF.Exp)
            pb_ = rot()
            for h in range(4):
                ts(s512a[sl, h * 128:h * 128 + n], identf[sl, sl], sm1[sl, 56 + h:57 + h], ALU.mult)
            mm(pb_[:, 0:4], onesf[sl, 0:128], s512a[sl, :].re("p (h s) -> p h s", h=4)[:, :, 0:1], True, True)
            cp(sm2[:, 16:20], pb_[:, 0:4])
            tt_(sm1[sl, 60:64], sm1[sl, 24:28], sm1[sl, 32:36], ALU.add)
            for h in range(4):
                ts(s512a[sl, h * 128:h * 128 + n], identf[sl, sl], sm1[sl, 60 + h:61 + h], ALU.mult)
            pb2 = rot()
            mm(pb2[:, 0:4], onesf[sl, 0:128], s512a[sl, :].re("p (h s) -> p h s", h=4)[:, :, 0:1], True, True)
            cp(sm2[:, 20:24], pb2[:, 0:4])
            for h in range(4):
                pk = rot()
                for dc in range(2):
                    mm(pk[sl, 0:n], kT_o(h * 2 + dc, c0, n), qT_o(h * 2 + dc, c0, n), dc == 0, dc == 1)
                tt_(b512b[sl, 0:n], pk[sl, 0:n], s512c[sl, h * 128:h * 128 + n], ALU.mult)
                pn1 = PS[4 + (h % 2) * 2]; pn2 = PS[5 + (h % 2) * 2]
                mm(pn1[sl, 0:257], b512b[sl, 0:n], vaug(t)[sl, h, 0:257], True, True)
                for dc in range(2):
                    mm(pn2[sl, 0:257], qT_o(h * 2 + dc, c0, n), CTb[:, h, dc, 0:257], dc == 0, dc == 1)
                ts(s512b[sl, 0:257], pn2[sl, 0:257], sm1[sl, 48 + h:49 + h], ALU.mult)
                tt_(s512b[sl, 0:257], s512b[sl, 0:257], pn1[sl, 0:257], ALU.add)
                ts(sm2[sl, 26:27], s512b[sl, 256:257], -1.0, ALU.mult)
                tt_(sm2[sl, 24:25], s512b[sl, 256:257], sm2[sl, 26:27], ALU.max)
                tt_(sm2[sl, 24:25], sm2[sl, 24:25], sm1[sl, 44 + h:45 + h], ALU.max)
                recip(sm2[sl, 25:26], sm2[sl, 24:25])
                ts(aout[sl, h * 256:(h + 1) * 256], s512b[sl, 0:256], sm2[sl, 25:26], ALU.mult)
            for h in range(4):
                ts(b512a[sl, 0:256], ktok(t)[sl, h * 256:(h + 1) * 256], sm1[sl, 52 + h:53 + h], ALU.mult)
                for dc in range(2):
                    pc = rot()
                    mm(pc[:, 0:257], b512a[sl, dc * 128:(dc + 1) * 128], vaug(t)[sl, h, 0:257], True, True)
                    stt(CT[:, h, dc, 0:257], CT[:, h, dc, 0:257], sm2[:, 16 + h:17 + h], pc[:, 0:257], ALU.mult, ALU.add)
            cp(CTb[:].re("p a b c -> p (a b c)"), CT[:].re("p a b c -> p (a b c)"), e="act")
            cp(mprev[:, :], sm2[:, 20:24])

        def convert_all():
            convW(WgB[0], Wg[0], D); convW(WuB[0], Wu[0], D); convW(WdB[0], Wd[0], DFF)
            convW(EWinB, EWin, D); convW(EWoutB, EWout, D)
            convW(WgB[1], Wg[1], D); convW(WuB[1], Wu[1], D); convW(WdB[1], Wd[1], DFF)
            convW(WgB[2], Wg[2], D); convW(WuB[2], Wu[2], D); convW(WdB[2], Wd[2], DFF)
            convW(OWinB, OWin, D); convW(OWoutB, OWout, D)
            convW(WgB[3], Wg[3], D); convW(WuB[3], Wu[3], D); convW(WdB[3], Wd[3], DFF)
        def main_prog():
            convert_all()
            memset(Sg[:, :, :], 0.0); memset(Sgb[:, :, :], 0.0)
            memset(CT[:].re("p a b c -> p (a b c)"), 0.0); memset(CTb[:].re("p a b c -> p (a b c)"), 0.0)
            memset(mprev[:, :], 0.0)
            if KSTOP == 1: return
            for s_i in range(NST):
                g0 = s_i * STT
                for t in range(STT):
                    p.dma("sp", X[:, t, :], xp[(g0 + t) * 128:(g0 + t + 1) * 128, :])
                ffn(0, 0, STT)
                if KSTOP == 2: return
                even_mixer(STT, g0, False)
                if KSTOP == 3: return
                ffn(1, 2, STT)
                ffn(2, 3, STT)
                odd_mixer(STT, False)
                if KSTOP == 4: return
                ffn(3, 5, STT)
                for t in range(STT):
                    p.dma("sp", yp[(g0 + t) * 128:(g0 + t + 1) * 128, :], X[:, t, :])
            p.dma("sp", glap.rearrange("(j hh) d v -> (hh d) j v", hh=2), Sg[:])
            store_C(Cp, np_, mp[0:1, :])
            if KSTOP == 5: return
            for i_, nm in enumerate(("K0", "V0", "K1", "V1", "qbc")):
                SB[nm] = KTbuf.sub((slice(None), slice(i_ * 1024, (i_ + 1) * 1024)), nm)
            SB["vb0"] = KTbuf.sub((slice(None), slice(5120, 5632)), "vb0"); SB["vb1"] = KTbuf.sub((slice(None), slice(5632, 6144)), "vb1")
            SB["vn"] = KTbuf.sub((slice(None), slice(6144, 7168)), "vn")
            SB["abs"] = Vbuf.sub((slice(None), slice(0, 2 * NPG * 8)), "abs")
            SB["idx"] = Vbuf.sub((slice(None), slice(2 * NPG * 8, 2 * NPG * 8 + 8 * NPG)), "idx")
            p.dma("sp", SB["abs"][:, :].bitcast(F32), c_abs[:, :])
            ptb = s512b[:, 0:4 * NPG].bitcast(I32)
            p.dma("sp", ptb, ptab[0:1, :].partition_broadcast(128))
            cp(s512c[:, 0:4 * NPG], ptb)
            ts(SB["idx"][:, :].bitcast(I32), s512c[:, 0:4 * NPG], 128.0, ALU.mult, iota[:, 0:1], ALU.add)
            memset(X[:, 0:2, :], 0.0)
            for r in range(4):
                p.dma("sp", X[64 * (r % 2):64 * (r % 2) + 1, r // 2, :], xs[r:r + 1, :])
            ffn(0, 0, 2)
            if KSTOP == 6: return
            even_mixer(2, 0, True)
            if KSTOP == 7: return
            ffn(1, 2, 2)
            ffn(2, 3, 2)
            odd_mixer(2, True)
            ffn(3, 5, 2)
            for r in range(4):
                p.dma("sp", ys[r:r + 1, :], X[64 * (r % 2):64 * (r % 2) + 1, r // 2, :])

        main_prog()
        p.finish()
    return nc


def make_consts(TSEQ, NPG):
    NTT = TSEQ // 128; PAST = NPG * 128
    ar = np.arange(128)
    c = {}
    c["c_identf"] = np.eye(128, dtype=np.float32)
    c["c_tri"] = (ar[:, None] <= ar[None, :]).astype(np.float32)
    c["c_trirev"] = (ar[:, None] > ar[None, :]).astype(np.float32)
    c["c_negts"] = np.where(ar[None, :] > ar[:, None], -1e30, 0.0).astype(np.float32)
    pos = np.where(ar[:, None] > ar[None, :], 1e30, 0.0).astype(np.float32)
    c["c_pos4"] = pos
    c["c_negbf"] = np.where(ar[:, None] > ar[None, :], -30000.0, 0.0).astype(np.float32)
    slopes = np.array([2.0 ** (-8.0 * (h + 1) / 4) for h in range(4)], np.float32)
    abp = np.zeros((128, 4, NTT), np.float32)
    for h in range(4):
        for dd in range(NTT):
            abp[:, h, dd] = slopes[h] * (-128.0 * dd + ar - 127.0)
    c["c_abp"] = abp.reshape(128, 4 * NTT)
    ab = np.zeros((128, NPG, 8), np.float32)
    for j in range(NPG):
        for h in range(4):
            ab[:, j, 2 * h] = ab[:, j, 2 * h + 1] = slopes[h] * (128.0 * j + ar - PAST)
    c["c_abs"] = ab.reshape(128, NPG * 8)
    c["c_iota"] = ar.astype(np.float32).reshape(128, 1)
    bmk = np.zeros((8, 8), np.float32)
    for h in range(4):
        bmk[2 * h, h] = 1.0
        bmk[2 * h + 1, 4 + h] = 1.0
    c["c_bm"] = bmk
    return c


def run(inputs, TSEQ, NPG, NPOOL, n_cores=8):
    f = lambda a: np.ascontiguousarray(np.asarray(a))
    nc = build(TSEQ, NPG, NPOOL)
    consts = make_consts(TSEQ, NPG)
    B = inputs["x_prompt"].shape[0]
    shared = {
        "ck": f(inputs["cache_k"]).reshape(NPOOL * 128, 512), "cv": f(inputs["cache_v"]).reshape(NPOOL * 128, 512),
        "normg": f(inputs["norm_g"]).reshape(6, D),
        "wg": f(inputs["ffn_w_gate"]).reshape(4, D, DFF), "wu": f(inputs["ffn_w_up"]).reshape(4, D, DFF),
        "wd": f(inputs["ffn_w_down"]).reshape(4, DFF, D),
        "ewin": f(inputs["even_w_in"])[0], "ewout": f(inputs["even_w_out"])[0],
        "aqk": f(inputs["a_qk_norm"]).reshape(1, 128), "alam": f(inputs["a_lambda"]).reshape(1, 256),
        "ahn": f(inputs["a_head_norm"]).reshape(1, 512), "bw2": f(inputs["b_gate_w2"])[0],
        "bgb": f(inputs["b_gate_bias"]).reshape(1, 256), "bhn": f(inputs["b_head_norm"]).reshape(1, 512),
        "owin": f(inputs["odd_w_in"])[0], "owout": f(inputs["odd_w_out"])[0],
        "cgb": f(inputs["c_gate_bias"]).reshape(1, 8), "chn": f(inputs["c_head_norm"]).reshape(1, 1024),
    }
    shared.update(consts)
    shared["c_negbf"] = consts["c_negbf"]
    in_maps = []
    for c in range(n_cores):
        b = c % B; s0 = 4 * c
        m = dict(shared)
        m["xp"] = f(inputs["x_prompt"][b]); m["xs"] = f(inputs["x_sample"][s0:s0 + 4, 0])
        m["sgla"] = f(inputs["state_gla"][0, s0:s0 + 4]); m["sC"] = f(inputs["state_mlstm_C"][0, s0:s0 + 4])
        m["sn"] = f(inputs["state_mlstm_n"][0, s0:s0 + 4]); m["sm"] = f(inputs["state_mlstm_m"][0, s0:s0 + 4])
        m["ptab"] = f(inputs["page_table"][s0:s0 + 4]).reshape(1, 4 * NPG).astype(np.int32)
        in_maps.append(m)
    res = run_bass_kernel_spmd(nc, in_maps, core_ids=list(range(n_cores)))
    R = res.results
    cat = lambda k, cores: np.stack([R[c][k] for c in cores])
    pc = list(range(min(B, n_cores))); ac = list(range(n_cores)); B = len(pc)
    y_prompt = cat("yp", pc)
    y_sample = np.concatenate([R[c]["ys"] for c in ac])[:, None, :]
    k_prompt = cat("kp", pc).reshape(1, B, TSEQ, 4, 2, 64)
    v_prompt = cat("vp", pc).reshape(1, B, TSEQ, 4, 128)
    k_sample = np.concatenate([R[c]["ks"] for c in ac]).reshape(1, 4 * n_cores, 1, 4, 2, 64)
    v_sample = np.concatenate([R[c]["vs"] for c in ac]).reshape(1, 4 * n_cores, 1, 4, 128)
    gla_prompt = cat("glap", pc)[None]
    gla_sample = np.concatenate([R[c]["glas"] for c in ac])[None]
    C_prompt = cat("Cp", pc)[None]; n_prompt = cat("np", pc)[None]; m_prompt = cat("mp", pc).reshape(1, B, 4)
    C_sample = np.concatenate([R[c]["Cs"] for c in ac])[None]
    n_sample = np.concatenate([R[c]["ns"] for c in ac])[None]
    m_sample = np.concatenate([R[c]["ms"] for c in ac])[None]
    outs = (y_prompt, y_sample, k_prompt, v_prompt, k_sample, v_sample, gla_prompt, gla_sample,
            C_prompt, n_prompt, m_prompt, C_sample, n_sample, m_sample)
    return tuple(np.ascontiguousarray(o, dtype=np.float32) for o in outs)


def kernel(**inputs):
    TSEQ = inputs["x_prompt"].shape[1]
    NPG = inputs["page_table"].shape[1]
    NPOOL = inputs["cache_k"].shape[1]
    return run(inputs, TSEQ, NPG, NPOOL)
```

```python
import contextlib, math, os
KSTOP = int(os.environ.get('KSTOP', '0'))
KSUB = int(os.environ.get('KSUB', '0'))
import numpy as np
import ml_dtypes
import concourse.bass as bass
import concourse.mybir as mybir
from concourse.bass_utils import run_bass_kernel_spmd

F32 = mybir.dt.float32; BF16 = mybir.dt.bfloat16; I32 = mybir.dt.int32
AF = mybir.ActivationFunctionType; ALU = mybir.AluOpType; AX = mybir.AxisListType
EPS = 1e-6
D = 1024; DFF = 2816; PE_ = 3088; PO_ = 4104


class V:
    def __init__(s, t, ap): s.t = t; s.ap = ap
    def __getitem__(s, k): return V(s.t, s.ap[k])
    def re(s, pat, **kw): return V(s.t, s.ap.rearrange(pat, **kw))
    def bitcast(s, dt): return V(s.t, s.ap.bitcast(dt))
    def bc(s, shape): return V(s.t, s.ap.to_broadcast(list(shape)))
    def un(s, ax): return V(s.t, s.ap.unsqueeze(ax))


class T:
    def __init__(s, h, name): s.h = h; s.name = name; s.w = None; s.r = []; s.psum = False
    def __getitem__(s, k): return V(s, s.h[k])
    def sub(s, k, name):
        n = T(s.h[k], name); n.w = s.w; n.r = list(s.r)
        return n


class P:
    def __init__(self, nc, es, n_dma_sems=32):
        self.nc = nc; self.es = es
        self.eng = {"pe": nc.tensor, "act": nc.scalar, "dve": nc.vector, "pool": nc.gpsimd, "sp": nc.sync}
        self.sem = {e: es.enter_context(nc.semaphore("s_" + e)) for e in self.eng}
        self.cnt = {e: 0 for e in self.eng}
        self.seen = {e: {} for e in self.eng}
        self.dsems = [es.enter_context(nc.semaphore("d%d" % i)) for i in range(n_dma_sems)]
        self.dval = [0] * n_dma_sems
        self.dnext = {"sp": 0, "pool": 0}
        self.drange = {"sp": (0, n_dma_sems // 2), "pool": (n_dma_sems // 2, n_dma_sems)}
    def sb(self, name, shape, dt=F32):
        return T(self.es.enter_context(self.nc.sbuf_tensor(name, list(shape), dt)), name)
    def psum(self, name, shape, dt=F32):
        t = T(self.es.enter_context(self.nc.psum_tensor(name, list(shape), dt)), name)
        t.psum = True
        return t
    def _wait(self, e, d):
        key, sem, val, deng = d
        if deng == e and e == "pe":
            return
        if self.seen[e].get(key, 0) >= val:
            return
        self.eng[e].wait_ge(sem, val)
        self.seen[e][key] = val
    def _deps(self, e, reads, writes):
        deps = []
        for t in reads:
            if t.w is not None: deps.append(t.w)
            if t.psum:
                deps.extend(d for d in t.r if d[3] != e)
        for t in writes:
            if t.w is not None: deps.append(t.w)
            deps.extend(t.r)
        for d in deps:
            self._wait(e, d)
    def _mark(self, me, reads, writes):
        for t in reads:
            t.r.append(me)
            if len(t.r) > 64:
                t.r = self._prune(t.r)
        for t in writes:
            t.w = me; t.r = []
    @staticmethod
    def _prune(r):
        best = {}
        for d in r:
            if d[0] not in best or best[d[0]][2] < d[2]:
                best[d[0]] = d
        return list(best.values())
    def I(self, e, name, outs, ins, **kw):
        args = {}
        reads = []; writes = []
        for k, v in outs.items():
            args[k] = v.ap; writes.append(v.t)
        for k, v in ins.items():
            if isinstance(v, V):
                args[k] = v.ap; reads.append(v.t)
            else:
                args[k] = v
        args.update(kw)
        self._deps(e, reads, writes)
        ins_ = getattr(self.eng[e], name)(**args)
        self.cnt[e] += 1
        ins_.then_inc(self.sem[e], 1)
        self._mark((e, self.sem[e], self.cnt[e], e), reads, writes)
        return ins_
    def dma(self, e, out, in_, indirect=None):
        reads = [in_.t] if isinstance(in_, V) else []
        writes = [out.t] if isinstance(out, V) else []
        if indirect is not None:
            reads.append(indirect.t)
        self._deps(e, reads, writes)
        lo, hi = self.drange[e]
        i = lo + self.dnext[e]; self.dnext[e] = (self.dnext[e] + 1) % (hi - lo)
        sem = self.dsems[i]
        if self.dval[i] > 0:
            self._wait(e, ("d%d" % i, sem, self.dval[i], "dma"))
        self.dval[i] += 16
        o = out.ap if isinstance(out, V) else out
        s = in_.ap if isinstance(in_, V) else in_
        if indirect is not None:
            ins_ = self.eng[e].indirect_dma_start(out=o, out_offset=None, in_=s,
                                                  in_offset=bass.IndirectOffsetOnAxis(ap=indirect.ap, axis=0))
        else:
            ins_ = self.eng[e].dma_start(out=o, in_=s)
        ins_.then_inc(sem, 16)
        self._mark(("d%d" % i, sem, self.dval[i], "dma"), reads, writes)
        return ins_
    def finish(self):
        for i, sem in enumerate(self.dsems):
            if self.dval[i] > 0:
                self._wait("sp", ("d%d" % i, sem, self.dval[i], "dma"))
        for e in self.eng:
            if e != "sp" and self.cnt[e] > 0:
                self._wait("sp", (e, self.sem[e], self.cnt[e], e))


def build(TSEQ, NPG, NPOOL):
    NTT = TSEQ // 128
    STT = min(4, NTT)
    NST = NTT // STT
    NMAX = STT * 128
    PAST = NPG * 128
    nc = bass.Bass("TRN2", target_bir_lowering=False)

    def din(name, shape, dt=F32):
        return nc.dram_tensor(name, list(shape), dt, kind="ExternalInput").ap()
    def dout(name, shape):
        return nc.dram_tensor(name, list(shape), F32, kind="ExternalOutput").ap()

    xp = din("xp", [TSEQ, D]); xs = din("xs", [4, D])
    ck = din("ck", [NPOOL * 128, 512]); cv = din("cv", [NPOOL * 128, 512])
    sgla = din("sgla", [4, 4, 64, 128]); sC = din("sC", [4, 4, 256, 256]); sn = din("sn", [4, 4, 256]); sm = din("sm", [4, 4])
    ptab = din("ptab", [1, 4 * NPG], I32)
    normg = din("normg", [6, D])
    Wg = din("wg", [4, D, DFF]); Wu = din("wu", [4, D, DFF]); Wd = din("wd", [4, DFF, D])
    EWin = din("ewin", [D, PE_]); EWout = din("ewout", [D, D])
    aqk = din("aqk", [1, 128]); alam = din("alam", [1, 256]); ahn = din("ahn", [1, 512])
    bw2 = din("bw2", [16, 256]); bgb = din("bgb", [1, 256]); bhn = din("bhn", [1, 512])
    OWin = din("owin", [D, PO_]); OWout = din("owout", [D, D])
    cgb = din("cgb", [1, 8]); chn = din("chn", [1, 1024])
    c_identf = din("c_identf", [128, 128]); c_tri = din("c_tri", [128, 128]); c_trirev = din("c_trirev", [128, 128])
    c_negts = din("c_negts", [128, 128]); c_pos4 = din("c_pos4", [128, 128]); c_negbf = din("c_negbf", [128, 256])
    c_abp = din("c_abp", [128, 4 * NTT]); c_abs = din("c_abs", [128, NPG * 8]); c_iota = din("c_iota", [128, 1])
    c_bm = din("c_bm", [8, 8])

    yp = dout("yp", [TSEQ, D]); ys = dout("ys", [4, D])
    kp = dout("kp", [TSEQ, 512]); vp = dout("vp", [TSEQ, 512]); ks = dout("ks", [4, 512]); vs = dout("vs", [4, 512])
    glap = dout("glap", [4, 64, 128]); glas = dout("glas", [4, 4, 64, 128])
    Cp = dout("Cp", [4, 256, 256]); np_ = dout("np", [4, 256]); mp = dout("mp", [1, 4])
    Cs = dout("Cs", [4, 4, 256, 256]); ns = dout("ns", [4, 4, 256]); ms = dout("ms", [4, 4])

    es = contextlib.ExitStack()
    with es:
        p = P(nc, es)
        X = p.sb("X", [128, STT, D])
        tokb = p.sb("tokb", [128, D], BF16)
        junk = p.sb("junk", [128, D], BF16)
        tokb2 = p.sb("tokb2", [128, D], BF16)
        nsm = [p.sb("nsm%d" % i, [128, 2]) for i in range(2)]
        actT = p.sb("actT", [128, 8, NMAX], BF16)
        NW = 6
        Wb = [p.sb("Wb%d" % i, [128, 8, 256], BF16) for i in range(NW)]
        def dscr(name, shape):
            return T(nc.dram_tensor(name, list(shape), BF16, kind="Internal").ap(), name)
        WgB = [dscr("wgb%d" % f, [D, DFF]) for f in range(4)]
        WuB = [dscr("wub%d" % f, [D, DFF]) for f in range(4)]
        WdB = [dscr("wdb%d" % f, [DFF, D]) for f in range(4)]
        EWinB = dscr("ewinb", [D, PE_]); EWoutB = dscr("ewoutb", [D, D])
        OWinB = dscr("owinb", [D, PO_]); OWoutB = dscr("owoutb", [D, D])
        KTbuf = p.sb("KT", [128, max(4 * TSEQ, 7168)], BF16)
        KT = KTbuf[:, 0:4 * TSEQ].re("p (h n) -> p h n", h=4)
        VSZ = max(NTT * 520, 2 * NPG * 8 + 2 * 4 * NPG + 16)
        Vbuf = p.sb("Vst", [128, VSZ], BF16)
        Vst = Vbuf[:, 0:NTT * 520].re("p (a b c) -> p a b c", a=NTT, b=4)
        gbc = p.sb("gbc", [128, D])
        mixA = p.sb("mixA", [128, 8192], BF16)
        mixB = p.sb("mixB", [128, 8256], BF16)
        mixC = p.sb("mixC", [128, 4096], BF16)
        Sg = p.sb("Sg", [128, 2, 128]); Sgb = p.sb("Sgb", [128, 2, 128], BF16)
        CT = p.sb("CT", [128, 4, 2, 258]); CTb = p.sb("CTb", [128, 4, 2, 258], BF16)
        mprev = p.sb("mprev", [128, 4])
        identf = p.sb("identf", [128, 128]); identb = p.sb("identb", [128, 128], BF16)
        tri = p.sb("tri", [128, 128]); trirev = p.sb("trirev", [128, 128]); negts = p.sb("negts", [128, 128])
        pos4 = p.sb("pos4", [128, 128]); negbf = p.sb("negbf", [128, 256], BF16)
        onesf = p.sb("onesf", [128, 128]); onesb = p.sb("onesb", [128, 128], BF16)
        abp = p.sb("abp", [128, 4 * NTT]); iota = p.sb("iota", [128, 1])
        bm = p.sb("bm", [8, 8])
        aqk_bc = p.sb("aqk_bc", [128, 128]); ahn_bc = p.sb("ahn_bc", [128, 512]); bhn_bc = p.sb("bhn_bc", [128, 512])
        chn_bc = p.sb("chn_bc", [128, 1024]); cgb_bc = p.sb("cgb_bc", [128, 8]); alam_bc = p.sb("alam_bc", [128, 256])
        W2aug = p.sb("W2aug", [32, 256])
        neglam = p.sb("neglam", [128, 1])
        sm1 = p.sb("sm1", [128, 64]); sm2 = p.sb("sm2", [128, 64]); sm3 = p.sb("sm3", [128, 64])
        s512a = p.sb("s512a", [128, 512]); s512b = p.sb("s512b", [128, 512]); s512c = p.sb("s512c", [128, 512])
        b512a = p.sb("b512a", [128, 512], BF16); b512b = p.sb("b512b", [128, 512], BF16); b512c = p.sb("b512c", [128, 512], BF16)
        sg = [s512a, s512b]
        bgT = p.sb("bgT", [32, NMAX])
        gates = p.sb("gates", [128, STT, 8])
        aout = p.sb("aout", [128, 1024])
        PS = [p.psum("ps%d" % i, [128, 512]) for i in range(8)]
        st = {"rot": 0, "w": 0, "sg": 0}
        def rot():
            b = PS[st["rot"]]; st["rot"] = (st["rot"] + 1) % 4
            return b
        def nextW():
            w = Wb[st["w"]]; st["w"] = (st["w"] + 1) % NW
            return w

        def mm(out, lhsT, rhs, start, stop):
            p.I("pe", "matmul", {"out": out}, {"lhsT": lhsT, "rhs": rhs}, start=start, stop=stop)
        def tr(out, in_, ident):
            p.I("pe", "transpose", {"out": out}, {"in_": in_, "identity": ident})
        def act(out, in_, func, bias=None, scale=None, accum=None):
            outs = {"out": out}; ins = {"in_": in_}
            kw = {"func": func}
            if accum is not None: outs["accum_out"] = accum
            if bias is not None: ins["bias"] = bias
            if scale is not None: ins["scale"] = scale
            p.I("act", "activation", outs, ins, **kw)
        def tt_(out, in0, in1, op, e="dve"):
            p.I(e, "tensor_tensor", {"out": out}, {"in0": in0, "in1": in1}, op=op)
        def ts(out, in0, s1, op0, s2=None, op1=None, e="dve"):
            kw = {"op0": op0}
            if op1 is not None: kw["op1"] = op1
            p.I(e, "tensor_scalar", {"out": out}, {"in0": in0, "scalar1": s1, "scalar2": s2}, **kw)
        def stt(out, in0, scalar, in1, op0, op1):
            p.I("dve", "scalar_tensor_tensor", {"out": out}, {"in0": in0, "scalar": scalar, "in1": in1}, op0=op0, op1=op1)
        def red(out, in_, op=ALU.add):
            p.I("dve", "tensor_reduce", {"out": out}, {"in_": in_}, axis=AX.X, op=op)
        def recip(out, in_):
            p.I("dve", "reciprocal", {"out": out}, {"in_": in_})
        def cp(out, in_, e="dve"):
            if e == "act":
                act(out, in_, AF.Copy)
            else:
                p.I(e, "tensor_copy", {"out": out}, {"in_": in_})
        def memset(v, val, e="dve"):
            p.I(e, "memset", {"ap": v}, {}, constant=val)
        def loadW(dst, WT, r0, nk, c0, ncol):
            src = WT[r0:r0 + nk * 128, c0:c0 + ncol].re("(k p) f -> p k f", p=128)
            p.dma("sp", dst, src)
        def convW(WT, src2d, nrows):
            for r0 in range(0, nrows, 128):
                p.dma("pool", WT[r0:r0 + 128, :], src2d[r0:r0 + 128, :])
        def rstd_of(out, ss, n, sl):
            act(out, ss, AF.Sqrt, bias=eps_t[sl, 0:1], scale=1.0 / n)
            recip(out, out)

        eps_t = p.sb("eps_t", [128, 1]); one_t = p.sb("one_t", [128, 1])
        memset(eps_t[:, :], EPS); memset(one_t[:, :], 1.0)

        for t_, d_ in ((identf, c_identf), (tri, c_tri), (trirev, c_trirev), (negts, c_negts), (pos4, c_pos4),
                       (abp, c_abp), (iota, c_iota), (bm, c_bm)):
            p.dma("sp", t_[:], d_[:, :] if len(d_.shape) == 2 else d_)
        p.dma("pool", negbf[:], c_negbf[:, :])
        cp(identb[:, :], identf[:, :])
        memset(onesf[:, :], 1.0); memset(onesb[:, :], 1.0)
        for t_, d_, n_ in ((aqk_bc, aqk, 128), (ahn_bc, ahn, 512), (bhn_bc, bhn, 512), (chn_bc, chn, 1024),
                           (cgb_bc, cgb, 8), (alam_bc, alam, 256)):
            p.dma("sp", t_[:], d_[0:1, :].partition_broadcast(128))
        memset(W2aug[:, :], 0.0)
        p.dma("sp", W2aug[0:16, :], bw2[:, :])
        p.dma("sp", W2aug[16:17, :], bgb[0:1, :])
        memset(bgT[:, :], 1.0)
        memset(Vst.re("p a b c -> p (a b) c")[:, :, 128:130], 1.0)
        tt_(s512a[:, 0:64], alam_bc[:, 0:64], alam_bc[:, 64:128], ALU.mult)
        tt_(s512a[:, 64:128], alam_bc[:, 128:192], alam_bc[:, 192:256], ALU.mult)
        red(sm1[:, 0:2], s512a[:, 0:128].re("p (a b) -> p a b", a=2))
        act(sm1[:, 2:4], sm1[:, 0:2], AF.Exp)
        tt_(sm1[:, 4:5], sm1[:, 3:4], sm1[:, 2:3], ALU.subtract)
        ts(neglam[:, :], sm1[:, 4:5], -0.2, ALU.add)

        hT = mixA
        def hT_view(fb, N):
            per = 8192 // NMAX
            reg = (mixA, mixB, mixC)[fb // per]
            o = (fb % per) * NMAX
            return reg[:, o:o + N]

        def norm_T(nt, gi):
            p.dma("sp", gbc[:], normg[gi:gi + 1, :].partition_broadcast(128))
            for t in range(nt):
                par = t % 2
                tb = (tokb, tokb2)[par]
                act(junk[:, :], X[:, t, :], AF.Square, accum=nsm[par][:, 0:1])
                rstd_of(nsm[par][:, 1:2], nsm[par][:, 0:1], D, slice(0, 128))
                stt(tb[:, :], X[:, t, :], nsm[par][:, 1:2], gbc[:, :], ALU.mult, ALU.mult)
                tok_to_actT(t, tb, "dve" if par else "act")
        def tok_to_actT(t, src=None, ce="act"):
            src = tokb if src is None else src
            ps = rot(); psb = ps[:, :].bitcast(BF16)
            for kc in range(8):
                tr(psb[:, kc * 128:(kc + 1) * 128], src[:, kc * 128:(kc + 1) * 128], identb[:, :])
            cp(actT[:, :, t * 128:(t + 1) * 128], psb[:, 0:1024].re("p (k n) -> p k n", k=8), e=ce)

        def ffn(f, gi, nt):
            N = nt * 128
            norm_T(nt, gi)
            for sbk in range(11):
                c0 = sbk * 256; ncol = 256
                wg_ = nextW(); loadW(wg_[:, :, 0:ncol], WgB[f], 0, 8, c0, ncol)
                wu_ = nextW(); loadW(wu_[:, :, 0:ncol], WuB[f], 0, 8, c0, ncol)
                for q in range(ncol // 128):
                    fb = sbk * 2 + q
                    pg = rot(); pu = rot()
                    for kc in range(8):
                        mm(pg[:, 0:N], wg_[:, kc, q * 128:(q + 1) * 128], actT[:, kc, 0:N], kc == 0, kc == 7)
                    for kc in range(8):
                        mm(pu[:, 0:N], wu_[:, kc, q * 128:(q + 1) * 128], actT[:, kc, 0:N], kc == 0, kc == 7)
                    s_ = sg[st["sg"]]; st["sg"] ^= 1
                    act(s_[:, 0:N], pg[:, 0:N], AF.Silu)
                    tt_(hT_view(fb, N), s_[:, 0:N], pu[:, 0:N], ALU.mult)
            for db in range(2):
                for g in range(6):
                    nf = min(4, 22 - g * 4)
                    w = nextW()
                    wv = w[:].re("p k f -> p (k f)")[:, 0:nf * 512].re("p (k f) -> p k f", k=nf)
                    loadW(wv, WdB[f], g * 512, nf, db * 512, 512)
                    for t in range(nt):
                        for q in range(nf):
                            fc = g * 4 + q
                            mm(PS[4 + t][:, :], hT_view(fc, NMAX)[:, t * 128:(t + 1) * 128], wv[:, q, :], fc == 0, fc == 21)
                for t in range(nt):
                    stt(X[:, t, db * 512:(db + 1) * 512], PS[4 + t][:, :], 0.5, X[:, t, db * 512:(db + 1) * 512], ALU.mult, ALU.add)

        def out_proj(W2d, nt):
            for db in range(2):
                ws = []
                for hf in range(2):
                    w = nextW(); loadW(w[:, :, :], W2d, 0, 8, db * 512 + hf * 256, 256); ws.append(w)
                for t in range(nt):
                    ps = rot()
                    for hf in range(2):
                        for kc in range(8):
                            mm(ps[:, hf * 256:(hf + 1) * 256], actT[:, kc, t * 128:(t + 1) * 128], ws[hf][:, kc, :], kc == 0, kc == 7)
                    tt_(X[:, t, db * 512:(db + 1) * 512], X[:, t, db * 512:(db + 1) * 512], ps[:, :], ALU.add)

        def proj_blocks(W2d, nt, c0, ncol, consumer):
            ws = []
            for h0 in range(0, ncol, 256):
                hw = min(256, ncol - h0)
                w = nextW(); loadW(w[:, :, 0:hw], W2d, 0, 8, c0 + h0, hw); ws.append((w, h0, hw))
            for t in range(nt):
                ps = rot()
                for (w, h0, hw) in ws:
                    for kc in range(8):
                        mm(ps[:, h0:h0 + hw], actT[:, kc, t * 128:(t + 1) * 128], w[:, kc, 0:hw], kc == 0, kc == 7)
                consumer(t, ps)

        def headnorm(src, nh, dh, gain_bc, sl, extra_scale=None):
            n = nh * dh
            sq = s512a[sl, 0:n] if n <= 512 else gbc[sl, 0:n]
            act(sq, src, AF.Square)
            red(sm2[sl, 0:nh], sq.re("p (a b) -> p a b", a=nh))
            rstd_of(sm2[sl, 8:8 + nh], sm2[sl, 0:nh], dh, sl)
            if extra_scale is not None:
                ts(sm2[sl, 8:8 + nh], sm2[sl, 8:8 + nh], extra_scale, ALU.mult)
            tt_(src.re("p (a b) -> p a b", a=nh), src.re("p (a b) -> p a b", a=nh),
                sm2[sl, 8:8 + nh].un(2).bc([sl.stop - sl.start, nh, dh]), ALU.mult)
            tt_(src, src, gain_bc[sl, 0:n], ALU.mult)

        def QTp(h, t): return mixA[:, (t * 4 + h) * 256:(t * 4 + h + 1) * 256]
        def vB(t): return mixA[:, 4096 + t * 512: 4096 + (t + 1) * 512]
        def qkB(t): return mixB[:, t * 1024:(t + 1) * 1024].bitcast(F32)
        def srB(t): return mixB[:, 4096 + t * 1024: 4096 + (t + 1) * 1024].bitcast(F32)
        def qnS(t): return mixC[:, t * 1024:(t + 1) * 1024].bitcast(F32)
        def knS(t): return mixC[:, 2048 + t * 1024: 2048 + (t + 1) * 1024].bitcast(F32)
        def vnS(t): return SB["vn"][:, t * 512:(t + 1) * 512]
        SB = {}

        def qknorm(ps, which, dst):
            act(s512a[:, :], ps[:, :], AF.Square)
            red(sm2[:, 0:8], s512a[:, :].re("p (a b) -> p a b", a=8))
            rstd_of(sm2[:, 8:16], sm2[:, 0:8], 64, slice(0, 128))
            tt_(dst.re("p (a b) -> p a b", a=8), ps[:, :].re("p (a b) -> p a b", a=8), sm2[:, 8:16].un(2).bc([128, 8, 64]), ALU.mult)
            tt_(dst.re("p (a b) -> p a b", a=8), dst.re("p (a b) -> p a b", a=8),
                aqk_bc[:, which * 64:(which + 1) * 64].un(1).bc([128, 8, 64]), ALU.mult)

        def even_mixer(nt, g0, sample):
            N = nt * 128
            norm_T(nt, 1)
            def c_q(t, ps):
                dst = qnS(t) if sample else s512b[:, :]
                qknorm(ps, 0, dst)
                if not sample:
                    cp(b512a[:, :], dst, e="act")
                    pt_ = rot(); ptb_ = pt_[:, :].bitcast(BF16)
                    for h in range(4):
                        tr(ptb_[:, h * 128:(h + 1) * 128], b512a[:, h * 128:(h + 1) * 128], identb[:, :])
                    qv = mixA[:, t * 1024:(t + 1) * 1024].re("p (h c) -> p h c", h=4)
                    pv = ptb_[:, 0:512].re("p (h c) -> p h c", h=4)
                    cp(qv[0:64, :, 0:128], pv[0:64, :, :], e="act")
                    cp(qv[64:128, :, 128:256], pv[64:128, :, :])
            def c_k(t, ps):
                dst = knS(t) if sample else s512c[:, :]
                qknorm(ps, 1, dst)
                if sample:
                    for r2 in range(2):
                        p.dma("sp", ks[2 * t + r2:2 * t + r2 + 1, :], dst[64 * r2:64 * r2 + 1, :])
                else:
                    g = g0 + t
                    p.dma("sp", kp[g * 128:(g + 1) * 128, :], dst)
                    cp(b512b[:, :], dst, e="act")
                    pt_ = rot(); ptb_ = pt_[:, :].bitcast(BF16)
                    for h in range(4):
                        tr(ptb_[:, h * 128:(h + 1) * 128], b512b[:, h * 128:(h + 1) * 128], identb[:, :])
                    cp(KT[:, :, g * 128:(g + 1) * 128], ptb_[:, 0:512].re("p (h n) -> p h n", h=4))
            def c_v(t, ps):
                cp(s512b[:, :], ps[:, :], e="act")
                if sample:
                    for r2 in range(2):
                        p.dma("sp", vs[2 * t + r2:2 * t + r2 + 1, :], s512b[64 * r2:64 * r2 + 1, :])
                    cp(vnS(t), ps[:, :])
                else:
                    g = g0 + t
                    p.dma("sp", vp[g * 128:(g + 1) * 128, :], s512b[:, :])
                    cp(Vst[:, g, :, 0:128], ps[:, :].re("p (h v) -> p h v", h=4))
            def c_qkB(t, ps): cp(qkB(t), ps[:, :], e="act")
            def c_vB(t, ps): cp(vB(t), ps[:, :])
            def c_r(t, ps): act(srB(t), ps[:, :], AF.Silu)
            if not sample:
                memset(mixA[:, 0:nt * 1024], 0.0)
            proj_blocks(EWinB, nt, 0, 512, c_q)
            proj_blocks(EWinB, nt, 512, 512, c_k)
            proj_blocks(EWinB, nt, 1024, 512, c_v)
            proj_blocks(EWinB, nt, 1536, 512, c_qkB)
            proj_blocks(EWinB, nt, 2048, 512, c_vB)
            proj_blocks(EWinB, nt, 2576, 512, c_r)
            if KSUB == 1: return
            w = nextW(); loadW(w[:, :, 0:16], EWinB, 0, 8, 2560, 16)
            ps = rot()
            for kc in range(8):
                mm(ps[0:16, 0:N], w[:, kc, 0:16], actT[:, kc, 0:N], kc == 0, kc == 7)
            cp(bgT[0:16, 0:N], ps[0:16, 0:N])
            if KSUB == 2: return

            for t in range(nt):
                if sample:
                    for r2 in range(2):
                        r = 2 * t + r2; sl = slice(64 * r2, 64 * r2 + 1)
                        sample_attn(r, t, sl)
                        p.dma("sp", Sg[:], sgla[r].rearrange("(j hh) d v -> (hh d) j v", hh=2))
                        cp(Sgb[:, :, :], Sg[:, :, :], e="act")
                        gla_chunk(t, sl)
                        p.dma("sp", glas[r].rearrange("(j hh) d v -> (hh d) j v", hh=2), Sg[:])
                    sl_all = slice(0, 128)
                else:
                    attn_tile(t, g0 + t)
                    if KSUB == 3: return
                    if KSUB == 8 and t == 1: return
                    if not os.environ.get("KSKIPGLA"): gla_chunk(t, slice(0, 128))
                    if KSUB == 4: return
                headnorm(aout[:, 0:512], 4, 128, ahn_bc, slice(0, 128), extra_scale=0.8)
                headnorm(aout[:, 512:1024], 4, 128, bhn_bc, slice(0, 128))
                tt_(aout[:, 512:1024], aout[:, 512:1024], srB(t), ALU.mult)
                if KSUB == 5: return
                cp(tokb[:, :], aout[:, :], e="act")
                tok_to_actT(t)
                if KSUB == 6: return
            if KSUB == 7: return
            out_proj(EWoutB, nt)

        def attn_tile(t, i):
            pTs = (b512a, b512b, b512c)
            for h in range(4):
                a0 = PS[4 + 2 * (h % 2)]; a1 = PS[5 + 2 * (h % 2)]
                def stage1(kb):
                    s_ = rot()
                    mm(s_[:, 0:256], KT[:, h, kb * 128:(kb + 1) * 128], QTp(h, t), True, kb != i)
                    if kb == i:
                        mm(s_[:, 0:256], identb[:, :], negbf[:, :], False, True)
                    pT = pTs[kb % 3]
                    act(pT[:, 0:256], s_[:, 0:256], AF.Exp,
                        bias=abp[:, h * NTT + (i - kb): h * NTT + (i - kb) + 1], scale=0.125)
                    return pT
                pend = stage1(0)
                for kb in range(i + 1):
                    nxt = stage1(kb + 1) if kb + 1 <= i else None
                    mm(a0[:, 0:129], pend[:, 0:128], Vst[:, kb, h, 0:129], kb == 0, kb == i)
                    mm(a1[:, 0:129], pend[:, 128:256], Vst[:, kb, h, 0:129], kb == 0, kb == i)
                    pend = nxt
                recip(sm3[:, 0:1], a0[:, 128:129]); recip(sm3[:, 1:2], a1[:, 128:129])
                tt_(sm3[:, 2:3], sm3[:, 1:2], neglam[:, :], ALU.mult)
                ts(s512a[:, 0:128], a0[:, 0:128], sm3[:, 0:1], ALU.mult)
                stt(aout[:, h * 128:(h + 1) * 128], a1[:, 0:128], sm3[:, 2:3], s512a[:, 0:128], ALU.mult, ALU.add)

        def sample_attn(r, t, sl):
            qn = qnS(t); kn = knS(t)
            ps = rot()
            mm(ps[:, :], onesf[sl, 0:128], qn[sl, :], True, True)
            qbc = SB["qbc"][:, :].bitcast(F32); abs_ = SB["abs"][:, :].bitcast(F32); idx = SB["idx"][:, :].bitcast(I32)
            cp(qbc, ps[:, :], e="act")
            tt_(s512a[sl, :], qn[sl, :], kn[sl, :], ALU.mult)
            red(sm3[sl, 8:16], s512a[sl, :].re("p (a b) -> p a b", a=8))
            act(b512c[sl, 0:8], sm3[sl, 8:16], AF.Exp, scale=0.125)
            acc = PS[4]; den = PS[5]
            for j in range(NPG):
                Kb = SB["K%d" % (j % 2)][:, :].bitcast(F32); Vb_ = SB["V%d" % (j % 2)][:, :].bitcast(F32)
                col = r * NPG + j
                p.dma("pool", Kb, ck[:, :], indirect=idx[:, col:col + 1])
                p.dma("pool", Vb_, cv[:, :], indirect=idx[:, col:col + 1])
                tt_(s512b[:, :], Kb, qbc, ALU.mult)
                red(sm3[:, 16:24], s512b[:, :].re("p (a b) -> p a b", a=8))
                stt(sm3[:, 24:32], sm3[:, 16:24], 0.125, abs_[:, j * 8:(j + 1) * 8], ALU.mult, ALU.add)
                pb = b512a if j % 2 == 0 else b512b
                act(pb[:, 0:8], sm3[:, 24:32], AF.Exp)
                vbf = SB["vb%d" % (j % 2)][:, :]
                cp(vbf, Vb_, e="act")
                mm(acc[0:8, :], pb[:, 0:8], vbf, j == 0, False)
                mm(den[0:8, 0:1], pb[:, 0:8], onesb[:, 0:1], j == 0, False)
            mm(acc[0:8, :], b512c[sl, 0:8], vnS(t)[sl, :], NPG == 0, True)
            mm(den[0:8, 0:1], b512c[sl, 0:8], onesb[sl, 0:1], NPG == 0, True)
            recip(sm3[0:8, 32:33], den[0:8, 0:1])
            ts(s512a[0:8, :], acc[0:8, :], sm3[0:8, 32:33], ALU.mult)
            v3 = lambda x: x.re("p (h v) -> p h v", h=4)
            tt_(v3(s512b[0:8, :]), v3(s512a[0:8, :]), bm[0:8, 0:4].un(2).bc([8, 4, 128]), ALU.mult)
            tt_(v3(s512c[0:8, :]), v3(s512a[0:8, :]), bm[0:8, 4:8].un(2).bc([8, 4, 128]), ALU.mult)
            stt(s512b[0:8, :], s512c[0:8, :], neglam[0:8, 0:1], s512b[0:8, :], ALU.mult, ALU.add)
            ps2 = rot()
            mm(ps2[sl, :], onesf[0:8, 0:1], s512b[0:8, :], True, True)
            cp(aout[sl, 0:512], ps2[sl, :])

        def gla_chunk(t, sl):
            n = sl.stop - sl.start; c0 = t * 128 + sl.start
            ps = rot()
            mm(ps[sl, 0:256], bgT[0:17, c0:c0 + n], W2aug[0:17, :], True, True)
            act(s512a[sl, 0:256], ps[sl, 0:256], AF.Exp, scale=-1.0)
            act(s512a[sl, 0:256], s512a[sl, 0:256], AF.Ln, bias=one_t[sl, 0:1])
            ts(s512a[sl, 0:256], s512a[sl, 0:256], -1.0 / 16.0, ALU.mult)
            ps = rot()
            mm(ps[sl, 0:256], tri[sl, sl], s512a[sl, 0:256], True, True)
            mm(ps[sl, 256:512], trirev[sl, sl], s512a[sl, 0:256], True, True)
            psl = rot()
            for j in range(2):
                mm(psl[:, j:j + 1], s512a[sl, j * 128:(j + 1) * 128], onesf[sl, 0:1], True, True)
            act(sm3[:, 40:42], psl[:, 0:2], AF.Exp)
            act(s512b[sl, 0:256], ps[sl, 0:256], AF.Exp)
            act(s512b[sl, 256:512], ps[sl, 0:256], AF.Exp, scale=-1.0)
            act(s512c[sl, 0:256], ps[sl, 256:512], AF.Exp)
            qk = qkB(t)
            stt(b512a[sl, 0:256], qk[sl, 0:256], 0.125, s512b[sl, 0:256], ALU.mult, ALU.mult)
            tt_(b512a[sl, 256:512], qk[sl, 256:512], s512b[sl, 256:512], ALU.mult)
            tt_(b512b[sl, 0:256], qk[sl, 256:512], s512c[sl, 0:256], ALU.mult)
            pt_ = rot(); ptb_ = pt_[:, :].bitcast(BF16)
            for q in range(4):
                tr(ptb_[:, q * 128:q * 128 + n], b512a[sl, q * 128:(q + 1) * 128], identb[sl, sl])
            for q in range(4):
                cp(b512c[:, q * 128:q * 128 + n], ptb_[:, q * 128:q * 128 + n], e="act")
            for h in range(4):
                j = h // 2; hp = slice(64 * (h % 2), 64 * (h % 2) + 64)
                pa = rot()
                mm(pa[sl, 0:n], b512c[hp, (2 + j) * 128:(2 + j) * 128 + n], b512c[hp, j * 128:j * 128 + n], True, True)
                tt_(b512b[sl, 256:256 + n], pa[sl, 0:n], tri[sl, sl], ALU.mult)
                po = rot(); po2 = rot()
                mm(po[sl, 0:128], b512b[sl, 256:256 + n], vB(t)[sl, h * 128:(h + 1) * 128], True, True)
                mm(po2[sl, 0:128], b512c[hp, j * 128:j * 128 + n], Sgb[hp, j, :], True, True)
                cp(aout[sl, 512 + h * 128:512 + (h + 1) * 128], po[sl, 0:128], e="act")
                tt_(aout[sl, 512 + h * 128:512 + (h + 1) * 128], aout[sl, 512 + h * 128:512 + (h + 1) * 128], po2[sl, 0:128], ALU.add)
            for h in range(4):
                j = h // 2; hp = slice(64 * (h % 2), 64 * (h % 2) + 64)
                pS = rot()
                mm(pS[:, 0:128], b512b[sl, j * 128:(j + 1) * 128], vB(t)[sl, h * 128:(h + 1) * 128], True, True)
                stt(Sg[hp, j, :], Sg[hp, j, :], sm3[hp, 40 + j:41 + j], pS[hp, 0:128], ALU.mult, ALU.add)
            cp(Sgb[:, :, :], Sg[:, :, :], e="act")

        def qT_o(i, c0, n): return mixA[:, i * NMAX + c0: i * NMAX + c0 + n]
        def kT_o(i, c0, n): return mixA[:, 4096 + i * NMAX + c0: 4096 + i * NMAX + c0 + n]
        def ktok(t): return mixB[:, t * 1024:(t + 1) * 1024]
        def vaug(t): return mixB[:, 4096 + t * 1032: 4096 + (t + 1) * 1032].re("p (h v) -> p h v", h=4)
        def so(t): return mixC[:, t * 1024:(t + 1) * 1024]

        def odd_mixer(nt, sample):
            N = nt * 128
            norm_T(nt, 4)
            for t in range(nt):
                memset(vaug(t)[:, :, 256:258], 1.0)
            def c_q(blk):
                def c(t, ps):
                    cp(b512a[:, :], ps[:, :], e="act")
                    pt_ = rot(); ptb_ = pt_[:, :].bitcast(BF16)
                    for q in range(4):
                        tr(ptb_[:, q * 128:(q + 1) * 128], b512a[:, q * 128:(q + 1) * 128], identb[:, :])
                    for q in range(4):
                        cp(qT_o(blk * 4 + q, t * 128, 128), ptb_[:, q * 128:(q + 1) * 128])
                return c
            def c_k(blk):
                def c(t, ps):
                    act(ktok(t)[:, blk * 512:(blk + 1) * 512], ps[:, :], AF.Copy, scale=1.0 / 16.0)
                    pt_ = rot(); ptb_ = pt_[:, :].bitcast(BF16)
                    for q in range(4):
                        tr(ptb_[:, q * 128:(q + 1) * 128], ktok(t)[:, blk * 512 + q * 128: blk * 512 + (q + 1) * 128], identb[:, :])
                    for q in range(4):
                        cp(kT_o(blk * 4 + q, t * 128, 128), ptb_[:, q * 128:(q + 1) * 128])
                return c
            def c_v(blk):
                def c(t, ps):
                    cp(vaug(t)[:, 2 * blk:2 * blk + 2, 0:256], ps[:, :].re("p (h v) -> p h v", h=2))
                return c
            def c_o(blk):
                def c(t, ps):
                    act(so(t)[:, blk * 512:(blk + 1) * 512], ps[:, :], AF.Sigmoid)
                return c
            def c_g(t, ps):
                tt_(gates[:, t, :], ps[:, 0:8], cgb_bc[:, :], ALU.add)
            for blk in range(2): proj_blocks(OWinB, nt, blk * 512, 512, c_q(blk))
            for blk in range(2): proj_blocks(OWinB, nt, 1024 + blk * 512, 512, c_k(blk))
            for blk in range(2): proj_blocks(OWinB, nt, 2048 + blk * 512, 512, c_v(blk))
            for blk in range(2): proj_blocks(OWinB, nt, 3072 + blk * 512, 512, c_o(blk))
            proj_blocks(OWinB, nt, 4096, 8, c_g)
            for t in range(nt):
                if sample:
                    for r2 in range(2):
                        r = 2 * t + r2; sl = slice(64 * r2, 64 * r2 + 1)
                        load_C(r)
                        mlstm_chunk(t, sl)
                        store_C(Cs[r], ns[r], ms[r:r + 1, :])
                else:
                    mlstm_chunk(t, slice(0, 128))
                headnorm(aout[:, 0:1024], 4, 256, chn_bc, slice(0, 128))
                tt_(aout[:, :], aout[:, :], so(t), ALU.mult)
                cp(tokb[:, :], aout[:, :], e="act")
                tok_to_actT(t)
            out_proj(OWoutB, nt)

        def load_C(r):
            for h in range(4):
                for vc in range(2):
                    p.dma("sp", s512a[:, 0:256], sC[r, h, vc * 128:(vc + 1) * 128, :])
                    ps = rot()
                    for dc in range(2):
                        tr(ps[:, dc * 128:(dc + 1) * 128], s512a[:, dc * 128:(dc + 1) * 128], identf[:, :])
                    for dc in range(2):
                        cp(CT[:, h, dc, vc * 128:(vc + 1) * 128], ps[:, dc * 128:(dc + 1) * 128])
                for dc in range(2):
                    p.dma("sp", CT[:, h, dc, 256:257], sn[r, h:h + 1, dc * 128:(dc + 1) * 128].rearrange("o d -> d o"))
            p.dma("sp", mprev[:], sm[r:r + 1, :].partition_broadcast(128))
            cp(CTb[:].re("p a b c -> p (a b c)"), CT[:].re("p a b c -> p (a b c)"), e="act")

        def store_C(Cd, nd, md):
            for h in range(4):
                for vc in range(2):
                    ps = rot()
                    for dc in range(2):
                        tr(ps[:, dc * 128:(dc + 1) * 128], CT[:, h, dc, vc * 128:(vc + 1) * 128], identf[:, :])
                    cp(s512b[:, 0:256], ps[:, 0:256])
                    p.dma("sp", Cd[h, vc * 128:(vc + 1) * 128, :], s512b[:, 0:256])
                for dc in range(2):
                    p.dma("sp", nd[h:h + 1, dc * 128:(dc + 1) * 128].rearrange("o d -> d o"), CT[:, h, dc, 256:257])
            p.dma("sp", md, mprev[0:1, :])

        def mlstm_chunk(t, sl):
            n = sl.stop - sl.start; c0 = t * 128 + sl.start
            g = gates
            ipre = g[sl, t, 0:4]
            act(sm1[sl, 16:20], g[sl, t, 4:8], AF.Exp, scale=-1.0)
            act(sm1[sl, 16:20], sm1[sl, 16:20], AF.Ln, bias=one_t[sl, 0:1])
            ts(sm1[sl, 16:20], sm1[sl, 16:20], -1.0, ALU.mult)
            ps = rot()
            mm(ps[sl, 0:4], tri[sl, sl], sm1[sl, 16:20], True, True)
            mm(ps[sl, 4:8], onesf[sl, sl], sm1[sl, 16:20], True, True)
            cp(sm1[sl, 20:28], ps[sl, 0:8])
            tt_(sm1[sl, 28:32], ipre, sm1[sl, 20:24], ALU.subtract)
            for h in range(4):
                ts(s512a[sl, h * 128:h * 128 + n], identf[sl, sl], sm1[sl, 28 + h:29 + h], ALU.mult)
            pa = rot()
            mm(pa[sl, :].re("p (h s) -> p h s", h=4)[:, :, 0:n], onesf[sl, sl], s512a[sl, :].re("p (h s) -> p h s", h=4)[:, :, 0:n], True, True)
            red(sm1[sl, 32:36], pa[sl, :].re("p (h s) -> p h s", h=4)[:, :, 0:n], op=ALU.max)
            tt_(sm1[sl, 32:36], sm1[sl, 32:36], mprev[sl, :], ALU.max)
            tt_(s512b[sl, :].re("p (h s) -> p h s", h=4)[:, :, 0:n], pa[sl, :].re("p (h s) -> p h s", h=4)[:, :, 0:n],
                negts[sl, sl].un(1).bc([n, 4, n]), ALU.add)
            red(sm1[sl, 36:40], s512b[sl, :].re("p (h s) -> p h s", h=4)[:, :, 0:n], op=ALU.max)
            tt_(sm1[sl, 36:40], sm1[sl, 36:40], mprev[sl, :], ALU.max)
            tt_(sm1[sl, 40:44], sm1[sl, 20:24], sm1[sl, 36:40], ALU.add)
            act(sm1[sl, 44:48], sm1[sl, 40:44], AF.Exp, scale=-1.0)
            tt_(sm1[sl, 48:52], mprev[sl, :], sm1[sl, 36:40], ALU.subtract)
            act(sm1[sl, 48:52], sm1[sl, 48:52], AF.Exp)
            for h in range(4):
                ts(s512a[sl, h * 128:h * 128 + n], identf[sl, sl], sm1[sl, 36 + h:37 + h], ALU.mult)
            pm = rot()
            mv = pm[sl, :].re("p (h s) -> p h s", h=4)[:, :, 0:n]
            for h in range(4):
                mm(pm[sl, h * 128:h * 128 + n], onesf[sl, sl], s512a[sl, h * 128:h * 128 + n], True, False)
                mm(pm[sl, h * 128:h * 128 + n], identf[sl, sl], pos4[sl, sl.start:sl.start + n], False, True)
            for h in range(4):
                act(s512c[sl, h * 128:h * 128 + n], pm[sl, h * 128:h * 128 + n], AF.Exp, bias=sm1[sl, 28 + h:29 + h], scale=-1.0)
            tt_(sm1[sl, 52:56], sm1[sl, 28:32], sm1[sl, 32:36], ALU.subtract)
            act(sm1[sl, 52:56], sm1[sl, 52:56], AF.Exp)
            tt_(sm1[sl, 56:60], mprev[sl, :], sm1[sl, 32:36], ALU.subtract)
            act(sm1[sl, 56:60], sm1[sl, 56:60], AF.Exp)
            pb_ = rot()
            for h in range(4):
                ts(s512a[sl, h * 128:h * 128 + n], identf[sl, sl], sm1[sl, 56 + h:57 + h], ALU.mult)
            mm(pb_[:, 0:4], onesf[sl, 0:128], s512a[sl, :].re("p (h s) -> p h s", h=4)[:, :, 0:1], True, True)
            cp(sm2[:, 16:20], pb_[:, 0:4])
            tt_(sm1[sl, 60:64], sm1[sl, 24:28], sm1[sl, 32:36], ALU.add)
            for h in range(4):
                ts(s512a[sl, h * 128:h * 128 + n], identf[sl, sl], sm1[sl, 60 + h:61 + h], ALU.mult)
            pb2 = rot()
            mm(pb2[:, 0:4], onesf[sl, 0:128], s512a[sl, :].re("p (h s) -> p h s", h=4)[:, :, 0:1], True, True)
            cp(sm2[:, 20:24], pb2[:, 0:4])
            for h in range(4):
                pk = rot()
                for dc in range(2):
                    mm(pk[sl, 0:n], kT_o(h * 2 + dc, c0, n), qT_o(h * 2 + dc, c0, n), dc == 0, dc == 1)
                tt_(b512b[sl, 0:n], pk[sl, 0:n], s512c[sl, h * 128:h * 128 + n], ALU.mult)
                pn1 = PS[4 + (h % 2) * 2]; pn2 = PS[5 + (h % 2) * 2]
                mm(pn1[sl, 0:257], b512b[sl, 0:n], vaug(t)[sl, h, 0:257], True, True)
                for dc in range(2):
                    mm(pn2[sl, 0:257], qT_o(h * 2 + dc, c0, n), CTb[:, h, dc, 0:257], dc == 0, dc == 1)
                ts(s512b[sl, 0:257], pn2[sl, 0:257], sm1[sl, 48 + h:49 + h], ALU.mult)
                tt_(s512b[sl, 0:257], s512b[sl, 0:257], pn1[sl, 0:257], ALU.add)
                ts(sm2[sl, 26:27], s512b[sl, 256:257], -1.0, ALU.mult)
                tt_(sm2[sl, 24:25], s512b[sl, 256:257], sm2[sl, 26:27], ALU.max)
                tt_(sm2[sl, 24:25], sm2[sl, 24:25], sm1[sl, 44 + h:45 + h], ALU.max)
                recip(sm2[sl, 25:26], sm2[sl, 24:25])
                ts(aout[sl, h * 256:(h + 1) * 256], s512b[sl, 0:256], sm2[sl, 25:26], ALU.mult)
            for h in range(4):
                ts(b512a[sl, 0:256], ktok(t)[sl, h * 256:(h + 1) * 256], sm1[sl, 52 + h:53 + h], ALU.mult)
                for dc in range(2):
                    pc = rot()
                    mm(pc[:, 0:257], b512a[sl, dc * 128:(dc + 1) * 128], vaug(t)[sl, h, 0:257], True, True)
                    stt(CT[:, h, dc, 0:257], CT[:, h, dc, 0:257], sm2[:, 16 + h:17 + h], pc[:, 0:257], ALU.mult, ALU.add)
            cp(CTb[:].re("p a b c -> p (a b c)"), CT[:].re("p a b c -> p (a b c)"), e="act")
            cp(mprev[:, :], sm2[:, 20:24])

        def convert_all():
            convW(WgB[0], Wg[0], D); convW(WuB[0], Wu[0], D); convW(WdB[0], Wd[0], DFF)
            convW(EWinB, EWin, D); convW(EWoutB, EWout, D)
            convW(WgB[1], Wg[1], D); convW(WuB[1], Wu[1], D); convW(WdB[1], Wd[1], DFF)
            convW(WgB[2], Wg[2], D); convW(WuB[2], Wu[2], D); convW(WdB[2], Wd[2], DFF)
            convW(OWinB, OWin, D); convW(OWoutB, OWout, D)
            convW(WgB[3], Wg[3], D); convW(WuB[3], Wu[3], D); convW(WdB[3], Wd[3], DFF)
        def main_prog():
            convert_all()
            memset(Sg[:, :, :], 0.0); memset(Sgb[:, :, :], 0.0)
            memset(CT[:].re("p a b c -> p (a b c)"), 0.0); memset(CTb[:].re("p a b c -> p (a b c)"), 0.0)
            memset(mprev[:, :], 0.0)
            if KSTOP == 1: return
            for s_i in range(NST):
                g0 = s_i * STT
                for t in range(STT):
                    p.dma("sp", X[:, t, :], xp[(g0 + t) * 128:(g0 + t + 1) * 128, :])
                ffn(0, 0, STT)
                if KSTOP == 2: return
                even_mixer(STT, g0, False)
                if KSTOP == 3: return
                ffn(1, 2, STT)
                ffn(2, 3, STT)
                odd_mixer(STT, False)
                if KSTOP == 4: return
                ffn(3, 5, STT)
                for t in range(STT):
                    p.dma("sp", yp[(g0 + t) * 128:(g0 + t + 1) * 128, :], X[:, t, :])
            p.dma("sp", glap.rearrange("(j hh) d v -> (hh d) j v", hh=2), Sg[:])
            store_C(Cp, np_, mp[0:1, :])
            if KSTOP == 5: return
            for i_, nm in enumerate(("K0", "V0", "K1", "V1", "qbc")):
                SB[nm] = KTbuf.sub((slice(None), slice(i_ * 1024, (i_ + 1) * 1024)), nm)
            SB["vb0"] = KTbuf.sub((slice(None), slice(5120, 5632)), "vb0"); SB["vb1"] = KTbuf.sub((slice(None), slice(5632, 6144)), "vb1")
            SB["vn"] = KTbuf.sub((slice(None), slice(6144, 7168)), "vn")
            SB["abs"] = Vbuf.sub((slice(None), slice(0, 2 * NPG * 8)), "abs")
            SB["idx"] = Vbuf.sub((slice(None), slice(2 * NPG * 8, 2 * NPG * 8 + 8 * NPG)), "idx")
            p.dma("sp", SB["abs"][:, :].bitcast(F32), c_abs[:, :])
            ptb = s512b[:, 0:4 * NPG].bitcast(I32)
            p.dma("sp", ptb, ptab[0:1, :].partition_broadcast(128))
            cp(s512c[:, 0:4 * NPG], ptb)
            ts(SB["idx"][:, :].bitcast(I32), s512c[:, 0:4 * NPG], 128.0, ALU.mult, iota[:, 0:1], ALU.add)
            memset(X[:, 0:2, :], 0.0)
            for r in range(4):
                p.dma("sp", X[64 * (r % 2):64 * (r % 2) + 1, r // 2, :], xs[r:r + 1, :])
            ffn(0, 0, 2)
            if KSTOP == 6: return
            even_mixer(2, 0, True)
            if KSTOP == 7: return
            ffn(1, 2, 2)
            ffn(2, 3, 2)
            odd_mixer(2, True)
            ffn(3, 5, 2)
            for r in range(4):
                p.dma("sp", ys[r:r + 1, :], X[64 * (r % 2):64 * (r % 2) + 1, r // 2, :])

        main_prog()
        p.finish()
    return nc


def make_consts(TSEQ, NPG):
    NTT = TSEQ // 128; PAST = NPG * 128
    ar = np.arange(128)
    c = {}
    c["c_identf"] = np.eye(128, dtype=np.float32)
    c["c_tri"] = (ar[:, None] <= ar[None, :]).astype(np.float32)
    c["c_trirev"] = (ar[:, None] > ar[None, :]).astype(np.float32)
    c["c_negts"] = np.where(ar[None, :] > ar[:, None], -1e30, 0.0).astype(np.float32)
    pos = np.where(ar[:, None] > ar[None, :], 1e30, 0.0).astype(np.float32)
    c["c_pos4"] = pos
    nb = np.where(ar[:, None] > ar[None, :], -30000.0, 0.0).astype(np.float32)
    c["c_negbf"] = np.concatenate([nb, nb], axis=1)
    slopes = np.array([2.0 ** (-8.0 * (h + 1) / 4) for h in range(4)], np.float32)
    abp = np.zeros((128, 4, NTT), np.float32)
    for h in range(4):
        for dd in range(NTT):
            abp[:, h, dd] = slopes[h] * (-128.0 * dd + ar - 127.0)
    c["c_abp"] = abp.reshape(128, 4 * NTT)
    ab = np.zeros((128, NPG, 8), np.float32)
    for j in range(NPG):
        for h in range(4):
            ab[:, j, 2 * h] = ab[:, j, 2 * h + 1] = slopes[h] * (128.0 * j + ar - PAST)
    c["c_abs"] = ab.reshape(128, NPG * 8)
    c["c_iota"] = ar.astype(np.float32).reshape(128, 1)
    bmk = np.zeros((8, 8), np.float32)
    for h in range(4):
        bmk[2 * h, h] = 1.0
        bmk[2 * h + 1, 4 + h] = 1.0
    c["c_bm"] = bmk
    return c


def run(inputs, TSEQ, NPG, NPOOL, n_cores=8):
    f = lambda a: np.ascontiguousarray(np.asarray(a))
    nc = build(TSEQ, NPG, NPOOL)
    consts = make_consts(TSEQ, NPG)
    B = inputs["x_prompt"].shape[0]
    shared = {
        "ck": f(inputs["cache_k"]).reshape(NPOOL * 128, 512), "cv": f(inputs["cache_v"]).reshape(NPOOL * 128, 512),
        "normg": f(inputs["norm_g"]).reshape(6, D),
        "wg": f(inputs["ffn_w_gate"]).reshape(4, D, DFF), "wu": f(inputs["ffn_w_up"]).reshape(4, D, DFF),
        "wd": f(inputs["ffn_w_down"]).reshape(4, DFF, D),
        "ewin": f(inputs["even_w_in"])[0], "ewout": f(inputs["even_w_out"])[0],
        "aqk": f(inputs["a_qk_norm"]).reshape(1, 128), "alam": f(inputs["a_lambda"]).reshape(1, 256),
        "ahn": f(inputs["a_head_norm"]).reshape(1, 512), "bw2": f(inputs["b_gate_w2"])[0],
        "bgb": f(inputs["b_gate_bias"]).reshape(1, 256), "bhn": f(inputs["b_head_norm"]).reshape(1, 512),
        "owin": f(inputs["odd_w_in"])[0], "owout": f(inputs["odd_w_out"])[0],
        "cgb": f(inputs["c_gate_bias"]).reshape(1, 8), "chn": f(inputs["c_head_norm"]).reshape(1, 1024),
    }
    shared.update(consts)
    shared["c_negbf"] = consts["c_negbf"]
    in_maps = []
    for c in range(n_cores):
        b = c % B; s0 = 4 * c
        m = dict(shared)
        m["xp"] = f(inputs["x_prompt"][b]); m["xs"] = f(inputs["x_sample"][s0:s0 + 4, 0])
        m["sgla"] = f(inputs["state_gla"][0, s0:s0 + 4]); m["sC"] = f(inputs["state_mlstm_C"][0, s0:s0 + 4])
        m["sn"] = f(inputs["state_mlstm_n"][0, s0:s0 + 4]); m["sm"] = f(inputs["state_mlstm_m"][0, s0:s0 + 4])
        m["ptab"] = f(inputs["page_table"][s0:s0 + 4]).reshape(1, 4 * NPG).astype(np.int32)
        in_maps.append(m)
    res = run_bass_kernel_spmd(nc, in_maps, core_ids=list(range(n_cores)))
    R = res.results
    cat = lambda k, cores: np.stack([R[c][k] for c in cores])
    pc = list(range(min(B, n_cores))); ac = list(range(n_cores)); B = len(pc)
    y_prompt = cat("yp", pc)
    y_sample = np.concatenate([R[c]["ys"] for c in ac])[:, None, :]
    k_prompt = cat("kp", pc).reshape(1, B, TSEQ, 4, 2, 64)
    v_prompt = cat("vp", pc).reshape(1, B, TSEQ, 4, 128)
    k_sample = np.concatenate([R[c]["ks"] for c in ac]).reshape(1, 4 * n_cores, 1, 4, 2, 64)
    v_sample = np.concatenate([R[c]["vs"] for c in ac]).reshape(1, 4 * n_cores, 1, 4, 128)
    gla_prompt = cat("glap", pc)[None]
    gla_sample = np.concatenate([R[c]["glas"] for c in ac])[None]
    C_prompt = cat("Cp", pc)[None]; n_prompt = cat("np", pc)[None]; m_prompt = cat("mp", pc).reshape(1, B, 4)
    C_sample = np.concatenate([R[c]["Cs"] for c in ac])[None]
    n_sample = np.concatenate([R[c]["ns"] for c in ac])[None]
    m_sample = np.concatenate([R[c]["ms"] for c in ac])[None]
    outs = (y_prompt, y_sample, k_prompt, v_prompt, k_sample, v_sample, gla_prompt, gla_sample,
            C_prompt, n_prompt, m_prompt, C_sample, n_sample, m_sample)
    return tuple(np.ascontiguousarray(o, dtype=np.float32) for o in outs)


def kernel(**inputs):
    TSEQ = inputs["x_prompt"].shape[1]
    NPG = inputs["page_table"].shape[1]
    NPOOL = inputs["cache_k"].shape[1]
    return run(inputs, TSEQ, NPG, NPOOL)
```

```python
import contextlib, math, os
KSTOP = int(os.environ.get('KSTOP', '0'))
KSUB = int(os.environ.get('KSUB', '0'))
import numpy as np
import ml_dtypes
import concourse.bass as bass
import concourse.mybir as mybir
from concourse.bass_utils import run_bass_kernel_spmd

F32 = mybir.dt.float32; BF16 = mybir.dt.bfloat16; I32 = mybir.dt.int32
AF = mybir.ActivationFunctionType; ALU = mybir.AluOpType; AX = mybir.AxisListType
EPS = 1e-6
D = 1024; DFF = 2816; PE_ = 3088; PO_ = 4104


class V:
    def __init__(s, t, ap): s.t = t; s.ap = ap
    def __getitem__(s, k): return V(s.t, s.ap[k])
    def re(s, pat, **kw): return V(s.t, s.ap.rearrange(pat, **kw))
    def bitcast(s, dt): return V(s.t, s.ap.bitcast(dt))
    def bc(s, shape): return V(s.t, s.ap.to_broadcast(list(shape)))
    def un(s, ax): return V(s.t, s.ap.unsqueeze(ax))


class T:
    def __init__(s, h, name): s.h = h; s.name = name; s.w = None; s.r = []; s.psum = False
    def __getitem__(s, k): return V(s, s.h[k])
    def sub(s, k, name):
        n = T(s.h[k], name); n.w = s.w; n.r = list(s.r)
        return n


class P:
    def __init__(self, nc, es, n_dma_sems=32):
        self.nc = nc; self.es = es
        self.eng = {"pe": nc.tensor, "act": nc.scalar, "dve": nc.vector, "pool": nc.gpsimd, "sp": nc.sync}
        self.sem = {e: es.enter_context(nc.semaphore("s_" + e)) for e in self.eng}
        self.cnt = {e: 0 for e in self.eng}
        self.seen = {e: {} for e in self.eng}
        self.dsems = [es.enter_context(nc.semaphore("d%d" % i)) for i in range(n_dma_sems)]
        self.dval = [0] * n_dma_sems
        self.dnext = {"sp": 0, "pool": 0}
        self.drange = {"sp": (0, n_dma_sems // 2), "pool": (n_dma_sems // 2, n_dma_sems)}
    def sb(self, name, shape, dt=F32):
        return T(self.es.enter_context(self.nc.sbuf_tensor(name, list(shape), dt)), name)
    def psum(self, name, shape, dt=F32):
        t = T(self.es.enter_context(self.nc.psum_tensor(name, list(shape), dt)), name)
        t.psum = True
        return t
    def _wait(self, e, d):
        key, sem, val, deng = d
        if deng == e and e == "pe":
            return
        if self.seen[e].get(key, 0) >= val:
            return
        self.eng[e].wait_ge(sem, val)
        self.seen[e][key] = val
    def _deps(self, e, reads, writes):
        deps = []
        for t in reads:
            if t.w is not None: deps.append(t.w)
            if t.psum:
                deps.extend(d for d in t.r if d[3] != e)
        for t in writes:
            if t.w is not None: deps.append(t.w)
            deps.extend(t.r)
        for d in deps:
            self._wait(e, d)
    def _mark(self, me, reads, writes):
        for t in reads:
            t.r.append(me)
            if len(t.r) > 64:
                t.r = self._prune(t.r)
        for t in writes:
            t.w = me; t.r = []
    @staticmethod
    def _prune(r):
        best = {}
        for d in r:
            if d[0] not in best or best[d[0]][2] < d[2]:
                best[d[0]] = d
        return list(best.values())
    def I(self, e, name, outs, ins, **kw):
        args = {}
        reads = []; writes = []
        for k, v in outs.items():
            args[k] = v.ap; writes.append(v.t)
        for k, v in ins.items():
            if isinstance(v, V):
                args[k] = v.ap; reads.append(v.t)
            else:
                args[k] = v
        args.update(kw)
        self._deps(e, reads, writes)
        ins_ = getattr(self.eng[e], name)(**args)
        self.cnt[e] += 1
        ins_.then_inc(self.sem[e], 1)
        self._mark((e, self.sem[e], self.cnt[e], e), reads, writes)
        return ins_
    def dma(self, e, out, in_, indirect=None):
        reads = [in_.t] if isinstance(in_, V) else []
        writes = [out.t] if isinstance(out, V) else []
        if indirect is not None:
            reads.append(indirect.t)
        self._deps(e, reads, writes)
        lo, hi = self.drange[e]
        i = lo + self.dnext[e]; self.dnext[e] = (self.dnext[e] + 1) % (hi - lo)
        sem = self.dsems[i]
        if self.dval[i] > 0:
            self._wait(e, ("d%d" % i, sem, self.dval[i], "dma"))
        self.dval[i] += 16
        o = out.ap if isinstance(out, V) else out
        s = in_.ap if isinstance(in_, V) else in_
        if indirect is not None:
            ins_ = self.eng[e].indirect_dma_start(out=o, out_offset=None, in_=s,
                                                  in_offset=bass.IndirectOffsetOnAxis(ap=indirect.ap, axis=0))
        else:
            ins_ = self.eng[e].dma_start(out=o, in_=s)
        ins_.then_inc(sem, 16)
        self._mark(("d%d" % i, sem, self.dval[i], "dma"), reads, writes)
        return ins_
    def finish(self):
        for i, sem in enumerate(self.dsems):
            if self.dval[i] > 0:
                self._wait("sp", ("d%d" % i, sem, self.dval[i], "dma"))
        for e in self.eng:
            if e != "sp" and self.cnt[e] > 0:
                self._wait("sp", (e, self.sem[e], self.cnt[e], e))


def build(TSEQ, NPG, NPOOL):
    NTT = TSEQ // 128
    STT = min(4, NTT)
    NST = NTT // STT
    NMAX = STT * 128
    PAST = NPG * 128
    nc = bass.Bass("TRN2", target_bir_lowering=False)

    def din(name, shape, dt=F32):
        return nc.dram_tensor(name, list(shape), dt, kind="ExternalInput").ap()
    def dout(name, shape):
        return nc.dram_tensor(name, list(shape), F32, kind="ExternalOutput").ap()

    xp = din("xp", [TSEQ, D]); xs = din("xs", [4, D])
    ck = din("ck", [NPOOL * 128, 512]); cv = din("cv", [NPOOL * 128, 512])
    sgla = din("sgla", [4, 4, 64, 128]); sC = din("sC", [4, 4, 256, 256]); sn = din("sn", [4, 4, 256]); sm = din("sm", [4, 4])
    ptab = din("ptab", [1, 4 * NPG], I32)
    normg = din("normg", [6, D])
    Wg = din("wg", [4, D, DFF]); Wu = din("wu", [4, D, DFF]); Wd = din("wd", [4, DFF, D])
    EWin = din("ewin", [D, PE_]); EWout = din("ewout", [D, D])
    aqk = din("aqk", [1, 128]); alam = din("alam", [1, 256]); ahn = din("ahn", [1, 512])
    bw2 = din("bw2", [16, 256]); bgb = din("bgb", [1, 256]); bhn = din("bhn", [1, 512])
    OWin = din("owin", [D, PO_]); OWout = din("owout", [D, D])
    cgb = din("cgb", [1, 8]); chn = din("chn", [1, 1024])
    c_identf = din("c_identf", [128, 128]); c_tri = din("c_tri", [128, 128]); c_trirev = din("c_trirev", [128, 128])
    c_negts = din("c_negts", [128, 128]); c_pos4 = din("c_pos4", [128, 128]); c_negbf = din("c_negbf", [128, 256])
    c_abp = din("c_abp", [128, 4 * NTT]); c_abs = din("c_abs", [128, NPG * 8]); c_iota = din("c_iota", [128, 1])
    c_bm = din("c_bm", [8, 8])

    yp = dout("yp", [TSEQ, D]); ys = dout("ys", [4, D])
    kp = dout("kp", [TSEQ, 512]); vp = dout("vp", [TSEQ, 512]); ks = dout("ks", [4, 512]); vs = dout("vs", [4, 512])
    glap = dout("glap", [4, 64, 128]); glas = dout("glas", [4, 4, 64, 128])
    Cp = dout("Cp", [4, 256, 256]); np_ = dout("np", [4, 256]); mp = dout("mp", [1, 4])
    Cs = dout("Cs", [4, 4, 256, 256]); ns = dout("ns", [4, 4, 256]); ms = dout("ms", [4, 4])

    es = contextlib.ExitStack()
    with es:
        p = P(nc, es)
        X = p.sb("X", [128, STT, D])
        tokb = p.sb("tokb", [128, D], BF16)
        junk = p.sb("junk", [128, D], BF16)
        tokb2 = p.sb("tokb2", [128, D], BF16)
        nsm = [p.sb("nsm%d" % i, [128, 2]) for i in range(2)]
        actT = p.sb("actT", [128, 8, NMAX], BF16)
        NW = 6
        Wb = [p.sb("Wb%d" % i, [128, 8, 256], BF16) for i in range(NW)]
        def dscr(name, shape):
            return T(nc.dram_tensor(name, list(shape), BF16, kind="Internal").ap(), name)
        WgB = [dscr("wgb%d" % f, [D, DFF]) for f in range(4)]
        WuB = [dscr("wub%d" % f, [D, DFF]) for f in range(4)]
        WdB = [dscr("wdb%d" % f, [DFF, D]) for f in range(4)]
        EWinB = dscr("ewinb", [D, PE_]); EWoutB = dscr("ewoutb", [D, D])
        OWinB = dscr("owinb", [D, PO_]); OWoutB = dscr("owoutb", [D, D])
        KTbuf = p.sb("KT", [128, max(4 * TSEQ, 7168)], BF16)
        KT = KTbuf[:, 0:4 * TSEQ].re("p (h n) -> p h n", h=4)
        VSZ = max(NTT * 520, 2 * NPG * 8 + 2 * 4 * NPG + 16)
        Vbuf = p.sb("Vst", [128, VSZ], BF16)
        Vst = Vbuf[:, 0:NTT * 520].re("p (a b c) -> p a b c", a=NTT, b=4)
        gbc = p.sb("gbc", [128, D])
        mixA = p.sb("mixA", [128, 8192], BF16)
        mixB = p.sb("mixB", [128, 8256], BF16)
        mixC = p.sb("mixC", [128, 4096], BF16)
        Sg = p.sb("Sg", [128, 2, 128]); Sgb = p.sb("Sgb", [128, 2, 128], BF16)
        CT = p.sb("CT", [128, 4, 2, 258]); CTb = p.sb("CTb", [128, 4, 2, 258], BF16)
        mprev = p.sb("mprev", [128, 4])
        identf = p.sb("identf", [128, 128]); identb = p.sb("identb", [128, 128], BF16)
        tri = p.sb("tri", [128, 128]); trirev = p.sb("trirev", [128, 128]); negts = p.sb("negts", [128, 128])
        pos4 = p.sb("pos4", [128, 128]); negbf = p.sb("negbf", [128, 256], BF16)
        onesf = p.sb("onesf", [128, 128]); onesb = p.sb("onesb", [128, 128], BF16)
        abp = p.sb("abp", [128, 4 * NTT]); iota = p.sb("iota", [128, 1])
        bm = p.sb("bm", [8, 8])
        aqk_bc = p.sb("aqk_bc", [128, 128]); ahn_bc = p.sb("ahn_bc", [128, 512]); bhn_bc = p.sb("bhn_bc", [128, 512])
        chn_bc = p.sb("chn_bc", [128, 1024]); cgb_bc = p.sb("cgb_bc", [128, 8]); alam_bc = p.sb("alam_bc", [128, 256])
        W2aug = p.sb("W2aug", [32, 256])
        neglam = p.sb("neglam", [128, 1])
        sm1 = p.sb("sm1", [128, 64]); sm2 = p.sb("sm2", [128, 64]); sm3 = p.sb("sm3", [128, 64])
        s512a = p.sb("s512a", [128, 512]); s512b = p.sb("s512b", [128, 512]); s512c = p.sb("s512c", [128, 512])
        b512a = p.sb("b512a", [128, 512], BF16); b512b = p.sb("b512b", [128, 512], BF16); b512c = p.sb("b512c", [128, 512], BF16)
        sg = [s512a, s512b]
        bgT = p.sb("bgT", [32, NMAX])
        gates = p.sb("gates", [128, STT, 8])
        aout = p.sb("aout", [128, 1024])
        PS = [p.psum("ps%d" % i, [128, 512]) for i in range(8)]
        st = {"rot": 0, "w": 0, "sg": 0}
        def rot():
            b = PS[st["rot"]]; st["rot"] = (st["rot"] + 1) % 4
            return b
        def nextW():
            w = Wb[st["w"]]; st["w"] = (st["w"] + 1) % NW
            return w

        def mm(out, lhsT, rhs, start, stop):
            p.I("pe", "matmul", {"out": out}, {"lhsT": lhsT, "rhs": rhs}, start=start, stop=stop)
        def tr(out, in_, ident):
            p.I("pe", "transpose", {"out": out}, {"in_": in_, "identity": ident})
        def act(out, in_, func, bias=None, scale=None, accum=None):
            outs = {"out": out}; ins = {"in_": in_}
            kw = {"func": func}
            if accum is not None: outs["accum_out"] = accum
            if bias is not None: ins["bias"] = bias
            if scale is not None: ins["scale"] = scale
            p.I("act", "activation", outs, ins, **kw)
        def tt_(out, in0, in1, op, e="dve"):
            p.I(e, "tensor_tensor", {"out": out}, {"in0": in0, "in1": in1}, op=op)
        def ts(out, in0, s1, op0, s2=None, op1=None, e="dve"):
            kw = {"op0": op0}
            if op1 is not None: kw["op1"] = op1
            p.I(e, "tensor_scalar", {"out": out}, {"in0": in0, "scalar1": s1, "scalar2": s2}, **kw)
        def stt(out, in0, scalar, in1, op0, op1):
            p.I("dve", "scalar_tensor_tensor", {"out": out}, {"in0": in0, "scalar": scalar, "in1": in1}, op0=op0, op1=op1)
        def red(out, in_, op=ALU.add):
            p.I("dve", "tensor_reduce", {"out": out}, {"in_": in_}, axis=AX.X, op=op)
        def recip(out, in_):
            p.I("dve", "reciprocal", {"out": out}, {"in_": in_})
        def cp(out, in_, e="dve"):
            if e == "act":
                act(out, in_, AF.Copy)
            else:
                p.I(e, "tensor_copy", {"out": out}, {"in_": in_})
        def memset(v, val, e="dve"):
            p.I(e, "memset", {"ap": v}, {}, constant=val)
        def loadW(dst, WT, r0, nk, c0, ncol):
            src = WT[r0:r0 + nk * 128, c0:c0 + ncol].re("(k p) f -> p k f", p=128)
            p.dma("sp", dst, src)
        def convW(WT, src2d, nrows):
            for r0 in range(0, nrows, 128):
                p.dma("pool", WT[r0:r0 + 128, :], src2d[r0:r0 + 128, :])
        def rstd_of(out, ss, n, sl):
            act(out, ss, AF.Sqrt, bias=eps_t[sl, 0:1], scale=1.0 / n)
            recip(out, out)

        eps_t = p.sb("eps_t", [128, 1]); one_t = p.sb("one_t", [128, 1])
        memset(eps_t[:, :], EPS); memset(one_t[:, :], 1.0)

        for t_, d_ in ((identf, c_identf), (tri, c_tri), (trirev, c_trirev), (negts, c_negts), (pos4, c_pos4),
                       (abp, c_abp), (iota, c_iota), (bm, c_bm)):
            p.dma("sp", t_[:], d_[:, :] if len(d_.shape) == 2 else d_)
        p.dma("pool", negbf[:], c_negbf[:, :])
        cp(identb[:, :], identf[:, :])
        memset(onesf[:, :], 1.0); memset(onesb[:, :], 1.0)
        for t_, d_, n_ in ((aqk_bc, aqk, 128), (ahn_bc, ahn, 512), (bhn_bc, bhn, 512), (chn_bc, chn, 1024),
                           (cgb_bc, cgb, 8), (alam_bc, alam, 256)):
            p.dma("sp", t_[:], d_[0:1, :].partition_broadcast(128))
        memset(W2aug[:, :], 0.0)
        p.dma("sp", W2aug[0:16, :], bw2[:, :])
        p.dma("sp", W2aug[16:17, :], bgb[0:1, :])
        memset(bgT[:, :], 1.0)
        memset(Vst.re("p a b c -> p (a b) c")[:, :, 128:130], 1.0)
        tt_(s512a[:, 0:64], alam_bc[:, 0:64], alam_bc[:, 64:128], ALU.mult)
        tt_(s512a[:, 64:128], alam_bc[:, 128:192], alam_bc[:, 192:256], ALU.mult)
        red(sm1[:, 0:2], s512a[:, 0:128].re("p (a b) -> p a b", a=2))
        act(sm1[:, 2:4], sm1[:, 0:2], AF.Exp)
        tt_(sm1[:, 4:5], sm1[:, 3:4], sm1[:, 2:3], ALU.subtract)
        ts(neglam[:, :], sm1[:, 4:5], -0.2, ALU.add)

        hT = mixA
        def hT_view(fb, N):
            per = 8192 // NMAX
            reg = (mixA, mixB, mixC)[fb // per]
            o = (fb % per) * NMAX
            return reg[:, o:o + N]

        def norm_T(nt, gi):
            p.dma("sp", gbc[:], normg[gi:gi + 1, :].partition_broadcast(128))
            for t0 in range(0, nt, 2):
                tl = list(range(t0, min(nt, t0 + 2)))
                for t in tl:
                    par = t % 2
                    tb = (tokb, tokb2)[par]
                    act(junk[:, :], X[:, t, :], AF.Square, accum=nsm[par][:, 0:1])
                    rstd_of(nsm[par][:, 1:2], nsm[par][:, 0:1], D, slice(0, 128))
                    stt(tb[:, :], X[:, t, :], nsm[par][:, 1:2], gbc[:, :], ALU.mult, ALU.mult)
                for t in tl:
                    par = t % 2
                    tok_to_actT(t, (tokb, tokb2)[par], "dve" if par else "act")
        def tok_to_actT(t, src=None, ce="act"):
            src = tokb if src is None else src
            ps = rot(); psb = ps[:, :].bitcast(BF16)
            for kc in range(8):
                tr(psb[:, kc * 128:(kc + 1) * 128], src[:, kc * 128:(kc + 1) * 128], identb[:, :])
            cp(actT[:, :, t * 128:(t + 1) * 128], psb[:, 0:1024].re("p (k n) -> p k n", k=8), e=ce)

        def ffn(f, gi, nt):
            N = nt * 128
            norm_T(nt, gi)
            for sbk in range(11):
                c0 = sbk * 256; ncol = 256
                wg_ = nextW(); loadW(wg_[:, :, 0:ncol], WgB[f], 0, 8, c0, ncol)
                wu_ = nextW(); loadW(wu_[:, :, 0:ncol], WuB[f], 0, 8, c0, ncol)
                for q in range(ncol // 128):
                    fb = sbk * 2 + q
                    pg = rot(); pu = rot()
                    for kc in range(8):
                        mm(pg[:, 0:N], wg_[:, kc, q * 128:(q + 1) * 128], actT[:, kc, 0:N], kc == 0, kc == 7)
                    for kc in range(8):
                        mm(pu[:, 0:N], wu_[:, kc, q * 128:(q + 1) * 128], actT[:, kc, 0:N], kc == 0, kc == 7)
                    s_ = sg[st["sg"]]; st["sg"] ^= 1
                    act(s_[:, 0:N], pg[:, 0:N], AF.Silu)
                    tt_(hT_view(fb, N), s_[:, 0:N], pu[:, 0:N], ALU.mult)
            for db in range(2):
                for g in range(6):
                    nf = min(4, 22 - g * 4)
                    w = nextW()
                    wv = w[:].re("p k f -> p (k f)")[:, 0:nf * 512].re("p (k f) -> p k f", k=nf)
                    loadW(wv, WdB[f], g * 512, nf, db * 512, 512)
                    for t in range(nt):
                        for q in range(nf):
                            fc = g * 4 + q
                            mm(PS[4 + t][:, :], hT_view(fc, NMAX)[:, t * 128:(t + 1) * 128], wv[:, q, :], fc == 0, fc == 21)
                for t in range(nt):
                    stt(X[:, t, db * 512:(db + 1) * 512], PS[4 + t][:, :], 0.5, X[:, t, db * 512:(db + 1) * 512], ALU.mult, ALU.add)

        def out_proj(W2d, nt):
            for db in range(2):
                ws = []
                for hf in range(2):
                    w = nextW(); loadW(w[:, :, :], W2d, 0, 8, db * 512 + hf * 256, 256); ws.append(w)
                for t in range(nt):
                    ps = rot()
                    for hf in range(2):
                        for kc in range(8):
                            mm(ps[:, hf * 256:(hf + 1) * 256], actT[:, kc, t * 128:(t + 1) * 128], ws[hf][:, kc, :], kc == 0, kc == 7)
                    tt_(X[:, t, db * 512:(db + 1) * 512], X[:, t, db * 512:(db + 1) * 512], ps[:, :], ALU.add)

        def proj_blocks(W2d, nt, c0, ncol, consumer):
            ws = []
            for h0 in range(0, ncol, 256):
                hw = min(256, ncol - h0)
                w = nextW(); loadW(w[:, :, 0:hw], W2d, 0, 8, c0 + h0, hw); ws.append((w, h0, hw))
            for t in range(nt):
                ps = rot()
                for (w, h0, hw) in ws:
                    for kc in range(8):
                        mm(ps[:, h0:h0 + hw], actT[:, kc, t * 128:(t + 1) * 128], w[:, kc, 0:hw], kc == 0, kc == 7)
                consumer(t, ps)

        def headnorm(src, nh, dh, gain_bc, sl, extra_scale=None):
            n = nh * dh
            sq = s512a[sl, 0:n] if n <= 512 else gbc[sl, 0:n]
            act(sq, src, AF.Square)
            red(sm2[sl, 0:nh], sq.re("p (a b) -> p a b", a=nh))
            rstd_of(sm2[sl, 8:8 + nh], sm2[sl, 0:nh], dh, sl)
            if extra_scale is not None:
                ts(sm2[sl, 8:8 + nh], sm2[sl, 8:8 + nh], extra_scale, ALU.mult)
            tt_(src.re("p (a b) -> p a b", a=nh), src.re("p (a b) -> p a b", a=nh),
                sm2[sl, 8:8 + nh].un(2).bc([sl.stop - sl.start, nh, dh]), ALU.mult)
            tt_(src, src, gain_bc[sl, 0:n], ALU.mult)

        def QTp(h, t): return mixA[:, (t * 4 + h) * 256:(t * 4 + h + 1) * 256]
        def vB(t): return mixA[:, 4096 + t * 512: 4096 + (t + 1) * 512]
        def qkB(t): return mixB[:, t * 1024:(t + 1) * 1024].bitcast(F32)
        def srB(t): return mixB[:, 4096 + t * 1024: 4096 + (t + 1) * 1024].bitcast(F32)
        def qnS(t): return mixC[:, t * 1024:(t + 1) * 1024].bitcast(F32)
        def knS(t): return mixC[:, 2048 + t * 1024: 2048 + (t + 1) * 1024].bitcast(F32)
        def vnS(t): return SB["vn"][:, t * 512:(t + 1) * 512]
        SB = {}

        def qknorm(ps, which, dst):
            act(s512a[:, :], ps[:, :], AF.Square)
            red(sm2[:, 0:8], s512a[:, :].re("p (a b) -> p a b", a=8))
            rstd_of(sm2[:, 8:16], sm2[:, 0:8], 64, slice(0, 128))
            tt_(dst.re("p (a b) -> p a b", a=8), ps[:, :].re("p (a b) -> p a b", a=8), sm2[:, 8:16].un(2).bc([128, 8, 64]), ALU.mult)
            tt_(dst.re("p (a b) -> p a b", a=8), dst.re("p (a b) -> p a b", a=8),
                aqk_bc[:, which * 64:(which + 1) * 64].un(1).bc([128, 8, 64]), ALU.mult)

        def even_mixer(nt, g0, sample):
            N = nt * 128
            norm_T(nt, 1)
            def c_q(t, ps):
                dst = qnS(t) if sample else s512b[:, :]
                qknorm(ps, 0, dst)
                if not sample:
                    cp(b512a[:, :], dst, e="act")
                    pt_ = rot(); ptb_ = pt_[:, :].bitcast(BF16)
                    for h in range(4):
                        tr(ptb_[:, h * 128:(h + 1) * 128], b512a[:, h * 128:(h + 1) * 128], identb[:, :])
                    qv = mixA[:, t * 1024:(t + 1) * 1024].re("p (h c) -> p h c", h=4)
                    pv = ptb_[:, 0:512].re("p (h c) -> p h c", h=4)
                    cp(qv[0:64, :, 0:128], pv[0:64, :, :], e="act")
                    cp(qv[64:128, :, 128:256], pv[64:128, :, :])
            def c_k(t, ps):
                dst = knS(t) if sample else s512c[:, :]
                qknorm(ps, 1, dst)
                if sample:
                    for r2 in range(2):
                        p.dma("sp", ks[2 * t + r2:2 * t + r2 + 1, :], dst[64 * r2:64 * r2 + 1, :])
                else:
                    g = g0 + t
                    p.dma("sp", kp[g * 128:(g + 1) * 128, :], dst)
                    cp(b512b[:, :], dst, e="act")
                    pt_ = rot(); ptb_ = pt_[:, :].bitcast(BF16)
                    for h in range(4):
                        tr(ptb_[:, h * 128:(h + 1) * 128], b512b[:, h * 128:(h + 1) * 128], identb[:, :])
                    cp(KT[:, :, g * 128:(g + 1) * 128], ptb_[:, 0:512].re("p (h n) -> p h n", h=4))
            def c_v(t, ps):
                cp(s512b[:, :], ps[:, :], e="act")
                if sample:
                    for r2 in range(2):
                        p.dma("sp", vs[2 * t + r2:2 * t + r2 + 1, :], s512b[64 * r2:64 * r2 + 1, :])
                    cp(vnS(t), ps[:, :])
                else:
                    g = g0 + t
                    p.dma("sp", vp[g * 128:(g + 1) * 128, :], s512b[:, :])
                    cp(Vst[:, g, :, 0:128], ps[:, :].re("p (h v) -> p h v", h=4))
            def c_qkB(t, ps): cp(qkB(t), ps[:, :], e="act")
            def c_vB(t, ps): cp(vB(t), ps[:, :])
            def c_r(t, ps): act(srB(t), ps[:, :], AF.Silu)
            if not sample:
                memset(mixA[:, 0:nt * 1024], 0.0)
            proj_blocks(EWinB, nt, 0, 512, c_q)
            proj_blocks(EWinB, nt, 512, 512, c_k)
            proj_blocks(EWinB, nt, 1024, 512, c_v)
            proj_blocks(EWinB, nt, 1536, 512, c_qkB)
            proj_blocks(EWinB, nt, 2048, 512, c_vB)
            proj_blocks(EWinB, nt, 2576, 512, c_r)
            if KSUB == 1: return
            w = nextW(); loadW(w[:, :, 0:16], EWinB, 0, 8, 2560, 16)
            ps = rot()
            for kc in range(8):
                mm(ps[0:16, 0:N], w[:, kc, 0:16], actT[:, kc, 0:N], kc == 0, kc == 7)
            cp(bgT[0:16, 0:N], ps[0:16, 0:N])
            if KSUB == 2: return

            for t in range(nt):
                if sample:
                    for r2 in range(2):
                        r = 2 * t + r2; sl = slice(64 * r2, 64 * r2 + 1)
                        sample_attn(r, t, sl)
                        p.dma("sp", Sg[:], sgla[r].rearrange("(j hh) d v -> (hh d) j v", hh=2))
                        cp(Sgb[:, :, :], Sg[:, :, :], e="act")
                        gla_chunk(t, sl)
                        p.dma("sp", glas[r].rearrange("(j hh) d v -> (hh d) j v", hh=2), Sg[:])
                    sl_all = slice(0, 128)
                else:
                    attn_tile(t, g0 + t)
                    if KSUB == 3: return
                    if KSUB == 8 and t == 1: return
                    if not os.environ.get("KSKIPGLA"): gla_chunk(t, slice(0, 128))
                    if KSUB == 4: return
                headnorm(aout[:, 0:512], 4, 128, ahn_bc, slice(0, 128), extra_scale=0.8)
                headnorm(aout[:, 512:1024], 4, 128, bhn_bc, slice(0, 128))
                tt_(aout[:, 512:1024], aout[:, 512:1024], srB(t), ALU.mult)
                if KSUB == 5: return
                cp(tokb[:, :], aout[:, :], e="act")
                tok_to_actT(t)
                if KSUB == 6: return
            if KSUB == 7: return
            out_proj(EWoutB, nt)

        def attn_tile(t, i):
            pTs = (b512a, b512b, b512c)
            for h in range(4):
                a0 = PS[4 + 2 * (h % 2)]; a1 = PS[5 + 2 * (h % 2)]
                def stage1(kb):
                    s_ = rot()
                    mm(s_[:, 0:256], KT[:, h, kb * 128:(kb + 1) * 128], QTp(h, t), True, kb != i)
                    if kb == i:
                        mm(s_[:, 0:256], identb[:, :], negbf[:, :], False, True)
                    pT = pTs[kb % 3]
                    act(pT[:, 0:256], s_[:, 0:256], AF.Exp,
                        bias=abp[:, h * NTT + (i - kb): h * NTT + (i - kb) + 1], scale=0.125)
                    return pT
                pend = stage1(0)
                for kb in range(i + 1):
                    nxt = stage1(kb + 1) if kb + 1 <= i else None
                    mm(a0[:, 0:129], pend[:, 0:128], Vst[:, kb, h, 0:129], kb == 0, kb == i)
                    mm(a1[:, 0:129], pend[:, 128:256], Vst[:, kb, h, 0:129], kb == 0, kb == i)
                    pend = nxt
                recip(sm3[:, 0:1], a0[:, 128:129]); recip(sm3[:, 1:2], a1[:, 128:129])
                tt_(sm3[:, 2:3], sm3[:, 1:2], neglam[:, :], ALU.mult)
                ts(s512a[:, 0:128], a0[:, 0:128], sm3[:, 0:1], ALU.mult)
                stt(aout[:, h * 128:(h + 1) * 128], a1[:, 0:128], sm3[:, 2:3], s512a[:, 0:128], ALU.mult, ALU.add)

        def sample_attn(r, t, sl):
            qn = qnS(t); kn = knS(t)
            ps = rot()
            mm(ps[:, :], onesf[sl, 0:128], qn[sl, :], True, True)
            qbc = SB["qbc"][:, :].bitcast(F32); abs_ = SB["abs"][:, :].bitcast(F32); idx = SB["idx"][:, :].bitcast(I32)
            cp(qbc, ps[:, :], e="act")
            tt_(s512a[sl, :], qn[sl, :], kn[sl, :], ALU.mult)
            red(sm3[sl, 8:16], s512a[sl, :].re("p (a b) -> p a b", a=8))
            act(b512c[sl, 0:8], sm3[sl, 8:16], AF.Exp, scale=0.125)
            acc = PS[4]; den = PS[5]
            for j in range(NPG):
                Kb = SB["K%d" % (j % 2)][:, :].bitcast(F32); Vb_ = SB["V%d" % (j % 2)][:, :].bitcast(F32)
                col = r * NPG + j
                p.dma("pool", Kb, ck[:, :], indirect=idx[:, col:col + 1])
                p.dma("pool", Vb_, cv[:, :], indirect=idx[:, col:col + 1])
                tt_(s512b[:, :], Kb, qbc, ALU.mult)
                red(sm3[:, 16:24], s512b[:, :].re("p (a b) -> p a b", a=8))
                stt(sm3[:, 24:32], sm3[:, 16:24], 0.125, abs_[:, j * 8:(j + 1) * 8], ALU.mult, ALU.add)
                pb = b512a if j % 2 == 0 else b512b
                act(pb[:, 0:8], sm3[:, 24:32], AF.Exp)
                vbf = SB["vb%d" % (j % 2)][:, :]
                cp(vbf, Vb_, e="act")
                mm(acc[0:8, :], pb[:, 0:8], vbf, j == 0, False)
                mm(den[0:8, 0:1], pb[:, 0:8], onesb[:, 0:1], j == 0, False)
            mm(acc[0:8, :], b512c[sl, 0:8], vnS(t)[sl, :], NPG == 0, True)
            mm(den[0:8, 0:1], b512c[sl, 0:8], onesb[sl, 0:1], NPG == 0, True)
            recip(sm3[0:8, 32:33], den[0:8, 0:1])
            ts(s512a[0:8, :], acc[0:8, :], sm3[0:8, 32:33], ALU.mult)
            v3 = lambda x: x.re("p (h v) -> p h v", h=4)
            tt_(v3(s512b[0:8, :]), v3(s512a[0:8, :]), bm[0:8, 0:4].un(2).bc([8, 4, 128]), ALU.mult)
            tt_(v3(s512c[0:8, :]), v3(s512a[0:8, :]), bm[0:8, 4:8].un(2).bc([8, 4, 128]), ALU.mult)
            stt(s512b[0:8, :], s512c[0:8, :], neglam[0:8, 0:1], s512b[0:8, :], ALU.mult, ALU.add)
            ps2 = rot()
            mm(ps2[sl, :], onesf[0:8, 0:1], s512b[0:8, :], True, True)
            cp(aout[sl, 0:512], ps2[sl, :])

        def gla_chunk(t, sl):
            n = sl.stop - sl.start; c0 = t * 128 + sl.start
            ps = rot()
            mm(ps[sl, 0:256], bgT[0:17, c0:c0 + n], W2aug[0:17, :], True, True)
            act(s512a[sl, 0:256], ps[sl, 0:256], AF.Exp, scale=-1.0)
            act(s512a[sl, 0:256], s512a[sl, 0:256], AF.Ln, bias=one_t[sl, 0:1])
            ts(s512a[sl, 0:256], s512a[sl, 0:256], -1.0 / 16.0, ALU.mult)
            ps = rot()
            mm(ps[sl, 0:256], tri[sl, sl], s512a[sl, 0:256], True, True)
            mm(ps[sl, 256:512], trirev[sl, sl], s512a[sl, 0:256], True, True)
            psl = rot()
            for j in range(2):
                mm(psl[:, j:j + 1], s512a[sl, j * 128:(j + 1) * 128], onesf[sl, 0:1], True, True)
            act(sm3[:, 40:42], psl[:, 0:2], AF.Exp)
            act(s512b[sl, 0:256], ps[sl, 0:256], AF.Exp)
            act(s512b[sl, 256:512], ps[sl, 0:256], AF.Exp, scale=-1.0)
            act(s512c[sl, 0:256], ps[sl, 256:512], AF.Exp)
            qk = qkB(t)
            stt(b512a[sl, 0:256], qk[sl, 0:256], 0.125, s512b[sl, 0:256], ALU.mult, ALU.mult)
            tt_(b512a[sl, 256:512], qk[sl, 256:512], s512b[sl, 256:512], ALU.mult)
            tt_(b512b[sl, 0:256], qk[sl, 256:512], s512c[sl, 0:256], ALU.mult)
            pt_ = rot(); ptb_ = pt_[:, :].bitcast(BF16)
            for q in range(4):
                tr(ptb_[:, q * 128:q * 128 + n], b512a[sl, q * 128:(q + 1) * 128], identb[sl, sl])
            for q in range(4):
                cp(b512c[:, q * 128:q * 128 + n], ptb_[:, q * 128:q * 128 + n], e="act")
            for h in range(4):
                j = h // 2; hp = slice(64 * (h % 2), 64 * (h % 2) + 64)
                pa = rot()
                mm(pa[sl, 0:n], b512c[hp, (2 + j) * 128:(2 + j) * 128 + n], b512c[hp, j * 128:j * 128 + n], True, True)
                tt_(b512b[sl, 256:256 + n], pa[sl, 0:n], tri[sl, sl], ALU.mult)
                po = rot(); po2 = rot()
                mm(po[sl, 0:128], b512b[sl, 256:256 + n], vB(t)[sl, h * 128:(h + 1) * 128], True, True)
                mm(po2[sl, 0:128], b512c[hp, j * 128:j * 128 + n], Sgb[hp, j, :], True, True)
                cp(aout[sl, 512 + h * 128:512 + (h + 1) * 128], po[sl, 0:128], e="act")
                tt_(aout[sl, 512 + h * 128:512 + (h + 1) * 128], aout[sl, 512 + h * 128:512 + (h + 1) * 128], po2[sl, 0:128], ALU.add)
            for h in range(4):
                j = h // 2; hp = slice(64 * (h % 2), 64 * (h % 2) + 64)
                pS = rot()
                mm(pS[:, 0:128], b512b[sl, j * 128:(j + 1) * 128], vB(t)[sl, h * 128:(h + 1) * 128], True, True)
                stt(Sg[hp, j, :], Sg[hp, j, :], sm3[hp, 40 + j:41 + j], pS[hp, 0:128], ALU.mult, ALU.add)
            cp(Sgb[:, :, :], Sg[:, :, :], e="act")

        def qT_o(i, c0, n): return mixA[:, i * NMAX + c0: i * NMAX + c0 + n]
        def kT_o(i, c0, n): return mixA[:, 4096 + i * NMAX + c0: 4096 + i * NMAX + c0 + n]
        def ktok(t): return mixB[:, t * 1024:(t + 1) * 1024]
        def vaug(t): return mixB[:, 4096 + t * 1032: 4096 + (t + 1) * 1032].re("p (h v) -> p h v", h=4)
        def so(t): return mixC[:, t * 1024:(t + 1) * 1024]

        def odd_mixer(nt, sample):
            N = nt * 128
            norm_T(nt, 4)
            for t in range(nt):
                memset(vaug(t)[:, :, 256:258], 1.0)
            def c_q(blk):
                def c(t, ps):
                    cp(b512a[:, :], ps[:, :], e="act")
                    pt_ = rot(); ptb_ = pt_[:, :].bitcast(BF16)
                    for q in range(4):
                        tr(ptb_[:, q * 128:(q + 1) * 128], b512a[:, q * 128:(q + 1) * 128], identb[:, :])
                    for q in range(4):
                        cp(qT_o(blk * 4 + q, t * 128, 128), ptb_[:, q * 128:(q + 1) * 128])
                return c
            def c_k(blk):
                def c(t, ps):
                    act(ktok(t)[:, blk * 512:(blk + 1) * 512], ps[:, :], AF.Copy, scale=1.0 / 16.0)
                    pt_ = rot(); ptb_ = pt_[:, :].bitcast(BF16)
                    for q in range(4):
                        tr(ptb_[:, q * 128:(q + 1) * 128], ktok(t)[:, blk * 512 + q * 128: blk * 512 + (q + 1) * 128], identb[:, :])
                    for q in range(4):
                        cp(kT_o(blk * 4 + q, t * 128, 128), ptb_[:, q * 128:(q + 1) * 128])
                return c
            def c_v(blk):
                def c(t, ps):
                    cp(vaug(t)[:, 2 * blk:2 * blk + 2, 0:256], ps[:, :].re("p (h v) -> p h v", h=2))
                return c
            def c_o(blk):
                def c(t, ps):
                    act(so(t)[:, blk * 512:(blk + 1) * 512], ps[:, :], AF.Sigmoid)
                return c
            def c_g(t, ps):
                tt_(gates[:, t, :], ps[:, 0:8], cgb_bc[:, :], ALU.add)
            for blk in range(2): proj_blocks(OWinB, nt, blk * 512, 512, c_q(blk))
            for blk in range(2): proj_blocks(OWinB, nt, 1024 + blk * 512, 512, c_k(blk))
            for blk in range(2): proj_blocks(OWinB, nt, 2048 + blk * 512, 512, c_v(blk))
            for blk in range(2): proj_blocks(OWinB, nt, 3072 + blk * 512, 512, c_o(blk))
            proj_blocks(OWinB, nt, 4096, 8, c_g)
            for t in range(nt):
                if sample:
                    for r2 in range(2):
                        r = 2 * t + r2; sl = slice(64 * r2, 64 * r2 + 1)
                        load_C(r)
                        mlstm_chunk(t, sl)
                        store_C(Cs[r], ns[r], ms[r:r + 1, :])
                else:
                    mlstm_chunk(t, slice(0, 128))
                headnorm(aout[:, 0:1024], 4, 256, chn_bc, slice(0, 128))
                tt_(aout[:, :], aout[:, :], so(t), ALU.mult)
                cp(tokb[:, :], aout[:, :], e="act")
                tok_to_actT(t)
            out_proj(OWoutB, nt)

        def load_C(r):
            for h in range(4):
                for vc in range(2):
                    p.dma("sp", s512a[:, 0:256], sC[r, h, vc * 128:(vc + 1) * 128, :])
                    ps = rot()
                    for dc in range(2):
                        tr(ps[:, dc * 128:(dc + 1) * 128], s512a[:, dc * 128:(dc + 1) * 128], identf[:, :])
                    for dc in range(2):
                        cp(CT[:, h, dc, vc * 128:(vc + 1) * 128], ps[:, dc * 128:(dc + 1) * 128])
                for dc in range(2):
                    p.dma("sp", CT[:, h, dc, 256:257], sn[r, h:h + 1, dc * 128:(dc + 1) * 128].rearrange("o d -> d o"))
            p.dma("sp", mprev[:], sm[r:r + 1, :].partition_broadcast(128))
            cp(CTb[:].re("p a b c -> p (a b c)"), CT[:].re("p a b c -> p (a b c)"), e="act")

        def store_C(Cd, nd, md):
            for h in range(4):
                for vc in range(2):
                    ps = rot()
                    for dc in range(2):
                        tr(ps[:, dc * 128:(dc + 1) * 128], CT[:, h, dc, vc * 128:(vc + 1) * 128], identf[:, :])
                    cp(s512b[:, 0:256], ps[:, 0:256])
                    p.dma("sp", Cd[h, vc * 128:(vc + 1) * 128, :], s512b[:, 0:256])
                for dc in range(2):
                    p.dma("sp", nd[h:h + 1, dc * 128:(dc + 1) * 128].rearrange("o d -> d o"), CT[:, h, dc, 256:257])
            p.dma("sp", md, mprev[0:1, :])

        def mlstm_chunk(t, sl):
            n = sl.stop - sl.start; c0 = t * 128 + sl.start
            g = gates
            ipre = g[sl, t, 0:4]
            act(sm1[sl, 16:20], g[sl, t, 4:8], AF.Exp, scale=-1.0)
            act(sm1[sl, 16:20], sm1[sl, 16:20], AF.Ln, bias=one_t[sl, 0:1])
            ts(sm1[sl, 16:20], sm1[sl, 16:20], -1.0, ALU.mult)
            ps = rot()
            mm(ps[sl, 0:4], tri[sl, sl], sm1[sl, 16:20], True, True)
            mm(ps[sl, 4:8], onesf[sl, sl], sm1[sl, 16:20], True, True)
            cp(sm1[sl, 20:28], ps[sl, 0:8])
            tt_(sm1[sl, 28:32], ipre, sm1[sl, 20:24], ALU.subtract)
            for h in range(4):
                ts(s512a[sl, h * 128:h * 128 + n], identf[sl, sl], sm1[sl, 28 + h:29 + h], ALU.mult)
            pa = rot()
            mm(pa[sl, :].re("p (h s) -> p h s", h=4)[:, :, 0:n], onesf[sl, sl], s512a[sl, :].re("p (h s) -> p h s", h=4)[:, :, 0:n], True, True)
            red(sm1[sl, 32:36], pa[sl, :].re("p (h s) -> p h s", h=4)[:, :, 0:n], op=ALU.max)
            tt_(sm1[sl, 32:36], sm1[sl, 32:36], mprev[sl, :], ALU.max)
            tt_(s512b[sl, :].re("p (h s) -> p h s", h=4)[:, :, 0:n], pa[sl, :].re("p (h s) -> p h s", h=4)[:, :, 0:n],
                negts[sl, sl].un(1).bc([n, 4, n]), ALU.add)
            red(sm1[sl, 36:40], s512b[sl, :].re("p (h s) -> p h s", h=4)[:, :, 0:n], op=ALU.max)
            tt_(sm1[sl, 36:40], sm1[sl, 36:40], mprev[sl, :], ALU.max)
            tt_(sm1[sl, 40:44], sm1[sl, 20:24], sm1[sl, 36:40], ALU.add)
            act(sm1[sl, 44:48], sm1[sl, 40:44], AF.Exp, scale=-1.0)
            tt_(sm1[sl, 48:52], mprev[sl, :], sm1[sl, 36:40], ALU.subtract)
            act(sm1[sl, 48:52], sm1[sl, 48:52], AF.Exp)
            for h in range(4):
                ts(s512a[sl, h * 128:h * 128 + n], identf[sl, sl], sm1[sl, 36 + h:37 + h], ALU.mult)
            pm = rot()
            mv = pm[sl, :].re("p (h s) -> p h s", h=4)[:, :, 0:n]
            for h in range(4):
                mm(pm[sl, h * 128:h * 128 + n], onesf[sl, sl], s512a[sl, h * 128:h * 128 + n], True, False)
                mm(pm[sl, h * 128:h * 128 + n], identf[sl, sl], pos4[sl, sl.start:sl.start + n], False, True)
            for h in range(4):
                act(s512c[sl, h * 128:h * 128 + n], pm[sl, h * 128:h * 128 + n], AF.Exp, bias=sm1[sl, 28 + h:29 + h], scale=-1.0)
            tt_(sm1[sl, 52:56], sm1[sl, 28:32], sm1[sl, 32:36], ALU.subtract)
            act(sm1[sl, 52:56], sm1[sl, 52:56], AF.Exp)
            tt_(sm1[sl, 56:60], mprev[sl, :], sm1[sl, 32:36], ALU.subtract)
            act(sm1[sl, 56:60], sm1[sl, 56:60], AF.Exp)
            pb_ = rot()
            for h in range(4):
                ts(s512a[sl, h * 128:h * 128 + n], identf[sl, sl], sm1[sl, 56 + h:57 + h], ALU.mult)
            mm(pb_[:, 0:4], onesf[sl, 0:128], s512a[sl, :].re("p (h s) -> p h s", h=4)[:, :, 0:1], True, True)
            cp(sm2[:, 16:20], pb_[:, 0:4])
            tt_(sm1[sl, 60:64], sm1[sl, 24:28], sm1[sl, 32:36], ALU.add)
            for h in range(4):
                ts(s512a[sl, h * 128:h * 128 + n], identf[sl, sl], sm1[sl, 60 + h:61 + h], ALU.mult)
            pb2 = rot()
            mm(pb2[:, 0:4], onesf[sl, 0:128], s512a[sl, :].re("p (h s) -> p h s", h=4)[:, :, 0:1], True, True)
            cp(sm2[:, 20:24], pb2[:, 0:4])
            for h in range(4):
                pk = rot()
                for dc in range(2):
                    mm(pk[sl, 0:n], kT_o(h * 2 + dc, c0, n), qT_o(h * 2 + dc, c0, n), dc == 0, dc == 1)
                tt_(b512b[sl, 0:n], pk[sl, 0:n], s512c[sl, h * 128:h * 128 + n], ALU.mult)
                pn1 = PS[4 + (h % 2) * 2]; pn2 = PS[5 + (h % 2) * 2]
                mm(pn1[sl, 0:257], b512b[sl, 0:n], vaug(t)[sl, h, 0:257], True, True)
                for dc in range(2):
                    mm(pn2[sl, 0:257], qT_o(h * 2 + dc, c0, n), CTb[:, h, dc, 0:257], dc == 0, dc == 1)
                ts(s512b[sl, 0:257], pn2[sl, 0:257], sm1[sl, 48 + h:49 + h], ALU.mult)
                tt_(s512b[sl, 0:257], s512b[sl, 0:257], pn1[sl, 0:257], ALU.add)
                ts(sm2[sl, 26:27], s512b[sl, 256:257], -1.0, ALU.mult)
                tt_(sm2[sl, 24:25], s512b[sl, 256:257], sm2[sl, 26:27], ALU.max)
                tt_(sm2[sl, 24:25], sm2[sl, 24:25], sm1[sl, 44 + h:45 + h], ALU.max)
                recip(sm2[sl, 25:26], sm2[sl, 24:25])
                ts(aout[sl, h * 256:(h + 1) * 256], s512b[sl, 0:256], sm2[sl, 25:26], ALU.mult)
            for h in range(4):
                ts(b512a[sl, 0:256], ktok(t)[sl, h * 256:(h + 1) * 256], sm1[sl, 52 + h:53 + h], ALU.mult)
                for dc in range(2):
                    pc = rot()
                    mm(pc[:, 0:257], b512a[sl, dc * 128:(dc + 1) * 128], vaug(t)[sl, h, 0:257], True, True)
                    stt(CT[:, h, dc, 0:257], CT[:, h, dc, 0:257], sm2[:, 16 + h:17 + h], pc[:, 0:257], ALU.mult, ALU.add)
            cp(CTb[:].re("p a b c -> p (a b c)"), CT[:].re("p a b c -> p (a b c)"), e="act")
            cp(mprev[:, :], sm2[:, 20:24])

        def convert_all():
            convW(WgB[0], Wg[0], D); convW(WuB[0], Wu[0], D); convW(WdB[0], Wd[0], DFF)
            convW(EWinB, EWin, D); convW(EWoutB, EWout, D)
            convW(WgB[1], Wg[1], D); convW(WuB[1], Wu[1], D); convW(WdB[1], Wd[1], DFF)
            convW(WgB[2], Wg[2], D); convW(WuB[2], Wu[2], D); convW(WdB[2], Wd[2], DFF)
            convW(OWinB, OWin, D); convW(OWoutB, OWout, D)
            convW(WgB[3], Wg[3], D); convW(WuB[3], Wu[3], D); convW(WdB[3], Wd[3], DFF)
        def main_prog():
            convert_all()
            memset(Sg[:, :, :], 0.0); memset(Sgb[:, :, :], 0.0)
            memset(CT[:].re("p a b c -> p (a b c)"), 0.0); memset(CTb[:].re("p a b c -> p (a b c)"), 0.0)
            memset(mprev[:, :], 0.0)
            if KSTOP == 1: return
            for s_i in range(NST):
                g0 = s_i * STT
                for t in range(STT):
                    p.dma("sp", X[:, t, :], xp[(g0 + t) * 128:(g0 + t + 1) * 128, :])
                ffn(0, 0, STT)
                if KSTOP == 2: return
                even_mixer(STT, g0, False)
                if KSTOP == 3: return
                ffn(1, 2, STT)
                ffn(2, 3, STT)
                odd_mixer(STT, False)
                if KSTOP == 4: return
                ffn(3, 5, STT)
                for t in range(STT):
                    p.dma("sp", yp[(g0 + t) * 128:(g0 + t + 1) * 128, :], X[:, t, :])
            p.dma("sp", glap.rearrange("(j hh) d v -> (hh d) j v", hh=2), Sg[:])
            store_C(Cp, np_, mp[0:1, :])
            if KSTOP == 5: return
            for i_, nm in enumerate(("K0", "V0", "K1", "V1", "qbc")):
                SB[nm] = KTbuf.sub((slice(None), slice(i_ * 1024, (i_ + 1) * 1024)), nm)
            SB["vb0"] = KTbuf.sub((slice(None), slice(5120, 5632)), "vb0"); SB["vb1"] = KTbuf.sub((slice(None), slice(5632, 6144)), "vb1")
            SB["vn"] = KTbuf.sub((slice(None), slice(6144, 7168)), "vn")
            SB["abs"] = Vbuf.sub((slice(None), slice(0, 2 * NPG * 8)), "abs")
            SB["idx"] = Vbuf.sub((slice(None), slice(2 * NPG * 8, 2 * NPG * 8 + 8 * NPG)), "idx")
            p.dma("sp", SB["abs"][:, :].bitcast(F32), c_abs[:, :])
            ptb = s512b[:, 0:4 * NPG].bitcast(I32)
            p.dma("sp", ptb, ptab[0:1, :].partition_broadcast(128))
            cp(s512c[:, 0:4 * NPG], ptb)
            ts(SB["idx"][:, :].bitcast(I32), s512c[:, 0:4 * NPG], 128.0, ALU.mult, iota[:, 0:1], ALU.add)
            memset(X[:, 0:2, :], 0.0)
            for r in range(4):
                p.dma("sp", X[64 * (r % 2):64 * (r % 2) + 1, r // 2, :], xs[r:r + 1, :])
            ffn(0, 0, 2)
            if KSTOP == 6: return
            even_mixer(2, 0, True)
            if KSTOP == 7: return
            ffn(1, 2, 2)
            ffn(2, 3, 2)
            odd_mixer(2, True)
            ffn(3, 5, 2)
            for r in range(4):
                p.dma("sp", ys[r:r + 1, :], X[64 * (r % 2):64 * (r % 2) + 1, r // 2, :])

        main_prog()
        p.finish()
    return nc


def make_consts(TSEQ, NPG):
    NTT = TSEQ // 128; PAST = NPG * 128
    ar = np.arange(128)
    c = {}
    c["c_identf"] = np.eye(128, dtype=np.float32)
    c["c_tri"] = (ar[:, None] <= ar[None, :]).astype(np.float32)
    c["c_trirev"] = (ar[:, None] > ar[None, :]).astype(np.float32)
    c["c_negts"] = np.where(ar[None, :] > ar[:, None], -1e30, 0.0).astype(np.float32)
    pos = np.where(ar[:, None] > ar[None, :], 1e30, 0.0).astype(np.float32)
    c["c_pos4"] = pos
    nb = np.where(ar[:, None] > ar[None, :], -30000.0, 0.0).astype(np.float32)
    c["c_negbf"] = np.concatenate([nb, nb], axis=1)
    slopes = np.array([2.0 ** (-8.0 * (h + 1) / 4) for h in range(4)], np.float32)
    abp = np.zeros((128, 4, NTT), np.float32)
    for h in range(4):
        for dd in range(NTT):
            abp[:, h, dd] = slopes[h] * (-128.0 * dd + ar - 127.0)
    c["c_abp"] = abp.reshape(128, 4 * NTT)
    ab = np.zeros((128, NPG, 8), np.float32)
    for j in range(NPG):
        for h in range(4):
            ab[:, j, 2 * h] = ab[:, j, 2 * h + 1] = slopes[h] * (128.0 * j + ar - PAST)
    c["c_abs"] = ab.reshape(128, NPG * 8)
    c["c_iota"] = ar.astype(np.float32).reshape(128, 1)
    bmk = np.zeros((8, 8), np.float32)
    for h in range(4):
        bmk[2 * h, h] = 1.0
        bmk[2 * h + 1, 4 + h] = 1.0
    c["c_bm"] = bmk
    return c


def run(inputs, TSEQ, NPG, NPOOL, n_cores=8):
    f = lambda a: np.ascontiguousarray(np.asarray(a))
    nc = build(TSEQ, NPG, NPOOL)
    consts = make_consts(TSEQ, NPG)
    B = inputs["x_prompt"].shape[0]
    shared = {
        "ck": f(inputs["cache_k"]).reshape(NPOOL * 128, 512), "cv": f(inputs["cache_v"]).reshape(NPOOL * 128, 512),
        "normg": f(inputs["norm_g"]).reshape(6, D),
        "wg": f(inputs["ffn_w_gate"]).reshape(4, D, DFF), "wu": f(inputs["ffn_w_up"]).reshape(4, D, DFF),
        "wd": f(inputs["ffn_w_down"]).reshape(4, DFF, D),
        "ewin": f(inputs["even_w_in"])[0], "ewout": f(inputs["even_w_out"])[0],
        "aqk": f(inputs["a_qk_norm"]).reshape(1, 128), "alam": f(inputs["a_lambda"]).reshape(1, 256),
        "ahn": f(inputs["a_head_norm"]).reshape(1, 512), "bw2": f(inputs["b_gate_w2"])[0],
        "bgb": f(inputs["b_gate_bias"]).reshape(1, 256), "bhn": f(inputs["b_head_norm"]).reshape(1, 512),
        "owin": f(inputs["odd_w_in"])[0], "owout": f(inputs["odd_w_out"])[0],
        "cgb": f(inputs["c_gate_bias"]).reshape(1, 8), "chn": f(inputs["c_head_norm"]).reshape(1, 1024),
    }
    shared.update(consts)
    shared["c_negbf"] = consts["c_negbf"]
    in_maps = []
    for c in range(n_cores):
        b = c % B; s0 = 4 * c
        m = dict(shared)
        m["xp"] = f(inputs["x_prompt"][b]); m["xs"] = f(inputs["x_sample"][s0:s0 + 4, 0])
        m["sgla"] = f(inputs["state_gla"][0, s0:s0 + 4]); m["sC"] = f(inputs["state_mlstm_C"][0, s0:s0 + 4])
        m["sn"] = f(inputs["state_mlstm_n"][0, s0:s0 + 4]); m["sm"] = f(inputs["state_mlstm_m"][0, s0:s0 + 4])
        m["ptab"] = f(inputs["page_table"][s0:s0 + 4]).reshape(1, 4 * NPG).astype(np.int32)
        in_maps.append(m)
    res = run_bass_kernel_spmd(nc, in_maps, core_ids=list(range(n_cores)))
    R = res.results
    cat = lambda k, cores: np.stack([R[c][k] for c in cores])
    pc = list(range(min(B, n_cores))); ac = list(range(n_cores)); B = len(pc)
    y_prompt = cat("yp", pc)
    y_sample = np.concatenate([R[c]["ys"] for c in ac])[:, None, :]
    k_prompt = cat("kp", pc).reshape(1, B, TSEQ, 4, 2, 64)
    v_prompt = cat("vp", pc).reshape(1, B, TSEQ, 4, 128)
    k_sample = np.concatenate([R[c]["ks"] for c in ac]).reshape(1, 4 * n_cores, 1, 4, 2, 64)
    v_sample = np.concatenate([R[c]["vs"] for c in ac]).reshape(1, 4 * n_cores, 1, 4, 128)
    gla_prompt = cat("glap", pc)[None]
    gla_sample = np.concatenate([R[c]["glas"] for c in ac])[None]
    C_prompt = cat("Cp", pc)[None]; n_prompt = cat("np", pc)[None]; m_prompt = cat("mp", pc).reshape(1, B, 4)
    C_sample = np.concatenate([R[c]["Cs"] for c in ac])[None]
    n_sample = np.concatenate([R[c]["ns"] for c in ac])[None]
    m_sample = np.concatenate([R[c]["ms"] for c in ac])[None]
    outs = (y_prompt, y_sample, k_prompt, v_prompt, k_sample, v_sample, gla_prompt, gla_sample,
            C_prompt, n_prompt, m_prompt, C_sample, n_sample, m_sample)
    return tuple(np.ascontiguousarray(o, dtype=np.float32) for o in outs)


def kernel(**inputs):
    TSEQ = inputs["x_prompt"].shape[1]
    NPG = inputs["page_table"].shape[1]
    NPOOL = inputs["cache_k"].shape[1]
    return run(inputs, TSEQ, NPG, NPOOL)
```
